# Optimizing a Trainium2 kernel written in Bass

```python
import functools
import jax, jax.numpy as jnp
from jax import lax
import numpy as np

D_MODEL = 1024
BATCH = 16
SEQ = 256
DEPTH = 1
DEC_BATCH = 8
DEC_SEQ = 1024
PAST_LEN = 256

GRID_W = 64
N_HEADS = 8
HEAD_DIM = 64
D_ATTN = N_HEADS * HEAD_DIM
WIN_H = 8
WIN_W = 16
D_LRU = 1024
LRU_BLOCKS = 16
LRU_BLOCK = D_LRU // LRU_BLOCKS
CONV_W = 4
LRU_C = 8.0
D_FF = 2816
N_MOD = 9
IN_WIDTH = 3 * D_ATTN + 2 * D_LRU + 2 * D_MODEL
EPS = 1e-6
NEG_INF = -1e30

kernel_name = 'hybrid_na_rglru_diffusion_step'


def rmsnorm(x, g):
    x32 = x.astype(jnp.float32)
    y = x32 * lax.rsqrt(jnp.mean(x32 * x32, axis=-1, keepdims=True) + EPS)
    return (y * g.astype(jnp.float32)).astype(x.dtype)


def modulate(xn, shift, scale):
    return xn * (1 + scale[:, None, :]) + shift[:, None, :]


def swiglu(x, w_in, w_out):
    g, u = jnp.split(x @ w_in, 2, axis=-1)
    return (jax.nn.silu(g) * u) @ w_out


def in_projection(h, w_in):
    b, l, _ = h.shape
    z = h @ w_in
    cuts = [D_ATTN, 2 * D_ATTN, 3 * D_ATTN, 3 * D_ATTN + D_LRU, 3 * D_ATTN + 2 * D_LRU]
    q, k, v, xl, gl, gates = jnp.split(z, cuts, axis=-1)
    q = q.reshape(b, l, N_HEADS, HEAD_DIM)
    k = k.reshape(b, l, N_HEADS, HEAD_DIM)
    v = v.reshape(b, l, N_HEADS, HEAD_DIM)
    return q, k, v, xl, gl, gates


def context_attention(q, k, v):
    s = jnp.einsum('bqhd,bkhd->bhqk', q, k).astype(jnp.float32) * (HEAD_DIM ** -0.5)
    p = jax.nn.softmax(s, axis=-1).astype(v.dtype)
    o = jnp.einsum('bhqk,bkhd->bqhd', p, v)
    return o.reshape(q.shape[0], q.shape[1], D_ATTN)


def neighbourhood_attention(q, k, v, kc, vc, rpb):
    b, l, nh, dh = q.shape
    rows = l // GRID_W
    kh = min(WIN_H, rows)
    scale = HEAD_DIM ** -0.5
    qg = q.reshape(b, rows, GRID_W, nh, dh)
    kg = k.reshape(b, rows, GRID_W, nh, dh)
    vg = v.reshape(b, rows, GRID_W, nh, dh)
    r = jnp.arange(rows)
    row_start = jnp.clip(r - kh // 2, 0, rows - kh)
    key_rows = row_start[:, None] + jnp.arange(kh)[None, :]
    kb = kg[:, key_rows].reshape(b, rows, kh * GRID_W, nh, dh)
    vb = vg[:, key_rows].reshape(b, rows, kh * GRID_W, nh, dh)
    cols = jnp.arange(GRID_W)
    col_start = jnp.clip(cols - WIN_W // 2, 0, GRID_W - WIN_W)
    in_win = (cols[None, :] >= col_start[:, None]) & (cols[None, :] < col_start[:, None] + WIN_W)
    mask = jnp.broadcast_to(in_win[:, None, :], (GRID_W, kh, GRID_W)).reshape(GRID_W, kh * GRID_W)
    dr = key_rows - r[:, None] + (WIN_H - 1)
    dc = jnp.clip(cols[None, :] - cols[:, None], -(WIN_W - 1), WIN_W - 1) + (WIN_W - 1)
    bias = rpb[:, dr[:, None, :, None], dc[None, :, None, :]]
    bias = bias.reshape(nh, rows, GRID_W, kh * GRID_W).astype(jnp.float32)
    s_win = jnp.einsum('brqhd,brkhd->bhrqk', qg, kb).astype(jnp.float32) * scale + bias
    s_win = jnp.where(mask, s_win, NEG_INF)
    s_ctx = jnp.einsum('brqhd,bhkd->bhrqk', qg, kc).astype(jnp.float32) * scale
    p = jax.nn.softmax(jnp.concatenate([s_win, s_ctx], axis=-1), axis=-1).astype(v.dtype)
    nw = kh * GRID_W
    o = (jnp.einsum('bhrqk,brkhd->brqhd', p[..., :nw], vb)
         + jnp.einsum('bhrqk,bhkd->brqhd', p[..., nw:], vc))
    return o.reshape(b, l, D_ATTN)


def dwconv_centred(x, w, bias):
    l = x.shape[1]
    left = CONV_W // 2
    xp = jnp.pad(x, ((0, 0), (left, CONV_W - 1 - left), (0, 0)))
    out = bias
    for j in range(CONV_W):
        out = out + xp[:, j:j + l] * w[j]
    return out


def blockdiag(x, w):
    b, l, _ = x.shape
    xb = x.reshape(b, l, LRU_BLOCKS, LRU_BLOCK)
    return jnp.einsum('blnc,ncd->blnd', xb, w.astype(jnp.float32)).reshape(b, l, D_LRU)


def linear_scan(a, bterm, h0):
    def comb(e1, e2):
        a1, b1 = e1
        a2, b2 = e2
        return a1 * a2, a2 * b1 + b2
    a_cum, b_cum = lax.associative_scan(comb, (a, bterm), axis=1)
    return b_cum + a_cum * h0[:, None, :]


def rglru_direction(xc, wa, ba, wi, bi, lam, h0, reverse):
    xs = xc[:, ::-1] if reverse else xc
    r = jax.nn.sigmoid(blockdiag(xs, wa) + ba.astype(jnp.float32))
    i = jax.nn.sigmoid(blockdiag(xs, wi) + bi.astype(jnp.float32))
    log_a = -LRU_C * r * jax.nn.softplus(-lam.astype(jnp.float32))
    a = jnp.exp(log_a)
    bterm = jnp.sqrt(-jnp.expm1(2.0 * log_a)) * (i * xs)
    h = linear_scan(a, bterm, h0.astype(jnp.float32))
    h_final = h[:, -1]
    if reverse:
        h = h[:, ::-1]
    return h, h_final


def rglru_branch(xl, gl, p, h0_f, h0_b):
    xc = dwconv_centred(xl, p['conv_w'], p['conv_b']).astype(jnp.float32)
    hf, hf_T = rglru_direction(xc, p['lru_wa'][0], p['lru_ba'][0], p['lru_wi'][0], p['lru_bi'][0],
                               p['lru_lambda'][0], h0_f, False)
    hb, hb_T = rglru_direction(xc, p['lru_wa'][1], p['lru_ba'][1], p['lru_wi'][1], p['lru_bi'][1],
                               p['lru_lambda'][1], h0_b, True)
    y = ((hf + hb) * jax.nn.gelu(gl.astype(jnp.float32))).astype(xl.dtype)
    h_final = jnp.stack([hf_T, hb_T], axis=1).astype(xl.dtype)
    return y, h_final


def merge_branches(attn, lru, gates, p):
    g_attn, g_lru = jnp.split(gates, 2, axis=-1)
    m = (jax.nn.sigmoid(g_attn) * (attn @ p['w_br_attn'])
         + jax.nn.sigmoid(g_lru) * (lru @ p['w_br_lru']))
    return m @ p['w_out']


def context_mixer(h, p):
    q, k, v, xl, gl, gates = in_projection(h, p['w_in'])
    attn = context_attention(q, k, v)
    zero = jnp.zeros((h.shape[0], D_LRU), jnp.float32)
    lru, h_final = rglru_branch(xl, gl, p, zero, zero)
    out = merge_branches(attn, lru, gates, p)
    return out, (k.transpose(0, 2, 1, 3), v.transpose(0, 2, 1, 3), h_final)


def latent_mixer(h, p, kc, vc, h0):
    q, k, v, xl, gl, gates = in_projection(h, p['w_in'])
    attn = neighbourhood_attention(q, k, v, kc, vc, p['rpb'])
    lru, _ = rglru_branch(xl, gl, p, h0[:, 0], h0[:, 1])
    return merge_branches(attn, lru, gates, p), None


def apply_layer(x, cond, p, mixer):
    mods = jax.nn.silu(cond) @ p['w_mod'] + p['b_mod']
    s1, sc1, g1, s2, sc2, g2, s3, sc3, g3 = jnp.split(mods, N_MOD, axis=-1)
    h = modulate(rmsnorm(x, p['norm_g'][0]), s1, sc1)
    x = x + 0.5 * g1[:, None, :] * swiglu(h, p['ffn1_w_in'], p['ffn1_w_out'])
    h = modulate(rmsnorm(x, p['norm_g'][1]), s2, sc2)
    out, extras = mixer(h, p)
    x = x + g2[:, None, :] * out
    h = modulate(rmsnorm(x, p['norm_g'][2]), s3, sc3)
    x = x + 0.5 * g3[:, None, :] * swiglu(h, p['ffn2_w_in'], p['ffn2_w_out'])
    return x, extras


def setup_inputs(seed: int = 0) -> dict:
    key = jax.random.key(seed)
    ks = jax.random.split(key, 32)
    f32 = jnp.float32
    nrm = lambda k, shape, s: jax.random.normal(k, shape, f32) * s
    a0 = jax.random.uniform(ks[20], (DEPTH, 2, D_LRU), f32, 0.9, 0.999)
    return {
        'x_prompt': nrm(ks[0], (BATCH, SEQ, D_MODEL), 1.0),
        'x_sample': nrm(ks[1], (DEC_BATCH, DEC_SEQ, D_MODEL), 1.0),
        'cache_k': nrm(ks[2], (DEC_BATCH, DEPTH, N_HEADS, PAST_LEN, HEAD_DIM), 1.0),
        'cache_v': nrm(ks[3], (DEC_BATCH, DEPTH, N_HEADS, PAST_LEN, HEAD_DIM), 1.0),
        'state_lru': nrm(ks[4], (DEC_BATCH, DEPTH, 2, D_LRU), 0.5),
        'c': nrm(ks[5], (DEC_BATCH, D_MODEL), 1.0),
        'c_ctx': nrm(ks[6], (D_MODEL,), 1.0),
        'w_mod': nrm(ks[7], (DEPTH, D_MODEL, N_MOD * D_MODEL), 0.5 * D_MODEL ** -0.5),
        'b_mod': nrm(ks[8], (DEPTH, N_MOD * D_MODEL), 0.02),
        'norm_g': 1.0 + nrm(ks[9], (DEPTH, 3, D_MODEL), 0.02),
        'ffn1_w_in': nrm(ks[10], (DEPTH, D_MODEL, 2 * D_FF), D_MODEL ** -0.5),
        'ffn1_w_out': nrm(ks[11], (DEPTH, D_FF, D_MODEL), D_FF ** -0.5),
        'w_in': nrm(ks[12], (DEPTH, D_MODEL, IN_WIDTH), D_MODEL ** -0.5),
        'rpb': nrm(ks[13], (DEPTH, N_HEADS, 2 * WIN_H - 1, 2 * WIN_W - 1), 0.1),
        'conv_w': nrm(ks[14], (DEPTH, CONV_W, D_LRU), CONV_W ** -0.5),
        'conv_b': nrm(ks[15], (DEPTH, D_LRU), 0.02),
        'lru_wa': nrm(ks[16], (DEPTH, 2, LRU_BLOCKS, LRU_BLOCK, LRU_BLOCK), LRU_BLOCK ** -0.5),
        'lru_ba': nrm(ks[17], (DEPTH, 2, D_LRU), 0.02),
        'lru_wi': nrm(ks[18], (DEPTH, 2, LRU_BLOCKS, LRU_BLOCK, LRU_BLOCK), LRU_BLOCK ** -0.5),
        'lru_bi': nrm(ks[19], (DEPTH, 2, D_LRU), 0.02),
        'lru_lambda': jnp.log(a0) - jnp.log1p(-a0),
        'w_br_attn': nrm(ks[21], (DEPTH, D_ATTN, D_MODEL), D_ATTN ** -0.5),
        'w_br_lru': nrm(ks[22], (DEPTH, D_LRU, D_MODEL), D_LRU ** -0.5),
        'w_out': nrm(ks[23], (DEPTH, D_MODEL, D_MODEL), D_MODEL ** -0.5),
        'ffn2_w_in': nrm(ks[24], (DEPTH, D_MODEL, 2 * D_FF), D_MODEL ** -0.5),
        'ffn2_w_out': nrm(ks[25], (DEPTH, D_FF, D_MODEL), D_FF ** -0.5),
        'final_g': 1.0 + nrm(ks[26], (D_MODEL,), 0.02),
    }


def reference(x_prompt, x_sample, cache_k, cache_v, state_lru, c, c_ctx, w_mod, b_mod, norm_g,
              ffn1_w_in, ffn1_w_out, w_in, rpb, conv_w, conv_b, lru_wa, lru_ba, lru_wi, lru_bi,
              lru_lambda, w_br_attn, w_br_lru, w_out, ffn2_w_in, ffn2_w_out, final_g):
    yp = x_prompt
    ys = x_sample
    ks_, vs_, hs_ = [], [], []
    for l in range(DEPTH):
        p = {
            'w_mod': w_mod[l], 'b_mod': b_mod[l], 'norm_g': norm_g[l],
            'ffn1_w_in': ffn1_w_in[l], 'ffn1_w_out': ffn1_w_out[l], 'w_in': w_in[l], 'rpb': rpb[l],
            'conv_w': conv_w[l], 'conv_b': conv_b[l], 'lru_wa': lru_wa[l], 'lru_ba': lru_ba[l],
            'lru_wi': lru_wi[l], 'lru_bi': lru_bi[l], 'lru_lambda': lru_lambda[l],
            'w_br_attn': w_br_attn[l], 'w_br_lru': w_br_lru[l], 'w_out': w_out[l],
            'ffn2_w_in': ffn2_w_in[l], 'ffn2_w_out': ffn2_w_out[l],
        }
        yp, (k_l, v_l, h_l) = apply_layer(yp, c_ctx[None, :], p, context_mixer)
        ks_.append(k_l)
        vs_.append(v_l)
        hs_.append(h_l)
        mixer = functools.partial(latent_mixer, kc=cache_k[:, l], vc=cache_v[:, l], h0=state_lru[:, l])
        ys, _ = apply_layer(ys, c, p, mixer)
    y_prompt = rmsnorm(yp, final_g)
    y_sample = rmsnorm(ys, final_g)
    new_cache_k = jnp.stack(ks_, axis=1)
    new_cache_v = jnp.stack(vs_, axis=1)
    new_state_lru = jnp.stack(hs_, axis=1)
    return (y_prompt, y_sample, new_cache_k, new_cache_v, new_state_lru)
```

```python
from contextlib import ExitStack
import numpy as np
import concourse.bass as bass
import concourse.mybir as mybir
from concourse.bass_utils import run_bass_kernel_spmd

F32 = mybir.dt.float32
BF16 = mybir.dt.bfloat16
AF = mybir.ActivationFunctionType
ALU = mybir.AluOpType

ENGINES = ("pe", "act", "dve", "pool", "sp")


class _Op:
    __slots__ = ("idx", "eng", "fn", "reads", "writes", "is_dma", "dsem", "dcum", "signal", "count", "waits",
                 "bar", "snap")

    def __init__(self, idx, eng, fn, reads, writes, is_dma, dsem):
        self.idx = idx
        self.eng = eng
        self.fn = fn
        self.reads = tuple(reads)
        self.writes = tuple(writes)
        self.is_dma = is_dma
        self.dsem = dsem
        self.dcum = 0
        self.signal = False
        self.count = 0
        self.waits = []
        self.bar = -1
        self.snap = None


class Prog:
    def __init__(self):
        self.ops = []
        self.dma_cum = {}
        self.nbar = 0

    def barrier(self):
        for e in ENGINES:
            o = _Op(len(self.ops), e, None, (), (), False, None)
            o.bar = self.nbar
            o.snap = dict(self.dma_cum)
            self.ops.append(o)
        self.nbar += 1

    def op(self, eng, fn, reads=(), writes=()):
        o = _Op(len(self.ops), eng, fn, reads, writes, False, None)
        self.ops.append(o)
        return o

    def dma(self, eng, fn, sem, reads=(), writes=()):
        o = _Op(len(self.ops), eng, fn, reads, writes, True, sem)
        self.dma_cum[sem] = self.dma_cum.get(sem, 0) + 16
        o.dcum = self.dma_cum[sem]
        self.ops.append(o)
        return o

    def _analyze(self):
        ops = self.ops
        last_write = {}
        readers = {}
        cum_now = {}
        for o in ops:
            if o.bar >= 0:
                last_write = {}
                readers = {}
                o.waits = ({}, {})
                continue
            raw = set()
            war = set()
            for b in o.reads:
                if b in last_write:
                    raw.add(last_write[b])
            for b in o.writes:
                if b in last_write:
                    raw.add(last_write[b])
                for r in readers.get(b, ()):
                    war.add(r)
            raw.discard(o.idx)
            war.discard(o.idx)
            war -= raw
            dma_w = {}
            eng_deps = {}
            for d in raw | war:
                p = ops[d]
                if p.is_dma:
                    dma_w[p.dsem] = max(dma_w.get(p.dsem, 0), cum_now.get(p.dsem, 0))
                    continue
                if p.eng == o.eng and not o.is_dma:
                    if o.eng == "pe":
                        continue
                p.signal = True
                eng_deps.setdefault(p.eng, []).append(d)
            o.waits = (dma_w, eng_deps)
            if o.is_dma:
                cum_now[o.dsem] = o.dcum
            for b in o.reads:
                readers.setdefault(b, []).append(o.idx)
            for b in o.writes:
                last_write[b] = o.idx
                readers[b] = []
        cnt = {e: 0 for e in ENGINES}
        for o in ops:
            if o.signal:
                cnt[o.eng] += 1
                o.count = cnt[o.eng]
        waited = {e: {} for e in ENGINES}
        sofar = {e: 0 for e in ENGINES}
        for o in ops:
            if o.signal:
                sofar[o.eng] = o.count
            wd = waited[o.eng]
            if o.bar >= 0:
                for e in ENGINES:
                    wd[("e", e)] = max(wd.get(("e", e), 0), sofar[e])
                for s, v in o.snap.items():
                    wd[("d", s)] = max(wd.get(("d", s), 0), v)
                o.waits = []
                continue
            dma_w, eng_deps = o.waits
            fin = []
            for s, v in dma_w.items():
                k = ("d", s)
                if v > wd.get(k, 0):
                    wd[k] = v
                    fin.append((k, v))
            for e, ds in eng_deps.items():
                v = max(ops[d].count for d in ds)
                k = ("e", e)
                if v > wd.get(k, 0):
                    wd[k] = v
                    fin.append((k, v))
            o.waits = fin

    def emit(self, nc):
        self._analyze()
        ops = self.ops
        with ExitStack() as st:
            esem = {e: st.enter_context(nc.semaphore("s_" + e)) for e in ENGINES}
            bsem = st.enter_context(nc.semaphore("s_bar"))
            dsem = {k: st.enter_context(nc.semaphore("d_%d" % i)) for i, k in enumerate(self.dma_cum)}
            block = st.enter_context(nc.Block())

            def semof(key):
                return esem[key[1]] if key[0] == "e" else dsem[key[1]]

            def run(engname, eng):
                for o in ops:
                    if o.eng != engname:
                        continue
                    if o.bar >= 0:
                        eng.drain().then_inc(bsem, 1)
                        eng.wait_ge(bsem, len(ENGINES) * (o.bar + 1))
                        for k, v in o.snap.items():
                            eng.wait_ge(dsem[k], v)
                        continue
                    for key, val in o.waits:
                        eng.wait_ge(semof(key), val)
                    ins = o.fn(eng)
                    if o.is_dma:
                        ins.then_inc(dsem[o.dsem], 16)
                    elif o.signal:
                        ins.then_inc(esem[engname], 1)
                if engname == "sp":
                    for k, v in self.dma_cum.items():
                        eng.wait_ge(dsem[k], v)

            @block.tensor
            def _(e):
                run("pe", e)

            @block.scalar
            def _(e):
                run("act", e)

            @block.vector
            def _(e):
                run("dve", e)

            @block.gpsimd
            def _(e):
                run("pool", e)

            @block.sync
            def _(e):
                run("sp", e)


D = 1024
NTOK = 1536
NTT = 3
DFF = 2816
NFC = 22
NH = 8
EPS = 1e-6
NEG = -1e30
GELU_K = 0.7978845608028654
FULL_J0, FULL_N = 2, 14
INT_J0, INT_N = 5, 9
NSLOT = FULL_N + INT_N
SEG = [(0, 256, 0), (256, 256, 259), (512, 1024, 518)]
XLP = 518 + 1027


def _row_start(r):
    return min(max(r - 4, 0), 8)


def attn_plan():
    plan = []
    for qt in range(2):
        blocks = []
        for kb in range(8):
            rows = [r for r in range(8 * qt, 8 * qt + 8)
                    if _row_start(r) <= 2 * kb + 1 and _row_start(r) + 8 > 2 * kb]
            if not rows:
                continue
            assert rows == list(range(rows[0], rows[-1] + 1))
            segs = []
            for r in rows:
                tab = "full" if (r <= 3 or r >= 13) else "int"
                jj = r - 2 * kb + 8
                if segs and segs[-1][0] == tab and segs[-1][1] + segs[-1][2] == jj:
                    segs[-1][2] += 1
                else:
                    segs.append([tab, jj, 1])
            blocks.append((kb, (rows[0] - 8 * qt) * 64, (rows[-1] + 1 - 8 * qt) * 64, segs))
        plan.append(blocks)
    return plan


def host_bias_table(rpb):
    qc = np.arange(64)
    cs = np.clip(qc - 8, 0, 48)
    kc = np.arange(64)
    inwin = (kc[:, None] >= cs[None, :]) & (kc[:, None] < cs[None, :] + 16)
    dc = np.clip(kc[:, None] - qc[None, :], -15, 15) + 15
    out = np.full((NH, 128, NSLOT, 64), NEG, np.float32)
    slot = 0
    for tab, j0, n in (("full", FULL_J0, FULL_N), ("int", INT_J0, INT_N)):
        for jj in range(j0, j0 + n):
            for half in range(2):
                dr = 8 - jj + half
                ok = (abs(dr) <= 7) if tab == "full" else (-4 <= dr <= 3)
                if ok:
                    vals = rpb[:, dr + 7, :][:, dc]
                    out[:, half * 64:(half + 1) * 64, slot, :] = np.where(inwin[None], vals, NEG)
            slot += 1
    return out.reshape(NH, 128, NSLOT * 64)


def slot_of(tab, jj):
    return (jj - FULL_J0) if tab == "full" else (FULL_N + jj - INT_J0)


SKIP = set()


def build_nc(dumps=(), stop_after=None):
    nc = bass.Bass("TRN2", target_bir_lowering=False)
    P = Prog()

    def dram(name, shape, kind="ExternalInput", dt=F32):
        return nc.dram_tensor(name, list(shape), dt, kind=kind).ap()

    xin = dram("xin", [NTOK, D])
    ck_d = dram("ck", [8, 256, 64])
    cv_d = dram("cv", [8, 256, 64])
    pv_d = dram("pv", [256, 128])
    ident_d = dram("ident", [128, 128])
    btab_d = dram("btab", [NH, 128, NSLOT * 64])
    w_mod_d = dram("w_mod", [D, 9 * D])
    f1wi_d = dram("ffn1_w_in", [D, 2 * DFF])
    f1wo_d = dram("ffn1_w_out", [DFF, D])
    w_in_d = dram("w_in", [D, 5632])
    wa_d = dram("lru_wa", [2, 16, 64, 64])
    wi_d = dram("lru_wi", [2, 16, 64, 64])
    wbra_d = dram("w_br_attn", [512, D])
    wbrl_d = dram("w_br_lru", [D, D])
    wout_d = dram("w_out", [D, D])
    f2wi_d = dram("ffn2_w_in", [D, 2 * DFF])
    f2wo_d = dram("ffn2_w_out", [DFF, D])
    y_d = dram("y", [NTOK, D], kind="ExternalOutput")
    nk_d = dram("nk", [2, 8, 256, 64], kind="ExternalOutput")
    nv_d = dram("nv", [2, 8, 256, 64], kind="ExternalOutput")
    ns_d = dram("ns", [32, 128], kind="ExternalOutput")
    winv = w_in_d.rearrange("(kc p) n -> p kc n", p=128)

    dump_outs = {}
    XKEYS = [("xT", 0), ("xT", 1), ("xT", 2)]
    HKEYS = [("hT", 0), ("hT", 1), ("hT", 2)]

    class Scope:
        def __init__(self):
            self.st = ExitStack()

        def __enter__(self):
            self.st.__enter__()
            return self

        def T(self, name, shape, dt=F32):
            return self.st.enter_context(nc.sbuf_tensor(name, list(shape), dt))

        def __exit__(self, *a):
            P.barrier()
            return self.st.__exit__(*a)

    class Rot:
        def __init__(self, banks):
            self.b = list(banks)
            self.i = 0

        def next(self):
            r = self.b[self.i % len(self.b)]
            self.i += 1
            return r

    class WS:
        def __init__(self, sc, name, nslots, shape):
            self.name = name
            self.t = [sc.T("%s%d" % (name, i), shape, BF16) for i in range(nslots)]
            self.n = 0

        def load(self, fn):
            s = self.n % len(self.t)
            self.n += 1
            t = self.t[s]
            key = (self.name, s)
            for o, i in fn(t):
                P.dma("pool", lambda e, o=o, i=i: e.dma_start(out=o, in_=i), key, writes=[key])
            return t, key

    class Pipe:
        def __init__(self, n, depth, load, compute):
            self.n, self.depth, self.load, self.compute = n, depth, load, compute
            self.loaded = []

        def prefetch(self):
            for i in range(min(self.depth, self.n)):
                self.loaded.append(self.load(i))

        def step(self, i):
            if i + self.depth < self.n:
                self.loaded.append(self.load(i + self.depth))
            self.compute(i, *self.loaded[i])

        def run(self):
            if not self.loaded:
                self.prefetch()
            for i in range(self.n):
                self.step(i)

    def pipeline(n, depth, load, compute):
        Pipe(n, depth, load, compute).run()

    evac_i = [0]

    def evac_eng():
        evac_i[0] += 1
        return "act" if evac_i[0] % 2 else "dve"

    def copy_op(eng, out, in_, reads, writes, scale=None):
        if eng == "act":
            if scale is None:
                P.op("act", lambda e: e.activation(out=out, in_=in_, func=AF.Copy), reads, writes)
            else:
                P.op("act", lambda e: e.activation(out=out, in_=in_, func=AF.Copy, scale=scale), reads, writes)
        else:
            if scale is None:
                P.op(eng, lambda e: e.tensor_copy(out=out, in_=in_), reads, writes)
            else:
                P.op(eng, lambda e: e.tensor_scalar(out=out, in0=in_, scalar1=scale, scalar2=None, op0=ALU.mult),
                     reads, writes)

    def mm(out, lhsT, rhs, start, stop, reads, writes):
        P.op("pe", lambda e: e.matmul(out, lhsT=lhsT, rhs=rhs, start=start, stop=stop), reads, writes)

    def dump(name, ap, shape, dt=F32, reads=()):
        if name not in dumps:
            return
        d = dram("dbg_" + name, shape, kind="ExternalOutput", dt=dt)
        dump_outs[name] = d
        P.dma("sp", lambda e: e.dma_start(out=d, in_=ap), ("dump", name), reads=reads)

    with ExitStack() as st:
        def T(name, shape, dt=F32):
            return st.enter_context(nc.sbuf_tensor(name, list(shape), dt))

        ps = [st.enter_context(nc.psum_tensor("ps%d" % i, [128, 512], F32)) for i in range(8)]

        xT = T("xT", [128, 8, NTOK])
        hT = T("hT", [128, 8, NTOK], BF16)
        identf = T("identf", [128, 128])
        identb = T("identb", [128, 128], BF16)
        onesb = T("onesb", [128, 128], BF16)
        PA = T("PA", [128, 128])
        PB = T("PB", [128, 128])
        mods = T("mods", [128, 72, 2])
        Acoef = T("Acoef", [128, 3, 8, 2])
        Gcoef = T("Gcoef", [128, 3, 8, 2])
        lruc = T("lruc", [128, 5, 16])
        ltmp = T("ltmp", [128, 16])
        scb = T("scb", [128, 8, 2], BF16)
        rstd3 = T("rstd3", [128, 3, 512])
        nst = T("nst", [128, 32])
        epsb = T("epsb", [128, 1])
        qb25 = T("qb25", [128, 1])
        oneb = T("oneb", [128, 1])

        bmod = PA[:, 0:72]
        normg = PA[:, 72:96]
        convw = PA[:, 96:128]
        convb = PB[:, 0:8]
        ba_ = PB[:, 8:24]
        bi_ = PB[:, 24:40]
        lam = PB[:, 40:56]
        fing = PB[:, 56:64]
        st0 = PB[:, 64:80]
        cond = PB[:, 80:96]
        PKEY = ["PA", "PB"]

        P.op("pool", lambda e: e.memset(onesb[:], 1.0), writes=["onesb"])
        P.op("pool", lambda e: e.memset(epsb[:], EPS), writes=["epsb"])
        P.op("pool", lambda e: e.memset(qb25[:], 0.25 + 2e-7), writes=["qb25"])
        P.op("pool", lambda e: e.memset(oneb[:], 1.0), writes=["oneb"])
        P.op("pool", lambda e: e.memset(nst[:], 0.0), writes=["nst"])
        P.dma("sp", lambda e: e.dma_start(out=identf[:], in_=ident_d), "c0", writes=["identf"])
        P.dma("pool", lambda e: e.dma_start(out=identb[:], in_=ident_d), "c1", writes=["identb"])
        pstage = rstd3[:, 0, 0:256].rearrange("p (g n) -> p g n", g=2)
        P.dma("sp", lambda e: e.dma_start(out=pstage, in_=pv_d.rearrange("(g r) n -> r g n", g=2)),
              "c2", writes=["pstage"])
        for g_, (dst, key) in enumerate(((PA, "PA"), (PB, "PB"))):
            P.op("pe", lambda e, g_=g_: e.transpose(ps[g_][:, 0:128], pstage[:, g_, :], identf[:]),
                 reads=["pstage", "identf"], writes=[("ps", g_)])
            copy_op("dve", dst[:], ps[g_][:, 0:128], [("ps", g_)], [key])

        for ci in range(2):
            P.op("act", lambda e, ci=ci: e.activation(out=scb[:, :, ci], in_=cond[:, ci * 8:(ci + 1) * 8],
                                                      func=AF.Silu), reads=PKEY, writes=["scb"])
        P.op("act", lambda e: e.activation(out=ltmp[:], in_=lam, func=AF.Exp, scale=-1.0), reads=PKEY, writes=["ltmp"])
        P.op("act", lambda e: e.activation(out=ltmp[:], in_=ltmp[:], func=AF.Ln, bias=oneb[:, 0:1]),
             reads=["ltmp", "oneb"], writes=["ltmp"])
        P.op("dve", lambda e: e.tensor_scalar(out=lruc[:, 0, :], in0=ltmp[:], scalar1=-4.0, scalar2=None, op0=ALU.mult),
             reads=["ltmp"], writes=["lruc"])
        P.op("dve", lambda e: e.tensor_scalar(out=lruc[:, 1, :], in0=ltmp[:], scalar1=-8.0, scalar2=None, op0=ALU.mult),
             reads=["ltmp"], writes=["lruc"])
        P.op("dve", lambda e: e.tensor_scalar(out=lruc[:, 2, :], in0=ltmp[:], scalar1=-8.0, scalar2=float(np.log(0.25)),
                                              op0=ALU.mult, op1=ALU.add), reads=["ltmp"], writes=["lruc"])
        P.op("dve", lambda e: e.tensor_scalar(out=lruc[:, 3, :], in0=ba_, scalar1=0.5, scalar2=None, op0=ALU.mult),
             reads=PKEY, writes=["lruc"])
        P.op("dve", lambda e: e.tensor_scalar(out=lruc[:, 4, :], in0=bi_, scalar1=0.5, scalar2=None, op0=ALU.mult),
             reads=PKEY, writes=["lruc"])

        def norm_sq(sqt, tt):
            cs = slice(tt * 512, (tt + 1) * 512)
            b = 4 + tt
            for h in range(2):
                P.op("act", lambda e, h=h: e.activation(out=sqt[:, 4 * h:4 * h + 4, :], in_=xT[:, 4 * h:4 * h + 4, cs],
                                                        func=AF.Square), reads=[("xT", tt)], writes=[("sqt", h)])
                for c in range(4 * h, 4 * h + 4):
                    mm(ps[b][:], onesb[:], sqt[:, c, :], c == 0, c == 7, [("sqt", h), "onesb"], [("ps", b)])

        def norm_fin(tt):
            b = 4 + tt
            P.op("act", lambda e: e.activation(out=rstd3[:, tt, :], in_=ps[b][:], func=AF.Sqrt, scale=1.0 / D,
                                               bias=epsb[:, 0:1]), reads=[("ps", b), "epsb"], writes=[("rstd", tt)])
            P.op("dve", lambda e: e.reciprocal(out=rstd3[:, tt, :], in_=rstd3[:, tt, :]), reads=[("rstd", tt)],
                 writes=[("rstd", tt)])

        def norm_stats(sqt, tt):
            norm_sq(sqt, tt)
            norm_fin(tt)

        N_MODS_EARLY = 6
        N_MODS_FFN1 = 10
        wmv = w_mod_d.rearrange("(kc p) n -> p kc n", p=128)

        def mods_load(ws, i):
            return ws.load(lambda t: [(t[:], wmv[:, :, i * 512:(i + 1) * 512])])

        def mods_chunk(i, t, key, b):
            for q in range(4):
                for kc in range(8):
                    mm(ps[b][:, 2 * q:2 * q + 2], t[:, kc, q * 128:(q + 1) * 128], scb[:, kc, :],
                       kc == 0, kc == 7, [key, "scb"], [("ps", b)])
            for ci in range(2):
                P.op("dve", lambda e, ci=ci: e.tensor_tensor(
                    out=mods[:, 4 * i:4 * i + 4, ci], in0=ps[b][:, 0:8].rearrange("p (o c) -> p o c", c=2)[:, :, ci],
                    in1=bmod[:, 4 * i:4 * i + 4], op=ALU.add), reads=[("ps", b)] + PKEY, writes=[("mods", i // 2)])

        def mods_coefs(acoef_ks, gcoef_ks):
            for k in acoef_ks:
                for ci in range(2):
                    P.op("dve", lambda e, k=k, ci=ci: e.scalar_tensor_tensor(
                        out=Acoef[:, k, :, ci], in0=mods[:, (3 * k + 1) * 8:(3 * k + 2) * 8, ci], scalar=1.0,
                        in1=normg[:, k * 8:(k + 1) * 8], op0=ALU.add, op1=ALU.mult),
                        reads=[("mods", 3 * k + 1)] + PKEY, writes=[("Acoef", k)])
            for k in gcoef_ks:
                P.op("dve", lambda e, k=k: e.tensor_scalar(
                    out=Gcoef[:, k, :, :], in0=mods[:, (3 * k + 2) * 8:(3 * k + 3) * 8, :], scalar1=0.5, scalar2=None,
                    op0=ALU.mult), reads=[("mods", 3 * k + 2)], writes=[("Gcoef", k)])

        with Scope() as sc:
            xs = [sc.T("xs%d" % i, [128, D]) for i in range(4)]
            sqt_p = sc.T("sqt_p", [128, 8, 512], BF16)
            wsm = WS(sc, "wmod", 3, [128, 8, 512])
            mrot = Rot([0, 1])
            pipe_m = Pipe(N_MODS_EARLY, 2, lambda i: mods_load(wsm, i),
                          lambda i, t, key: mods_chunk(i, t, key, mrot.next()))
            pipe_m.prefetch()
            def x_dma(j):
                s = j % 4
                P.dma("sp", lambda e: e.dma_start(out=xs[s][:], in_=xin[j * 128:(j + 1) * 128, :]),
                      ("xs", s), writes=[("xs", s)])
            for j in range(4):
                x_dma(j)
            fin_at = {5: 0, 9: 1}
            for j in range(12):
                s = j % 4
                for half in range(2):
                    b = 2 + (2 * j + half) % 2
                    for q in range(4):
                        c = half * 4 + q
                        P.op("pe", lambda e, b=b, q=q, c=c, s=s: e.transpose(
                            ps[b][:, q * 128:(q + 1) * 128], xs[s][:, c * 128:(c + 1) * 128], identf[:]),
                            reads=[("xs", s), "identf"], writes=[("ps", b)])
                    copy_op(evac_eng(), xT[:, half * 4:half * 4 + 4, j * 128:(j + 1) * 128],
                            ps[b][:].rearrange("p (q t) -> p q t", q=4), [("ps", b)], [("xT", j // 4)])
                if j + 4 < 12:
                    x_dma(j + 4)
                if j % 2 == 1 and j // 2 < N_MODS_EARLY:
                    pipe_m.step(j // 2)
                if j % 4 == 3:
                    norm_sq(sqt_p, j // 4)
                if j in fin_at:
                    norm_fin(fin_at[j])
            norm_fin(2)
            mods_coefs(acoef_ks=(0,), gcoef_ks=(0,))
        dump("x0", xT[:], [128, 8, NTOK], reads=XKEYS)

        def rmsnorm(sqt, out_fn, stats_done=False):
            def apply(tt):
                ci = 0 if tt == 0 else 1
                cs = slice(tt * 512, (tt + 1) * 512)
                for c in range(8):
                    out_fn(tt, c, ci, cs)
            if stats_done:
                for tt in range(NTT):
                    apply(tt)
                return
            norm_sq(sqt, 0)
            norm_sq(sqt, 1)
            norm_fin(0)
            norm_sq(sqt, 2)
            norm_fin(1)
            apply(0)
            norm_fin(2)
            apply(1)
            apply(2)

        def norm_bufs(sc, k):
            sqt = sc.T("sqt%d" % k, [128, 8, 512], BF16)
            ntmp = [sc.T("ntmp%d_%d" % (k, i), [128, 512]) for i in range(2)]
            return sqt, ntmp

        def norm_to_hT(k, bufs, stats_done=False):
            sqt, ntmp = bufs

            def out_fn(tt, c, ci, cs):
                s = c % 2
                P.op("dve", lambda e: e.scalar_tensor_tensor(
                    out=ntmp[s][:], in0=xT[:, c, cs], scalar=Acoef[:, k, c, ci:ci + 1], in1=rstd3[:, tt, :],
                    op0=ALU.mult, op1=ALU.mult), reads=[("xT", tt), ("rstd", tt), ("Acoef", k)], writes=[("ntmp", s)])
                P.op("act", lambda e: e.activation(out=hT[:, c, cs], in_=ntmp[s][:], func=AF.Identity,
                                                   bias=mods[:, 3 * k * 8 + c, ci:ci + 1]),
                     reads=[("ntmp", s), ("mods", 3 * k)], writes=[("hT", tt)])
            rmsnorm(sqt, out_fn, stats_done)

        def run_ffn(k, wi_d, wo_d):
            with Scope() as sc:
                nb_ = norm_bufs(sc, k)
                hid = sc.T("hid%d" % k, [128, NFC, NTOK], BF16)
                gtmp = [sc.T("gtmp%d_%d" % (k, i), [128, 512]) for i in range(2)]
                wsi = WS(sc, "fwi%d" % k, 3, [128, 8, 512])
                wso = WS(sc, "fwo%d" % k, 2, [128, NFC, 128])
                wiv = wi_d.rearrange("(kc p) n -> p kc n", p=128)
                wov = wo_d.rearrange("(fc p) n -> p fc n", p=128)
                rot = Rot([0, 1, 2, 3])

                def load_i(i):
                    return wsi.load(lambda t: [(t[:, :, 0:256], wiv[:, :, i * 256:(i + 1) * 256]),
                                               (t[:, :, 256:512], wiv[:, :, DFF + i * 256:DFF + (i + 1) * 256])])

                mid = list(range(N_MODS_EARLY, N_MODS_FFN1)) if k == 0 else []
                mws = WS.__new__(WS)
                mws.name, mws.t, mws.n = "wmodf", [nb_[0]], 0
                mld = {}

                def comp_i(i, t, key):
                    if mid and i % 2 == 0 and i // 2 < len(mid):
                        mld[i // 2] = mods_load(mws, mid[i // 2])
                    if mid and i % 2 == 1 and i // 2 < len(mid):
                        mods_chunk(mid[i // 2], *mld[i // 2], 4 + (i // 2) % 2)
                    if i in (7, 9):
                        pre_o.append(load_o(len(pre_o)))
                    for q in range(2):
                        fc = 2 * i + q
                        for tt in range(NTT):
                            cs = slice(tt * 512, (tt + 1) * 512)
                            bg, bu = rot.next(), rot.next()
                            for kc in range(8):
                                mm(ps[bg][:], t[:, kc, q * 128:(q + 1) * 128], hT[:, kc, cs], kc == 0, kc == 7,
                                   [key, ("hT", tt)], [("ps", bg)])
                            for kc in range(8):
                                mm(ps[bu][:], t[:, kc, 256 + q * 128:256 + (q + 1) * 128], hT[:, kc, cs], kc == 0,
                                   kc == 7, [key, ("hT", tt)], [("ps", bu)])
                            s = (fc * NTT + tt) % 2
                            P.op("act", lambda e, bg=bg, s=s: e.activation(out=gtmp[s][:], in_=ps[bg][:], func=AF.Silu),
                                 reads=[("ps", bg)], writes=[("gtmp", s)])
                            P.op("dve", lambda e, bu=bu, s=s, fc=fc, cs=cs: e.tensor_tensor(
                                out=hid[:, fc, cs], in0=gtmp[s][:], in1=ps[bu][:], op=ALU.mult),
                                reads=[("gtmp", s), ("ps", bu)], writes=[("hid", fc, tt)])
                def load_o(oc):
                    return wso.load(lambda t: [(t[:], wov[:, :, oc * 128:(oc + 1) * 128])])

                pipe_i = Pipe(11, 2, load_i, comp_i)
                pipe_i.prefetch()
                pre_o = []
                norm_to_hT(k, nb_, stats_done=(k == 0))
                pipe_i.run()
                rot2 = Rot([4, 5, 6, 7])

                def comp_o(oc, t, key):
                    GK = k
                    for tt in range(NTT):
                        ci = 0 if tt == 0 else 1
                        cs = slice(tt * 512, (tt + 1) * 512)
                        b = rot2.next()
                        for fc in range(NFC):
                            mm(ps[b][:], t[:, fc, :], hid[:, fc, cs], fc == 0, fc == NFC - 1,
                               [key, ("hid", fc, tt)], [("ps", b)])
                        P.op("dve", lambda e, b=b, cs=cs, ci=ci: e.scalar_tensor_tensor(
                            out=xT[:, oc, cs], in0=ps[b][:], scalar=Gcoef[:, k, oc, ci:ci + 1], in1=xT[:, oc, cs],
                            op0=ALU.mult, op1=ALU.add), reads=[("ps", b), ("Gcoef", GK), ("xT", tt)], writes=[("xT", tt)])
                for oc in range(8):
                    comp_o(oc, *pre_o[oc])
                    if oc + 2 < 8:
                        pre_o.append(load_o(oc + 2))
                if k == 0:
                    mods_coefs(acoef_ks=(1,), gcoef_ks=())

        def mixer():
            with Scope() as sm:
                attnT = sm.T("attnT", [128, 4, NTOK], BF16)
                bdw = sm.T("bdw", [128, 2, 2, 8, 128], BF16)

                def load_bdw(d):
                    if d == 0:
                        P.op("pool", lambda e: e.memset(bdw[:], 0.0), writes=["bdw"])
                    for gi, wd in enumerate((wa_d, wi_d)):
                        for e_ in range(2):
                            src_ = wd[d].rearrange("(c e) i o -> e i c o", e=2)[e_]
                            P.dma("pool", lambda e, gi=gi, e_=e_, src_=src_: e.dma_start(
                                out=bdw[e_ * 64:(e_ + 1) * 64, d, gi, :, e_ * 64:(e_ + 1) * 64], in_=src_),
                                "bdw", writes=["bdw"])
                attention(attnT, load_bdw)
                dump("attnT", attnT[:], [128, 4, NTOK], dt=BF16, reads=[("attnT", 0), ("attnT", 1), ("attnT", 2)])
                if stop_after == "attn":
                    return
                lruT = sm.T("lruT", [128, 8, NTOK], BF16)
                lru(lruT, bdw)
                dump("lruT", lruT[:], [128, 8, NTOK], dt=BF16, reads=["lruT"])
                if stop_after == "lru":
                    return
                merge(attnT, lruT)
            dump("x2", xT[:], [128, 8, NTOK], reads=XKEYS)

        def attention(attnT, load_bdw):
            plan = attn_plan()
            with Scope() as sa:
                nb_ = norm_bufs(sa, 1)
                qT = sa.T("qT", [128, NTOK], BF16)
                kTz = sa.T("kTz", [128, 2, NTOK], BF16)
                ckTz = sa.T("ckTz", [128, 2, 256], BF16)
                Vp = sa.T("Vp", [128, 14, 2, 128], BF16)
                btab = sa.T("btab_sb", [128, 2, NSLOT * 64], BF16)
                Eb = [sa.T("Eb%d" % i, [128, 512], BF16) for i in range(3)]
                ktok = [sa.T("ktok%d" % i, [128, 128]) for i in range(8)]
                kcount = [0]
                cst = [sa.T("cst%d" % i, [128, 2, 8, 64]) for i in range(2)]
                rD = sa.T("rD", [128, 512])
                wqkv = WS(sa, "wqkv", 2, [128, 8, 384])
                VK = [("Vp", j) for j in range(14)]
                def setup_memsets():
                    P.op("pool", lambda e: e.memset(kTz[:], 0.0), writes=["kTz"])
                    P.op("pool", lambda e: e.memset(ckTz[:], 0.0), writes=["ckTz"])
                    P.op("pool", lambda e: e.memset(Vp[:], 1.0), writes=VK)
                for w_, src in enumerate((ck_d, cv_d)):
                    for kb in range(2):
                      if "cst" not in SKIP:
                        P.dma("sp", lambda e, w_=w_, src=src, kb=kb: e.dma_start(
                            out=cst[w_][:, kb, :, :], in_=src[:, kb * 128:(kb + 1) * 128, :].rearrange("h p d -> p h d")),
                            ("cst", w_), writes=[("cst", w_)])

                def vp_diag(j):
                    base = Vp[:, j, 0, 0:64]
                    return bass.AP(base.tensor, base.offset, [list(base.ap[0]), [192, 2], [1, 64]])

                rot = Rot([0, 1, 2, 3])
                srot = Rot([0, 1, 2, 3])
                ecount = [0]
                gcount = [0]

                def load(hp):
                    return wqkv.load(lambda t: [(t[:, :, 0:128], winv[:, :, hp * 128:(hp + 1) * 128]),
                                                (t[:, :, 128:256], winv[:, :, 512 + hp * 128:512 + (hp + 1) * 128]),
                                                (t[:, :, 256:384], winv[:, :, 1024 + hp * 128:1024 + (hp + 1) * 128])])

                def s_part(blk):
                    (h, e_, kT_ap, q_ap, ncols, c0, bias_segs, v_ap, rkeys, vkey) = blk["args"]
                    sb = srot.next()
                    nb = len(bias_segs)
                    mm(ps[sb][:, c0:c0 + ncols], kT_ap, q_ap, True, nb == 0, rkeys, [("ps", sb)])
                    off = c0
                    for bi, (tab, jj0, nr) in enumerate(bias_segs):
                        sl = slot_of(tab, jj0)
                        mm(ps[sb][:, off:off + nr * 64], identb[:], btab[:, e_, sl * 64:(sl + nr) * 64], False,
                           bi == nb - 1, ["identb", "btab"], [("ps", sb)])
                        off += nr * 64
                    ei = ecount[0] % len(Eb)
                    ecount[0] += 1
                    blk["ei"] = ei
                    P.op("act", lambda e: e.activation(out=Eb[ei][:, 0:ncols], in_=ps[sb][:, c0:c0 + ncols], func=AF.Exp),
                         reads=[("ps", sb)], writes=[("Eb", ei)])

                def pv_part(blk):
                    (h, e_, kT_ap, q_ap, ncols, c0, bias_segs, v_ap, rkeys, vkey) = blk["args"]
                    ei = blk["ei"]
                    ob = blk["banks"]
                    mm(ps[ob][:, c0:c0 + ncols], v_ap, Eb[ei][:, 0:ncols], blk["first"], blk["last"],
                       [("Eb", ei), vkey], [("ps", ob)])

                def finish(hp, e_, tok0, ncols, c0, tt, ob):
                    pr = slice(e_ * 64, (e_ + 1) * 64)
                    dr = slice((1 - e_) * 64, (2 - e_) * 64)
                    P.op("dve", lambda e: e.reciprocal(out=rD[pr, c0:c0 + ncols], in_=ps[ob][dr, c0:c0 + ncols]),
                         reads=[("ps", ob)], writes=[("rD", e_)])
                    P.op("dve", lambda e: e.tensor_tensor(out=attnT[pr, hp, tok0:tok0 + ncols],
                                                          in0=ps[ob][pr, c0:c0 + ncols], in1=rD[pr, c0:c0 + ncols],
                                                          op=ALU.mult),
                         reads=[("ps", ob), ("rD", e_)], writes=[("attnT", tt)])

                def comp(hp, t, key):
                    for h2 in range(2):
                      if "btab" not in SKIP:
                        P.dma("pool", lambda e, h2=h2: e.dma_start(out=btab[:, h2, :], in_=btab_d[2 * hp + h2]),
                              "btab", writes=["btab"])
                    if "proj" in SKIP:
                        return
                    for tt in range(NTT if "qk" not in SKIP else 0):
                        cs = slice(tt * 512, (tt + 1) * 512)
                        b = rot.next()
                        for kc in range(8):
                            mm(ps[b][:], t[:, kc, 0:128], hT[:, kc, cs], kc == 0, kc == 7, [key, ("hT", tt)], [("ps", b)])
                        copy_op(evac_eng(), qT[:, cs], ps[b][:], [("ps", b)], ["qT"], scale=0.125)
                        b = rot.next()
                        for kc in range(8):
                            mm(ps[b][:], t[:, kc, 128:256], hT[:, kc, cs], kc == 0, kc == 7, [key, ("hT", tt)], [("ps", b)])
                        eng_ = evac_eng()
                        for e_ in range(2):
                            pr = slice(e_ * 64, (e_ + 1) * 64)
                            copy_op(eng_, kTz[pr, e_, cs], ps[b][pr, :], [("ps", b)], ["kTz"])
                    for j in range(4 if "ktok" not in SKIP else 0):
                        b = rot.next()
                        for kc in range(8):
                            mm(ps[b][:, 0:128], hT[:, kc, j * 128:(j + 1) * 128], t[:, kc, 128:256], kc == 0, kc == 7,
                               [key, ("hT", 0)], [("ps", b)])
                        s = kcount[0] % 8
                        kcount[0] += 1
                        copy_op(evac_eng(), ktok[s][:], ps[b][:, 0:128], [("ps", b)], [("ktok", s)])
                        sq, t0 = j // 2, (j % 2) * 128
                        if "kvout" not in SKIP:
                          P.dma("sp", lambda e, s=s, sq=sq, t0=t0: e.dma_start(
                            out=nk_d[sq][2 * hp:2 * hp + 2, t0:t0 + 128, :].rearrange("h t d -> t h d"),
                            in_=ktok[s][:].rearrange("p (h d) -> p h d", h=2)), ("kst", s), reads=[("ktok", s)])
                    for j in range(12 if "vtok" not in SKIP else 0):
                        b = rot.next()
                        for kc in range(8):
                            mm(ps[b][:, 0:128], hT[:, kc, j * 128:(j + 1) * 128], t[:, kc, 256:384], kc == 0, kc == 7,
                               [key, ("hT", j // 4)], [("ps", b)])
                        if j >= 4:
                            copy_op(evac_eng(), vp_diag(j), ps[b][:, 0:128].rearrange("p (e d) -> p e d", e=2),
                                    [("ps", b)], [("Vp", j)])
                        else:
                            s = kcount[0] % 8
                            kcount[0] += 1
                            copy_op("act", ktok[s][:], ps[b][:, 0:128], [("ps", b)], [("ktok", s)])
                            copy_op("dve", vp_diag(j), ktok[s][:].rearrange("p (e d) -> p e d", e=2),
                                    [("ktok", s)], [("Vp", j)])
                            sq, t0 = j // 2, (j % 2) * 128
                            if "kvout" not in SKIP:
                              P.dma("sp", lambda e, s=s, sq=sq, t0=t0: e.dma_start(
                                out=nv_d[sq][2 * hp:2 * hp + 2, t0:t0 + 128, :].rearrange("h t d -> t h d"),
                                in_=ktok[s][:].rearrange("p (h d) -> p h d", h=2)), ("kst", s), reads=[("ktok", s)])
                    for kb in range(2 if "ctx" not in SKIP else 0):
                        b = rot.next()
                        P.op("pe", lambda e, kb=kb, b=b: e.transpose(
                            ps[b][:, 0:128], cst[0][:, kb, 2 * hp:2 * hp + 2, :].rearrange("p h d -> p (h d)"), identf[:]),
                            reads=[("cst", 0), "identf"], writes=[("ps", b)])
                        eng_ = evac_eng()
                        for e_ in range(2):
                            pr = slice(e_ * 64, (e_ + 1) * 64)
                            copy_op(eng_, ckTz[pr, e_, kb * 128:(kb + 1) * 128], ps[b][pr, 0:128],
                                    [("ps", b)], ["ckTz"])
                        copy_op("dve", vp_diag(12 + kb), cst[1][:, kb, 2 * hp:2 * hp + 2, :], [("cst", 1)],
                                [("Vp", 12 + kb)])
                    if hp == 0:
                        dump("qT", qT[:], [128, NTOK], dt=BF16, reads=["qT"])
                        dump("kTz", kTz[:], [128, 2, NTOK], dt=BF16, reads=["kTz"])
                        dump("Vp", Vp[:], [128, 14, 2, 128], dt=BF16, reads=VK)
                        dump("ckTz", ckTz[:], [128, 2, 256], dt=BF16, reads=["ckTz"])
                    groups = []
                    for sq in range(2):
                        for e_ in range(2):
                            blks = []
                            for kb in range(2):
                                k0 = sq * 256 + kb * 128
                                blks.append((2 * hp + e_, e_, kTz[:, e_, k0:k0 + 128], qT[:, sq * 256:(sq + 1) * 256], 256,
                                             sq * 256, [], Vp[:, 2 * sq + kb, e_, :], ["kTz", "qT"], ("Vp", 2 * sq + kb)))
                            groups.append((blks, (hp, e_, sq * 256, 256, sq * 256, 0)))
                    for qt in range(2):
                        q0 = 512 + qt * 512
                        for e_ in range(2):
                            blks = []
                            for kb in range(2):
                                blks.append((2 * hp + e_, e_, ckTz[:, e_, kb * 128:(kb + 1) * 128], qT[:, q0:q0 + 512], 512,
                                             0, [], Vp[:, 12 + kb, e_, :], ["ckTz", "qT"], ("Vp", 12 + kb)))
                            for (kb, c0, c1, segs) in plan[qt]:
                                blks.append((2 * hp + e_, e_, kTz[:, e_, 512 + kb * 128:512 + (kb + 1) * 128],
                                             qT[:, q0 + c0:q0 + c1], c1 - c0, c0, segs, Vp[:, 4 + kb, e_, :],
                                             ["kTz", "qT"], ("Vp", 4 + kb)))
                            groups.append((blks, (hp, e_, q0, 512, 0, 1 + qt)))
                    flat = []
                    for gi, (blks, fin) in enumerate(groups):
                        banks = 4 + (gcount[0] % 4)
                        gcount[0] += 1
                        for bi, args in enumerate(blks):
                            flat.append({"args": args, "first": bi == 0, "last": bi == len(blks) - 1,
                                         "banks": banks, "fin": fin})
                    LA = 2
                    for n in range(min(LA, len(flat))):
                        s_part(flat[n])
                    for n, blk in enumerate(flat):
                        if n + LA < len(flat):
                            s_part(flat[n + LA])
                        pv_part(blk)
                        if blk["last"]:
                            finish(*blk["fin"], blk["banks"])
                wsm2 = WS(sa, "wmod2", 2, [128, 8, 512])
                late = list(range(N_MODS_FFN1, 18))
                mloaded = {}

                def comp2(hp, t, key):
                    for i in late[2 * hp:2 * hp + 2]:
                        mloaded[i] = mods_load(wsm2, i)
                    if hp in (1, 2):
                        load_bdw(hp - 1)
                    comp(hp, t, key)
                    for i in late[2 * hp:2 * hp + 2]:
                        mods_chunk(i, *mloaded[i], rot.next())

                pipe_a = Pipe(4, 1, load, comp2)
                pipe_a.prefetch()
                setup_memsets()
                norm_to_hT(1, nb_)
                dump("h2", hT[:], [128, 8, NTOK], dt=BF16, reads=HKEYS)
                pipe_a.run()
                mods_coefs(acoef_ks=(2,), gcoef_ks=(1, 2))

        def lru(lruT, bdw):
            with Scope() as sl:
                W = 1024
                xlp = sl.T("xlp", [128, W + 6])
                xc = [sl.T("xc%d" % s, [128, W]) for s in range(2)]
                xcb = [sl.T("xcb%d" % s, [128, W], BF16) for s in range(2)]
                tr = [[sl.T("tr%d_%d" % (s, d), [128, W]) for d in range(2)] for s in range(2)]
                ti = [[sl.T("ti%d_%d" % (s, d), [128, W]) for d in range(2)] for s in range(2)]
                a2 = [sl.T("a2%d" % d, [128, W]) for d in range(2)]
                xg = [sl.T("xg%d" % s, [128, W]) for s in range(2)]
                x2 = [sl.T("x2%d" % s, [128, W]) for s in range(2)]
                wl = WS(sl, "wl", 2, [128, 8, 256])
                P.op("pool", lambda e: e.memset(xlp[:], 0.0), writes=["xlp"])
                rot = Rot([0, 1, 2, 3, 4, 5, 6, 7])
                units = [(0, c) for c in range(4)] + [(1, c) for c in range(8)] + [(0, c) for c in range(4, 8)]
                loaded = {}

                def load(i):
                    pss, c = units[i]
                    loaded[i] = wl.load(lambda t: [(t[:, :, 0:128], winv[:, :, 1536 + c * 128:1536 + (c + 1) * 128]),
                                                   (t[:, :, 128:256], winv[:, :, 2560 + c * 128:2560 + (c + 1) * 128])])

                def geom(pss):
                    if pss == 0:
                        return 0, 512, [0], [(0, 256, 0), (256, 256, 259)]
                    return 512, 1024, [1, 2], [(0, 1024, 0)]

                def stage_f1(i):
                    pss, c = units[i]
                    s = i % 2
                    t, key = loaded[i]
                    tok0, ntok, tiles, segs = geom(pss)
                    xl_b, gl_b = [], []
                    for tt in tiles:
                        cs = slice(tt * 512, (tt + 1) * 512)
                        b = rot.next()
                        xl_b.append(b)
                        for kc in range(8):
                            mm(ps[b][:], t[:, kc, 0:128], hT[:, kc, cs], kc == 0, kc == 7, [key, ("hT", tt)], [("ps", b)])
                    for tt in tiles:
                        cs = slice(tt * 512, (tt + 1) * 512)
                        b = rot.next()
                        gl_b.append(b)
                        for kc in range(8):
                            mm(ps[b][:], t[:, kc, 128:256], hT[:, kc, cs], kc == 0, kc == 7, [key, ("hT", tt)], [("ps", b)])
                    if pss == 0 and i > 0 and units[i - 1][0] == 1:
                        P.op("dve", lambda e: e.memset(xlp[:, 258:261], 0.0), reads=["xlp"], writes=["xlp"])
                        P.op("dve", lambda e: e.memset(xlp[:, 517:518], 0.0), reads=["xlp"], writes=["xlp"])
                    if pss == 0:
                        for sq in range(2):
                            copy_op("act", xlp[:, 259 * sq + 2:259 * sq + 258], ps[xl_b[0]][:, sq * 256:(sq + 1) * 256],
                                    [("ps", xl_b[0])], ["xlp"])
                    else:
                        for k_, b in enumerate(xl_b):
                            copy_op("act", xlp[:, 2 + k_ * 512:2 + (k_ + 1) * 512], ps[b][:], [("ps", b)], ["xlp"])
                    for k_, b in enumerate(gl_b):
                        ls = slice(k_ * 512, (k_ + 1) * 512)
                        P.op("act", lambda e, b=b, ls=ls: e.activation(out=xg[s][:, ls], in_=ps[b][:], func=AF.Copy),
                             reads=[("ps", b)], writes=[("xg", s)])
                    for (t0, ln, pb) in segs:
                        P.op("dve", lambda e, t0=t0, ln=ln, pb=pb: e.tensor_scalar(
                            out=xc[s][:, t0:t0 + ln], in0=xlp[:, pb:pb + ln], scalar1=convw[:, c:c + 1],
                            scalar2=convb[:, c:c + 1], op0=ALU.mult, op1=ALU.add), reads=["xlp"] + PKEY, writes=[("xc", s)])
                        for j in range(1, 4):
                            P.op("dve", lambda e, t0=t0, ln=ln, pb=pb, j=j: e.scalar_tensor_tensor(
                                out=xc[s][:, t0:t0 + ln], in0=xlp[:, pb + j:pb + j + ln],
                                scalar=convw[:, j * 8 + c:j * 8 + c + 1], in1=xc[s][:, t0:t0 + ln],
                                op0=ALU.mult, op1=ALU.add), reads=["xlp", ("xc", s)] + PKEY, writes=[("xc", s)])

                def stage_f2(i):
                    pss, c = units[i]
                    s = i % 2
                    tok0, ntok, tiles, segs = geom(pss)
                    P.op("act", lambda e: e.activation(out=xcb[s][:, 0:ntok], in_=xc[s][:, 0:ntok], func=AF.Copy),
                         reads=[("xc", s)], writes=[("xcb", s)])

                def stage_a1(i):
                    pss, c = units[i]
                    s = i % 2
                    tok0, ntok, tiles, segs = geom(pss)
                    P.op("dve", lambda e: e.scalar_tensor_tensor(
                        out=x2[s][:, 0:ntok], in0=xg[s][:, 0:ntok], scalar=0.044715 * GELU_K, in1=xg[s][:, 0:ntok],
                        op0=ALU.mult, op1=ALU.mult), reads=[("xg", s)], writes=[("x2", s)])
                    P.op("dve", lambda e: e.scalar_tensor_tensor(
                        out=x2[s][:, 0:ntok], in0=x2[s][:, 0:ntok], scalar=GELU_K, in1=xg[s][:, 0:ntok],
                        op0=ALU.add, op1=ALU.mult), reads=[("x2", s), ("xg", s)], writes=[("x2", s)])
                    for d in range(2):
                        col = d * 8 + c
                        for gi, dst, dkey, brow in ((0, tr[s][d], ("tr", s, d), 3), (1, ti[s][d], ("ti", s, d), 4)):
                            for k_ in range(len(tiles)):
                                ls = slice(k_ * 512, (k_ + 1) * 512)
                                b = rot.next()
                                mm(ps[b][:], bdw[:, d, gi, c, :], xcb[s][:, ls], True, True, ["bdw", ("xcb", s)], [("ps", b)])
                                P.op("act", lambda e, b=b, ls=ls, dst=dst, brow=brow, col=col: e.activation(
                                    out=dst[:, ls], in_=ps[b][:], func=AF.Tanh, scale=0.5,
                                    bias=lruc[:, brow, col:col + 1]), reads=[("ps", b), "lruc"], writes=[dkey])
                    P.op("act", lambda e: e.activation(out=x2[s][:, 0:ntok], in_=x2[s][:, 0:ntok], func=AF.Tanh),
                         reads=[("x2", s)], writes=[("x2", s)])
                    for d in range(2):
                        col = d * 8 + c
                        P.op("act", lambda e, d=d, col=col: e.activation(
                            out=a2[d][:, 0:ntok], in_=tr[s][d][:, 0:ntok], func=AF.Exp, scale=lruc[:, 1, col:col + 1],
                            bias=lruc[:, 2, col:col + 1]), reads=[("tr", s, d), "lruc"], writes=[("a2", d)])
                        P.op("act", lambda e, d=d, col=col: e.activation(
                            out=tr[s][d][:, 0:ntok], in_=tr[s][d][:, 0:ntok], func=AF.Exp, scale=lruc[:, 0, col:col + 1],
                            bias=lruc[:, 0, col:col + 1]), reads=[("tr", s, d), "lruc"], writes=[("tr", s, d)])
                    for d in range(2):
                        P.op("act", lambda e, d=d: e.activation(out=a2[d][:, 0:ntok], in_=a2[d][:, 0:ntok], func=AF.Sqrt,
                                                                scale=-1.0, bias=qb25[:, 0:1]),
                             reads=[("a2", d), "qb25"], writes=[("a2", d)])

                def stage_a2(i):
                    pss, c = units[i]
                    s = i % 2
                    tok0, ntok, tiles, segs = geom(pss)
                    P.op("dve", lambda e: e.scalar_tensor_tensor(out=x2[s][:, 0:ntok], in0=x2[s][:, 0:ntok], scalar=1.0,
                                                                 in1=xg[s][:, 0:ntok], op0=ALU.add, op1=ALU.mult),
                         reads=[("x2", s), ("xg", s)], writes=[("x2", s)])
                    for d in range(2):
                        P.op("dve", lambda e, d=d: e.scalar_tensor_tensor(
                            out=ti[s][d][:, 0:ntok], in0=ti[s][d][:, 0:ntok], scalar=1.0, in1=xc[s][:, 0:ntok],
                            op0=ALU.add, op1=ALU.mult), reads=[("ti", s, d), ("xc", s)], writes=[("ti", s, d)])
                        P.op("dve", lambda e, d=d: e.tensor_tensor(out=ti[s][d][:, 0:ntok], in0=ti[s][d][:, 0:ntok],
                                                                   in1=a2[d][:, 0:ntok], op=ALU.mult),
                             reads=[("ti", s, d), ("a2", d)], writes=[("ti", s, d)])

                def stage_b(i):
                    pss, c = units[i]
                    s = i % 2
                    tok0, ntok, tiles, segs = geom(pss)
                    for d in range(2):
                        col = d * 8 + c
                        hdst, hkey = ti[s][d], ("ti", s, d)
                        for si, (t0, ln, pb) in enumerate(segs):
                            init = 0.0 if pss == 0 else st0[:, col:col + 1]
                            if d == 0:
                                P.op("dve", lambda e, t0=t0, ln=ln, init=init, hdst=hdst, d=d: e.tensor_tensor_scan(
                                    out=hdst[:, t0:t0 + ln], data0=tr[s][d][:, t0:t0 + ln], data1=ti[s][d][:, t0:t0 + ln],
                                    initial=init, op0=ALU.mult, op1=ALU.add),
                                    reads=[("tr", s, d), ("ti", s, d)] + PKEY, writes=[hkey])
                            else:
                                P.op("dve", lambda e, t0=t0, ln=ln, init=init, hdst=hdst, d=d: e.tensor_tensor_scan(
                                    out=hdst[:, t0:t0 + ln][:, ::-1], data0=tr[s][d][:, t0:t0 + ln][:, ::-1],
                                    data1=ti[s][d][:, t0:t0 + ln][:, ::-1], initial=init,
                                    op0=ALU.mult, op1=ALU.add), reads=[("tr", s, d), ("ti", s, d)] + PKEY, writes=[hkey])
                            if pss == 0:
                                srccol = (t0 + ln - 1) if d == 0 else t0
                                dcol = si * 16 + d * 8 + c
                                P.op("dve", lambda e, srccol=srccol, dcol=dcol, hdst=hdst: e.tensor_copy(
                                    out=nst[:, dcol:dcol + 1], in_=hdst[:, srccol:srccol + 1]),
                                    reads=[hkey], writes=["nst"])
                    P.op("pool", lambda e: e.tensor_tensor(out=ti[s][0][:, 0:ntok], in0=ti[s][0][:, 0:ntok],
                                                           in1=ti[s][1][:, 0:ntok], op=ALU.add),
                         reads=[("ti", s, 0), ("ti", s, 1)], writes=[("ti", s, 0)])

                def stage_b2(i):
                    pss, c = units[i]
                    s = i % 2
                    tok0, ntok, tiles, segs = geom(pss)
                    P.op("dve", lambda e: e.scalar_tensor_tensor(out=lruT[:, c, tok0:tok0 + ntok], in0=x2[s][:, 0:ntok],
                                                                 scalar=0.5, in1=ti[s][0][:, 0:ntok], op0=ALU.mult,
                                                                 op1=ALU.mult),
                         reads=[("x2", s), ("ti", s, 0)], writes=["lruT"])

                n = len(units)
                load(0)
                load(1)
                stage_f1(0)
                load(2)
                stage_f2(0)
                stage_f1(1)
                load(3)
                stage_a1(0)
                stage_f2(1)
                stage_a2(0)
                for i in range(n):
                    if i + 2 < n:
                        stage_f1(i + 2)
                        if i + 4 < n:
                            load(i + 4)
                    if i + 1 < n:
                        stage_a1(i + 1)
                    if i + 2 < n:
                        stage_f2(i + 2)
                    stage_b(i)
                    if i + 1 < n:
                        stage_a2(i + 1)
                    stage_b2(i)

        def merge(attnT, lruT):
            with Scope() as sg:
                mT = sg.T("mT", [128, 8, NTOK], BF16)
                wba = sg.T("wba", [128, 4, D], BF16)
                wbl = sg.T("wbl", [128, 8, D], BF16)
                wo = sg.T("wo", [128, 8, D], BF16)
                sga = [sg.T("sga%d" % i, [128, 512]) for i in range(2)]
                sgb = [sg.T("sgb%d" % i, [128, 512]) for i in range(2)]
                wg = WS(sg, "wg", 2, [128, 8, 256])
                rot = Rot([0, 1, 2, 3, 4, 5, 6, 7])

                def load(oc):
                    r = wg.load(lambda t: [(t[:, :, 0:128], winv[:, :, 3584 + oc * 128:3584 + (oc + 1) * 128]),
                                           (t[:, :, 128:256], winv[:, :, 4608 + oc * 128:4608 + (oc + 1) * 128])])
                    if oc == 0:
                        for part, (c0_, c1_) in enumerate(((0, 256), (256, D))):
                            P.dma("pool", lambda e, c0_=c0_, c1_=c1_: e.dma_start(
                                out=wba[:, :, c0_:c1_], in_=wbra_d.rearrange("(kc p) n -> p kc n", p=128)[:, :, c0_:c1_]),
                                ("wba", part), writes=[("wba", part)])
                            P.dma("pool", lambda e, c0_=c0_, c1_=c1_: e.dma_start(
                                out=wbl[:, :, c0_:c1_], in_=wbrl_d.rearrange("(kc p) n -> p kc n", p=128)[:, :, c0_:c1_]),
                                ("wbl", part), writes=[("wbl", part)])
                    return r

                def comp(oc, t, key):
                    if oc == 2:
                        P.dma("pool", lambda e: e.dma_start(out=wo[:], in_=wout_d.rearrange("(kc p) n -> p kc n", p=128)),
                              "wo", writes=["wo"])
                    for tt in range(NTT):
                        cs = slice(tt * 512, (tt + 1) * 512)
                        s = (oc * NTT + tt) % 2
                        bga, bgb, bpa, bpl = rot.next(), rot.next(), rot.next(), rot.next()
                        for kc in range(8):
                            mm(ps[bga][:], t[:, kc, 0:128], hT[:, kc, cs], kc == 0, kc == 7, [key, ("hT", tt)], [("ps", bga)])
                        for kc in range(8):
                            mm(ps[bgb][:], t[:, kc, 128:256], hT[:, kc, cs], kc == 0, kc == 7, [key, ("hT", tt)], [("ps", bgb)])
                        wpart = 0 if oc < 2 else 1
                        for kc in range(4):
                            mm(ps[bpa][:], wba[:, kc, oc * 128:(oc + 1) * 128], attnT[:, kc, cs], kc == 0, kc == 3,
                               [("wba", wpart), ("attnT", tt)], [("ps", bpa)])
                        for kc in range(8):
                            mm(ps[bpl][:], wbl[:, kc, oc * 128:(oc + 1) * 128], lruT[:, kc, cs], kc == 0, kc == 7,
                               [("wbl", wpart), "lruT"], [("ps", bpl)])
                        P.op("act", lambda e, bga=bga, s=s: e.activation(out=sga[s][:], in_=ps[bga][:], func=AF.Tanh, scale=0.5),
                             reads=[("ps", bga)], writes=[("sga", s)])
                        P.op("act", lambda e, bgb=bgb, s=s: e.activation(out=sgb[s][:], in_=ps[bgb][:], func=AF.Tanh, scale=0.5),
                             reads=[("ps", bgb)], writes=[("sgb", s)])
                        P.op("dve", lambda e, bpa=bpa, s=s: e.scalar_tensor_tensor(
                            out=sga[s][:], in0=sga[s][:], scalar=1.0, in1=ps[bpa][:], op0=ALU.add, op1=ALU.mult),
                            reads=[("sga", s), ("ps", bpa)], writes=[("sga", s)])
                        P.op("dve", lambda e, bpl=bpl, s=s: e.scalar_tensor_tensor(
                            out=sgb[s][:], in0=sgb[s][:], scalar=1.0, in1=ps[bpl][:], op0=ALU.add, op1=ALU.mult),
                            reads=[("sgb", s), ("ps", bpl)], writes=[("sgb", s)])
                        P.op("dve", lambda e, s=s, cs=cs: e.tensor_tensor(
                            out=mT[:, oc, cs], in0=sga[s][:], in1=sgb[s][:], op=ALU.add),
                            reads=[("sga", s), ("sgb", s)], writes=[("mT", tt)])
                pipeline(8, 1, load, comp)
                GK = 1
                for oc in range(8):
                    for tt in range(NTT):
                        ci = 0 if tt == 0 else 1
                        cs = slice(tt * 512, (tt + 1) * 512)
                        b = rot.next()
                        for kc in range(8):
                            mm(ps[b][:], wo[:, kc, oc * 128:(oc + 1) * 128], mT[:, kc, cs], kc == 0, kc == 7,
                               ["wo", ("mT", tt)], [("ps", b)])
                        P.op("dve", lambda e, b=b, oc=oc, cs=cs, ci=ci: e.scalar_tensor_tensor(
                            out=xT[:, oc, cs], in0=ps[b][:], scalar=Gcoef[:, 1, oc, ci:ci + 1], in1=xT[:, oc, cs],
                            op0=ALU.mult, op1=ALU.add), reads=[("ps", b), ("Gcoef", GK), ("xT", tt)], writes=[("xT", tt)])

        if stop_after != "mods":
            run_ffn(0, f1wi_d, f1wo_d)
            dump("x1", xT[:], [128, 8, NTOK], reads=XKEYS)
        if stop_after not in ("mods", "ffn1"):
            mixer()
        if stop_after is None:
            run_ffn(2, f2wi_d, f2wo_d)
            dump("x3", xT[:], [128, 8, NTOK], reads=XKEYS)

        with Scope() as sc:
            ytok = [sc.T("ytok%d" % i, [128, D]) for i in range(4)]
            gbc = sc.T("gbc", [128, D])
            grow = sc.T("grow", [1, D])
            onesr = sc.T("onesr", [1, 128])
            junk = [sc.T("junk%d" % i, [128, 512], BF16) for i in range(2)]
            ssq = sc.T("ssq", [128, 24])
            rs = sc.T("rs", [128, 12])
            nsT = sc.T("nsT", [32, 128])
            P.dma("sp", lambda e: e.dma_start(out=grow[:], in_=pv_d[184:192, :].rearrange("(o r) n -> o (r n)", o=1)),
                  "grow", writes=["grow"])
            P.op("pool", lambda e: e.memset(onesr[:], 1.0), writes=["onesr"])
            for half in range(2):
                P.op("pe", lambda e, half=half: e.matmul(ps[half][:], lhsT=onesr[:], rhs=grow[:, half * 512:(half + 1) * 512],
                                                         start=True, stop=True),
                     reads=["onesr", "grow"], writes=[("ps", half)])
                copy_op("dve", gbc[:, half * 512:(half + 1) * 512], ps[half][:], [("ps", half)], [("gbc", half)])
            frot = Rot([2, 3, 4, 5, 6, 7, 0, 1])
            for j in range(12):
                tt = j // 4
                s = j % 4
                bb = [frot.next(), frot.next()]
                for half in range(2):
                    b = bb[half]
                    for q in range(4):
                        c2 = half * 4 + q
                        P.op("pe", lambda e, b=b, q=q, c2=c2, j=j: e.transpose(
                            ps[b][:, q * 128:(q + 1) * 128], xT[:, c2, j * 128:(j + 1) * 128], identf[:]),
                            reads=[("xT", tt), "identf"], writes=[("ps", b)])
                    P.op("act", lambda e, b=b, half=half, j=j: e.activation(
                        out=junk[half][:], in_=ps[b][:], func=AF.Square,
                        accum_out=ssq[:, 2 * j + half:2 * j + half + 1]),
                        reads=[("ps", b)], writes=[("junk", half), ("ssq", j, half)])
                P.op("dve", lambda e, j=j: e.tensor_tensor(out=rs[:, j:j + 1], in0=ssq[:, 2 * j:2 * j + 1],
                                                           in1=ssq[:, 2 * j + 1:2 * j + 2], op=ALU.add),
                     reads=[("ssq", j, 0), ("ssq", j, 1)], writes=[("rs", j)])
                P.op("act", lambda e, j=j: e.activation(out=rs[:, j:j + 1], in_=rs[:, j:j + 1], func=AF.Sqrt,
                                                        scale=1.0 / D, bias=epsb[:, 0:1]),
                     reads=[("rs", j), "epsb"], writes=[("rs", j)])
                P.op("dve", lambda e, j=j: e.reciprocal(out=rs[:, j:j + 1], in_=rs[:, j:j + 1]),
                     reads=[("rs", j)], writes=[("rs", j)])
                for half in range(2):
                    b = bb[half]
                    P.op("dve", lambda e, b=b, half=half, j=j, s=s: e.scalar_tensor_tensor(
                        out=ytok[s][:, half * 512:(half + 1) * 512], in0=ps[b][:], scalar=rs[:, j:j + 1],
                        in1=gbc[:, half * 512:(half + 1) * 512], op0=ALU.mult, op1=ALU.mult),
                        reads=[("ps", b), ("rs", j), ("gbc", half)], writes=[("ytok", s, half)])
                P.dma("sp", lambda e, j=j, s=s: e.dma_start(out=y_d[j * 128:(j + 1) * 128, :], in_=ytok[s][:]),
                      ("yst", s), reads=[("ytok", s, 0), ("ytok", s, 1)])

            P.op("pe", lambda e: e.transpose(ps[7][0:32, 0:128], nst[:], identf[:]), reads=["nst", "identf"],
                 writes=[("ps", 7)])
            copy_op("dve", nsT[:], ps[7][0:32, 0:128], [("ps", 7)], ["nsT"])
            P.dma("sp", lambda e: e.dma_start(out=ns_d, in_=nsT[:]), "nsst", reads=["nsT"])

        P.emit(nc)
    return nc, dump_outs


_NC_CACHE = {}


def make_in_maps(inp):
    f = lambda a: np.ascontiguousarray(np.asarray(a, dtype=np.float32))
    x_prompt = f(inp["x_prompt"]); x_sample = f(inp["x_sample"])
    cache_k = f(inp["cache_k"]); cache_v = f(inp["cache_v"]); state = f(inp["state_lru"])
    c = f(inp["c"]); c_ctx = f(inp["c_ctx"])
    shared = {
        "ident": np.eye(128, dtype=np.float32),
        "btab": host_bias_table(f(inp["rpb"])[0]),
        "w_mod": f(inp["w_mod"])[0], "ffn1_w_in": f(inp["ffn1_w_in"])[0], "ffn1_w_out": f(inp["ffn1_w_out"])[0],
        "w_in": f(inp["w_in"])[0], "lru_wa": f(inp["lru_wa"])[0], "lru_wi": f(inp["lru_wi"])[0],
        "w_br_attn": f(inp["w_br_attn"])[0], "w_br_lru": f(inp["w_br_lru"])[0], "w_out": f(inp["w_out"])[0],
        "ffn2_w_in": f(inp["ffn2_w_in"])[0], "ffn2_w_out": f(inp["ffn2_w_out"])[0],
    }
    common_rows = [f(inp["b_mod"])[0].reshape(72, 128), f(inp["norm_g"])[0].reshape(24, 128),
                   f(inp["conv_w"])[0].reshape(32, 128), f(inp["conv_b"])[0].reshape(8, 128),
                   f(inp["lru_ba"])[0].reshape(16, 128), f(inp["lru_bi"])[0].reshape(16, 128),
                   f(inp["lru_lambda"])[0].reshape(16, 128), f(inp["final_g"]).reshape(8, 128)]
    maps = []
    for i in range(8):
        pv = np.zeros((256, 128), np.float32)
        rows = common_rows + [state[i, 0].reshape(16, 128), c_ctx.reshape(8, 128), c[i].reshape(8, 128)]
        cat = np.concatenate(rows, axis=0)
        pv[:cat.shape[0]] = cat
        m = dict(shared)
        m["xin"] = np.concatenate([x_prompt[2 * i:2 * i + 2].reshape(512, D), x_sample[i]], axis=0)
        m["ck"] = cache_k[i, 0]
        m["cv"] = cache_v[i, 0]
        m["pv"] = pv
        maps.append(m)
    return maps


def kernel(**inputs):
    if "nc" not in _NC_CACHE:
        _NC_CACHE["nc"] = build_nc()[0]
    nc = _NC_CACHE["nc"]
    maps = make_in_maps(inputs)
    res = run_bass_kernel_spmd(nc, maps, core_ids=list(range(8)))
    r = res.results
    y_prompt = np.concatenate([r[i]["y"][:512].reshape(2, 256, D) for i in range(8)], axis=0).astype(np.float32)
    y_sample = np.stack([r[i]["y"][512:] for i in range(8)], axis=0).astype(np.float32)
    nk = np.concatenate([r[i]["nk"] for i in range(8)], axis=0)[:, None].astype(np.float32)
    nv = np.concatenate([r[i]["nv"] for i in range(8)], axis=0)[:, None].astype(np.float32)
    ns = np.concatenate([r[i]["ns"].reshape(2, 2, D) for i in range(8)], axis=0)[:, None].astype(np.float32)
    return (y_prompt, y_sample, nk, nv, ns)
```

```python
from contextlib import ExitStack
import numpy as np
import concourse.bass as bass
import concourse.mybir as mybir
from concourse.bass_utils import run_bass_kernel_spmd

F32 = mybir.dt.float32
BF16 = mybir.dt.bfloat16
AF = mybir.ActivationFunctionType
ALU = mybir.AluOpType

ENGINES = ("pe", "act", "dve", "pool", "sp")


class _Op:
    __slots__ = ("idx", "eng", "fn", "reads", "writes", "is_dma", "dsem", "dcum", "signal", "count", "waits",
                 "bar", "snap")

    def __init__(self, idx, eng, fn, reads, writes, is_dma, dsem):
        self.idx = idx
        self.eng = eng
        self.fn = fn
        self.reads = tuple(reads)
        self.writes = tuple(writes)
        self.is_dma = is_dma
        self.dsem = dsem
        self.dcum = 0
        self.signal = False
        self.count = 0
        self.waits = []
        self.bar = -1
        self.snap = None


class Prog:
    def __init__(self):
        self.ops = []
        self.dma_cum = {}
        self.nbar = 0

    def barrier(self):
        if self.ops and self.ops[-1].bar >= 0:
            return
        for e in ENGINES:
            o = _Op(len(self.ops), e, None, (), (), False, None)
            o.bar = self.nbar
            o.snap = dict(self.dma_cum)
            self.ops.append(o)
        self.nbar += 1

    def op(self, eng, fn, reads=(), writes=()):
        o = _Op(len(self.ops), eng, fn, reads, writes, False, None)
        self.ops.append(o)
        return o

    def dma(self, eng, fn, sem, reads=(), writes=()):
        o = _Op(len(self.ops), eng, fn, reads, writes, True, sem)
        self.dma_cum[sem] = self.dma_cum.get(sem, 0) + 16
        o.dcum = self.dma_cum[sem]
        self.ops.append(o)
        return o

    def _analyze(self):
        ops = self.ops
        last_write = {}
        readers = {}
        cum_now = {}
        for o in ops:
            if o.bar >= 0:
                last_write = {}
                readers = {}
                o.waits = ({}, {})
                continue
            raw = set()
            war = set()
            for b in o.reads:
                if b in last_write:
                    raw.add(last_write[b])
            for b in o.writes:
                if b in last_write:
                    raw.add(last_write[b])
                for r in readers.get(b, ()):
                    war.add(r)
            raw.discard(o.idx)
            war.discard(o.idx)
            war -= raw
            dma_w = {}
            eng_deps = {}
            for d in raw | war:
                p = ops[d]
                if p.is_dma:
                    dma_w[p.dsem] = max(dma_w.get(p.dsem, 0), cum_now.get(p.dsem, 0))
                    continue
                if p.eng == o.eng and not o.is_dma:
                    if o.eng == "pe":
                        continue
                p.signal = True
                eng_deps.setdefault(p.eng, []).append(d)
            o.waits = (dma_w, eng_deps)
            if o.is_dma:
                cum_now[o.dsem] = o.dcum
            for b in o.reads:
                readers.setdefault(b, []).append(o.idx)
            for b in o.writes:
                last_write[b] = o.idx
                readers[b] = []
        cnt = {e: 0 for e in ENGINES}
        for o in ops:
            if o.signal:
                cnt[o.eng] += 1
                o.count = cnt[o.eng]
        waited = {e: {} for e in ENGINES}
        sofar = {e: 0 for e in ENGINES}
        for o in ops:
            if o.signal:
                sofar[o.eng] = o.count
            wd = waited[o.eng]
            if o.bar >= 0:
                for e in ENGINES:
                    wd[("e", e)] = max(wd.get(("e", e), 0), sofar[e])
                for s, v in o.snap.items():
                    wd[("d", s)] = max(wd.get(("d", s), 0), v)
                o.waits = []
                continue
            dma_w, eng_deps = o.waits
            fin = []
            for s, v in dma_w.items():
                k = ("d", s)
                if v > wd.get(k, 0):
                    wd[k] = v
                    fin.append((k, v))
            for e, ds in eng_deps.items():
                v = max(ops[d].count for d in ds)
                k = ("e", e)
                if v > wd.get(k, 0):
                    wd[k] = v
                    fin.append((k, v))
            o.waits = fin

    def emit(self, nc):
        self._analyze()
        ops = self.ops
        with ExitStack() as st:
            esem = {e: st.enter_context(nc.semaphore("s_" + e)) for e in ENGINES}
            bsem = st.enter_context(nc.semaphore("s_bar"))
            dsem = {k: st.enter_context(nc.semaphore("d_%d" % i)) for i, k in enumerate(self.dma_cum)}
            block = st.enter_context(nc.Block())

            def semof(key):
                return esem[key[1]] if key[0] == "e" else dsem[key[1]]

            def run(engname, eng):
                for o in ops:
                    if o.eng != engname:
                        continue
                    if o.bar >= 0:
                        eng.drain().then_inc(bsem, 1)
                        eng.wait_ge(bsem, len(ENGINES) * (o.bar + 1))
                        for k, v in o.snap.items():
                            eng.wait_ge(dsem[k], v)
                        continue
                    for key, val in o.waits:
                        eng.wait_ge(semof(key), val)
                    ins = o.fn(eng)
                    if o.is_dma:
                        ins.then_inc(dsem[o.dsem], 16)
                    elif o.signal:
                        ins.then_inc(esem[engname], 1)
                if engname == "sp":
                    for k, v in self.dma_cum.items():
                        eng.wait_ge(dsem[k], v)

            @block.tensor
            def _(e):
                run("pe", e)

            @block.scalar
            def _(e):
                run("act", e)

            @block.vector
            def _(e):
                run("dve", e)

            @block.gpsimd
            def _(e):
                run("pool", e)

            @block.sync
            def _(e):
                run("sp", e)


D = 1024
NTOK = 1536
NTT = 3
DFF = 2816
NFC = 22
NH = 8
EPS = 1e-6
NEG = -1e30
GELU_K = 0.7978845608028654
FULL_J0, FULL_N = 2, 14
INT_J0, INT_N = 5, 9
NSLOT = FULL_N + INT_N
SEG = [(0, 256, 0), (256, 256, 259), (512, 1024, 518)]
XLP = 518 + 1027


def _row_start(r):
    return min(max(r - 4, 0), 8)


def attn_plan():
    plan = []
    for qt in range(2):
        blocks = []
        for kb in range(8):
            rows = [r for r in range(8 * qt, 8 * qt + 8)
                    if _row_start(r) <= 2 * kb + 1 and _row_start(r) + 8 > 2 * kb]
            if not rows:
                continue
            assert rows == list(range(rows[0], rows[-1] + 1))
            segs = []
            for r in rows:
                tab = "full" if (r <= 3 or r >= 13) else "int"
                jj = r - 2 * kb + 8
                if segs and segs[-1][0] == tab and segs[-1][1] + segs[-1][2] == jj:
                    segs[-1][2] += 1
                else:
                    segs.append([tab, jj, 1])
            blocks.append((kb, (rows[0] - 8 * qt) * 64, (rows[-1] + 1 - 8 * qt) * 64, segs))
        plan.append(blocks)
    return plan


def host_bias_table(rpb):
    qc = np.arange(64)
    cs = np.clip(qc - 8, 0, 48)
    kc = np.arange(64)
    inwin = (kc[:, None] >= cs[None, :]) & (kc[:, None] < cs[None, :] + 16)
    dc = np.clip(kc[:, None] - qc[None, :], -15, 15) + 15
    out = np.full((NH, 128, NSLOT, 64), NEG, np.float32)
    slot = 0
    for tab, j0, n in (("full", FULL_J0, FULL_N), ("int", INT_J0, INT_N)):
        for jj in range(j0, j0 + n):
            for half in range(2):
                dr = 8 - jj + half
                ok = (abs(dr) <= 7) if tab == "full" else (-4 <= dr <= 3)
                if ok:
                    vals = rpb[:, dr + 7, :][:, dc]
                    out[:, half * 64:(half + 1) * 64, slot, :] = np.where(inwin[None], vals, NEG)
            slot += 1
    return out.reshape(NH, 128, NSLOT * 64)


def slot_of(tab, jj):
    return (jj - FULL_J0) if tab == "full" else (FULL_N + jj - INT_J0)


SKIP = set()


def build_nc(dumps=(), stop_after=None):
    nc = bass.Bass("TRN2", target_bir_lowering=False)
    P = Prog()

    def dram(name, shape, kind="ExternalInput", dt=F32):
        return nc.dram_tensor(name, list(shape), dt, kind=kind).ap()

    xin = dram("xin", [NTOK, D])
    ck_d = dram("ck", [8, 256, 64])
    cv_d = dram("cv", [8, 256, 64])
    pv_d = dram("pv", [256, 128])
    ident_d = dram("ident", [128, 128])
    btab_d = dram("btab", [NH, 128, NSLOT * 64])
    w_mod_d = dram("w_mod", [D, 9 * D])
    f1wi_d = dram("ffn1_w_in", [D, 2 * DFF])
    f1wo_d = dram("ffn1_w_out", [DFF, D])
    w_in_d = dram("w_in", [D, 5632])
    wa_d = dram("lru_wa", [2, 16, 64, 64])
    wi_d = dram("lru_wi", [2, 16, 64, 64])
    wbra_d = dram("w_br_attn", [512, D])
    wbrl_d = dram("w_br_lru", [D, D])
    wout_d = dram("w_out", [D, D])
    f2wi_d = dram("ffn2_w_in", [D, 2 * DFF])
    f2wo_d = dram("ffn2_w_out", [DFF, D])
    y_d = dram("y", [NTOK, D], kind="ExternalOutput")
    nk_d = dram("nk", [2, 8, 256, 64], kind="ExternalOutput")
    nv_d = dram("nv", [2, 8, 256, 64], kind="ExternalOutput")
    ns_d = dram("ns", [32, 128], kind="ExternalOutput")
    winv = w_in_d.rearrange("(kc p) n -> p kc n", p=128)

    dump_outs = {}
    XKEYS = [("xT", 0), ("xT", 1), ("xT", 2)]
    HKEYS = [("hT", 0), ("hT", 1), ("hT", 2)]

    class Scope:
        def __init__(self):
            self.st = ExitStack()

        def __enter__(self):
            self.st.__enter__()
            return self

        def T(self, name, shape, dt=F32):
            return self.st.enter_context(nc.sbuf_tensor(name, list(shape), dt))

        def __exit__(self, *a):
            P.barrier()
            return self.st.__exit__(*a)

    class Rot:
        def __init__(self, banks):
            self.b = list(banks)
            self.i = 0

        def next(self):
            r = self.b[self.i % len(self.b)]
            self.i += 1
            return r

    class WS:
        def __init__(self, sc, name, nslots, shape):
            self.name = name
            self.t = [sc.T("%s%d" % (name, i), shape, BF16) for i in range(nslots)]
            self.n = 0

        def load(self, fn):
            s = self.n % len(self.t)
            self.n += 1
            t = self.t[s]
            key = (self.name, s)
            for o, i in fn(t):
                P.dma("pool", lambda e, o=o, i=i: e.dma_start(out=o, in_=i), key, writes=[key])
            return t, key

    class Pipe:
        def __init__(self, n, depth, load, compute):
            self.n, self.depth, self.load, self.compute = n, depth, load, compute
            self.loaded = []

        def prefetch(self):
            for i in range(min(self.depth, self.n)):
                self.loaded.append(self.load(i))

        def step(self, i):
            if i + self.depth < self.n:
                self.loaded.append(self.load(i + self.depth))
            self.compute(i, *self.loaded[i])

        def run(self):
            if not self.loaded:
                self.prefetch()
            for i in range(self.n):
                self.step(i)

    def pipeline(n, depth, load, compute):
        Pipe(n, depth, load, compute).run()

    evac_i = [0]

    def evac_eng():
        evac_i[0] += 1
        return "act" if evac_i[0] % 2 else "dve"

    def copy_op(eng, out, in_, reads, writes, scale=None):
        if eng == "act":
            if scale is None:
                P.op("act", lambda e: e.activation(out=out, in_=in_, func=AF.Copy), reads, writes)
            else:
                P.op("act", lambda e: e.activation(out=out, in_=in_, func=AF.Copy, scale=scale), reads, writes)
        else:
            if scale is None:
                P.op(eng, lambda e: e.tensor_copy(out=out, in_=in_), reads, writes)
            else:
                P.op(eng, lambda e: e.tensor_scalar(out=out, in0=in_, scalar1=scale, scalar2=None, op0=ALU.mult),
                     reads, writes)

    def mm(out, lhsT, rhs, start, stop, reads, writes):
        P.op("pe", lambda e: e.matmul(out, lhsT=lhsT, rhs=rhs, start=start, stop=stop), reads, writes)

    def dump(name, ap, shape, dt=F32, reads=()):
        if name not in dumps:
            return
        d = dram("dbg_" + name, shape, kind="ExternalOutput", dt=dt)
        dump_outs[name] = d
        P.dma("sp", lambda e: e.dma_start(out=d, in_=ap), ("dump", name), reads=reads)

    with ExitStack() as st:
        def T(name, shape, dt=F32):
            return st.enter_context(nc.sbuf_tensor(name, list(shape), dt))

        ps = [st.enter_context(nc.psum_tensor("ps%d" % i, [128, 512], F32)) for i in range(8)]

        xT = T("xT", [128, 8, NTOK])
        hT = T("hT", [128, 8, NTOK], BF16)
        identf = T("identf", [128, 128])
        identb = T("identb", [128, 128], BF16)
        onesb = T("onesb", [128, 128], BF16)
        PA = T("PA", [128, 128])
        PB = T("PB", [128, 128])
        mods = T("mods", [128, 72, 2])
        Acoef = T("Acoef", [128, 3, 8, 2])
        Gcoef = T("Gcoef", [128, 3, 8, 2])
        lruc = T("lruc", [128, 5, 16])
        ltmp = T("ltmp", [128, 16])
        scb = T("scb", [128, 8, 2], BF16)
        rstd3 = T("rstd3", [128, 3, 512])
        nst = T("nst", [128, 32])
        epsb = T("epsb", [128, 1])
        qb25 = T("qb25", [128, 1])
        oneb = T("oneb", [128, 1])

        bmod = PA[:, 0:72]
        normg = PA[:, 72:96]
        convw = PA[:, 96:128]
        convb = PB[:, 0:8]
        ba_ = PB[:, 8:24]
        bi_ = PB[:, 24:40]
        lam = PB[:, 40:56]
        fing = PB[:, 56:64]
        st0 = PB[:, 64:80]
        cond = PB[:, 80:96]
        PKEY = ["PA", "PB"]

        P.op("pool", lambda e: e.memset(onesb[:], 1.0), writes=["onesb"])
        P.op("pool", lambda e: e.memset(epsb[:], EPS), writes=["epsb"])
        P.op("pool", lambda e: e.memset(qb25[:], 0.25 + 2e-7), writes=["qb25"])
        P.op("pool", lambda e: e.memset(oneb[:], 1.0), writes=["oneb"])
        P.op("pool", lambda e: e.memset(nst[:], 0.0), writes=["nst"])
        P.dma("sp", lambda e: e.dma_start(out=identf[:], in_=ident_d), "c0", writes=["identf"])
        P.dma("pool", lambda e: e.dma_start(out=identb[:], in_=ident_d), "c1", writes=["identb"])
        pstage = rstd3[:, 0, 0:256].rearrange("p (g n) -> p g n", g=2)
        P.dma("sp", lambda e: e.dma_start(out=pstage, in_=pv_d.rearrange("(g r) n -> r g n", g=2)),
              "c2", writes=["pstage"])
        for g_, (dst, key) in enumerate(((PA, "PA"), (PB, "PB"))):
            P.op("pe", lambda e, g_=g_: e.transpose(ps[g_][:, 0:128], pstage[:, g_, :], identf[:]),
                 reads=["pstage", "identf"], writes=[("ps", g_)])
            copy_op("dve", dst[:], ps[g_][:, 0:128], [("ps", g_)], [key])

        for ci in range(2):
            P.op("act", lambda e, ci=ci: e.activation(out=scb[:, :, ci], in_=cond[:, ci * 8:(ci + 1) * 8],
                                                      func=AF.Silu), reads=PKEY, writes=["scb"])
        P.op("act", lambda e: e.activation(out=ltmp[:], in_=lam, func=AF.Exp, scale=-1.0), reads=PKEY, writes=["ltmp"])
        P.op("act", lambda e: e.activation(out=ltmp[:], in_=ltmp[:], func=AF.Ln, bias=oneb[:, 0:1]),
             reads=["ltmp", "oneb"], writes=["ltmp"])
        P.op("dve", lambda e: e.tensor_scalar(out=lruc[:, 0, :], in0=ltmp[:], scalar1=-4.0, scalar2=None, op0=ALU.mult),
             reads=["ltmp"], writes=["lruc"])
        P.op("dve", lambda e: e.tensor_scalar(out=lruc[:, 1, :], in0=ltmp[:], scalar1=-8.0, scalar2=None, op0=ALU.mult),
             reads=["ltmp"], writes=["lruc"])
        P.op("dve", lambda e: e.tensor_scalar(out=lruc[:, 2, :], in0=ltmp[:], scalar1=-8.0, scalar2=float(np.log(0.25)),
                                              op0=ALU.mult, op1=ALU.add), reads=["ltmp"], writes=["lruc"])
        P.op("dve", lambda e: e.tensor_scalar(out=lruc[:, 3, :], in0=ba_, scalar1=0.5, scalar2=None, op0=ALU.mult),
             reads=PKEY, writes=["lruc"])
        P.op("dve", lambda e: e.tensor_scalar(out=lruc[:, 4, :], in0=bi_, scalar1=0.5, scalar2=None, op0=ALU.mult),
             reads=PKEY, writes=["lruc"])

        def norm_sq(sqt, tt):
            cs = slice(tt * 512, (tt + 1) * 512)
            b = 4 + tt
            for h in range(2):
                P.op("act", lambda e, h=h: e.activation(out=sqt[:, 4 * h:4 * h + 4, :], in_=xT[:, 4 * h:4 * h + 4, cs],
                                                        func=AF.Square), reads=[("xT", tt)], writes=[("sqt", h)])
                for c in range(4 * h, 4 * h + 4):
                    mm(ps[b][:], onesb[:], sqt[:, c, :], c == 0, c == 7, [("sqt", h), "onesb"], [("ps", b)])

        def norm_fin(tt):
            b = 4 + tt
            P.op("act", lambda e: e.activation(out=rstd3[:, tt, :], in_=ps[b][:], func=AF.Sqrt, scale=1.0 / D,
                                               bias=epsb[:, 0:1]), reads=[("ps", b), "epsb"], writes=[("rstd", tt)])
            P.op("dve", lambda e: e.reciprocal(out=rstd3[:, tt, :], in_=rstd3[:, tt, :]), reads=[("rstd", tt)],
                 writes=[("rstd", tt)])

        def norm_stats(sqt, tt):
            norm_sq(sqt, tt)
            norm_fin(tt)

        N_MODS_EARLY = 6
        N_MODS_FFN1 = 10
        wmv = w_mod_d.rearrange("(kc p) n -> p kc n", p=128)

        def mods_load(ws, i):
            return ws.load(lambda t: [(t[:], wmv[:, :, i * 512:(i + 1) * 512])])

        def mods_chunk(i, t, key, b):
            for q in range(4):
                for kc in range(8):
                    mm(ps[b][:, 2 * q:2 * q + 2], t[:, kc, q * 128:(q + 1) * 128], scb[:, kc, :],
                       kc == 0, kc == 7, [key, "scb"], [("ps", b)])
            for ci in range(2):
                P.op("dve", lambda e, ci=ci: e.tensor_tensor(
                    out=mods[:, 4 * i:4 * i + 4, ci], in0=ps[b][:, 0:8].rearrange("p (o c) -> p o c", c=2)[:, :, ci],
                    in1=bmod[:, 4 * i:4 * i + 4], op=ALU.add), reads=[("ps", b)] + PKEY, writes=[("mods", i // 2)])

        def mods_coefs(acoef_ks, gcoef_ks):
            for k in acoef_ks:
                for ci in range(2):
                    P.op("dve", lambda e, k=k, ci=ci: e.scalar_tensor_tensor(
                        out=Acoef[:, k, :, ci], in0=mods[:, (3 * k + 1) * 8:(3 * k + 2) * 8, ci], scalar=1.0,
                        in1=normg[:, k * 8:(k + 1) * 8], op0=ALU.add, op1=ALU.mult),
                        reads=[("mods", 3 * k + 1)] + PKEY, writes=[("Acoef", k)])
            for k in gcoef_ks:
                P.op("dve", lambda e, k=k: e.tensor_scalar(
                    out=Gcoef[:, k, :, :], in0=mods[:, (3 * k + 2) * 8:(3 * k + 3) * 8, :], scalar1=0.5, scalar2=None,
                    op0=ALU.mult), reads=[("mods", 3 * k + 2)], writes=[("Gcoef", k)])

        with Scope() as sc:
            xs = [sc.T("xs%d" % i, [128, D]) for i in range(4)]
            sqt_p = sc.T("sqt_p", [128, 8, 512], BF16)
            wsm = WS(sc, "wmod", 3, [128, 8, 512])
            mrot = Rot([0, 1])
            pipe_m = Pipe(N_MODS_EARLY, 2, lambda i: mods_load(wsm, i),
                          lambda i, t, key: mods_chunk(i, t, key, mrot.next()))
            pipe_m.prefetch()
            def x_dma(j):
                s = j % 4
                P.dma("sp", lambda e: e.dma_start(out=xs[s][:], in_=xin[j * 128:(j + 1) * 128, :]),
                      ("xs", s), writes=[("xs", s)])
            for j in range(4):
                x_dma(j)
            fin_at = {5: 0, 9: 1}
            for j in range(12):
                s = j % 4
                for half in range(2):
                    b = 2 + (2 * j + half) % 2
                    for q in range(4):
                        c = half * 4 + q
                        P.op("pe", lambda e, b=b, q=q, c=c, s=s: e.transpose(
                            ps[b][:, q * 128:(q + 1) * 128], xs[s][:, c * 128:(c + 1) * 128], identf[:]),
                            reads=[("xs", s), "identf"], writes=[("ps", b)])
                    copy_op(evac_eng(), xT[:, half * 4:half * 4 + 4, j * 128:(j + 1) * 128],
                            ps[b][:].rearrange("p (q t) -> p q t", q=4), [("ps", b)], [("xT", j // 4)])
                if j + 4 < 12:
                    x_dma(j + 4)
                if j % 2 == 1 and j // 2 < N_MODS_EARLY:
                    pipe_m.step(j // 2)
                if j % 4 == 3:
                    norm_sq(sqt_p, j // 4)
                if j in fin_at:
                    norm_fin(fin_at[j])
            norm_fin(2)
            mods_coefs(acoef_ks=(0,), gcoef_ks=(0,))
        dump("x0", xT[:], [128, 8, NTOK], reads=XKEYS)

        def rmsnorm(sqt, out_fn, stats_done=False):
            def apply(tt):
                ci = 0 if tt == 0 else 1
                cs = slice(tt * 512, (tt + 1) * 512)
                for c in range(8):
                    out_fn(tt, c, ci, cs)
            if stats_done:
                for tt in range(NTT):
                    apply(tt)
                return
            norm_sq(sqt, 0)
            norm_sq(sqt, 1)
            norm_fin(0)
            norm_sq(sqt, 2)
            norm_fin(1)
            apply(0)
            norm_fin(2)
            apply(1)
            apply(2)

        def norm_bufs(sc, k):
            sqt = sc.T("sqt%d" % k, [128, 8, 512], BF16)
            ntmp = [sc.T("ntmp%d_%d" % (k, i), [128, 512]) for i in range(2)]
            return sqt, ntmp

        def norm_to_hT(k, bufs, stats_done=False):
            sqt, ntmp = bufs

            def out_fn(tt, c, ci, cs):
                s = c % 2
                P.op("dve", lambda e: e.scalar_tensor_tensor(
                    out=ntmp[s][:], in0=xT[:, c, cs], scalar=Acoef[:, k, c, ci:ci + 1], in1=rstd3[:, tt, :],
                    op0=ALU.mult, op1=ALU.mult), reads=[("xT", tt), ("rstd", tt), ("Acoef", k)], writes=[("ntmp", s)])
                P.op("act", lambda e: e.activation(out=hT[:, c, cs], in_=ntmp[s][:], func=AF.Identity,
                                                   bias=mods[:, 3 * k * 8 + c, ci:ci + 1]),
                     reads=[("ntmp", s), ("mods", 3 * k)], writes=[("hT", tt)])
            rmsnorm(sqt, out_fn, stats_done)

        def run_ffn(k, wi_d, wo_d):
            with Scope() as sc:
                nb_ = norm_bufs(sc, k)
                hid = sc.T("hid%d" % k, [128, NFC, NTOK], BF16)
                gtmp = [sc.T("gtmp%d_%d" % (k, i), [128, 512]) for i in range(2)]
                wsi = WS(sc, "fwi%d" % k, 3, [128, 8, 512])
                wso = WS(sc, "fwo%d" % k, 2, [128, NFC, 128])
                wiv = wi_d.rearrange("(kc p) n -> p kc n", p=128)
                wov = wo_d.rearrange("(fc p) n -> p fc n", p=128)
                rot = Rot([0, 1, 2, 3])

                def load_i(i):
                    return wsi.load(lambda t: [(t[:, :, 0:256], wiv[:, :, i * 256:(i + 1) * 256]),
                                               (t[:, :, 256:512], wiv[:, :, DFF + i * 256:DFF + (i + 1) * 256])])

                mid = list(range(N_MODS_EARLY, N_MODS_FFN1)) if k == 0 else []
                mws = WS.__new__(WS)
                mws.name, mws.t, mws.n = "wmodf", [nb_[0]], 0
                mld = {}

                def comp_i(i, t, key):
                    if mid and i % 2 == 0 and i // 2 < len(mid):
                        mld[i // 2] = mods_load(mws, mid[i // 2])
                    if mid and i % 2 == 1 and i // 2 < len(mid):
                        mods_chunk(mid[i // 2], *mld[i // 2], 4 + (i // 2) % 2)
                    if i in (7, 9):
                        pre_o.append(load_o(len(pre_o)))
                    for q in range(2):
                        fc = 2 * i + q
                        for tt in range(NTT):
                            cs = slice(tt * 512, (tt + 1) * 512)
                            bg, bu = rot.next(), rot.next()
                            for kc in range(8):
                                mm(ps[bg][:], t[:, kc, q * 128:(q + 1) * 128], hT[:, kc, cs], kc == 0, kc == 7,
                                   [key, ("hT", tt)], [("ps", bg)])
                            for kc in range(8):
                                mm(ps[bu][:], t[:, kc, 256 + q * 128:256 + (q + 1) * 128], hT[:, kc, cs], kc == 0,
                                   kc == 7, [key, ("hT", tt)], [("ps", bu)])
                            s = (fc * NTT + tt) % 2
                            P.op("act", lambda e, bg=bg, s=s: e.activation(out=gtmp[s][:], in_=ps[bg][:], func=AF.Silu),
                                 reads=[("ps", bg)], writes=[("gtmp", s)])
                            P.op("dve", lambda e, bu=bu, s=s, fc=fc, cs=cs: e.tensor_tensor(
                                out=hid[:, fc, cs], in0=gtmp[s][:], in1=ps[bu][:], op=ALU.mult),
                                reads=[("gtmp", s), ("ps", bu)], writes=[("hid", fc, tt)])
                def load_o(oc):
                    return wso.load(lambda t: [(t[:], wov[:, :, oc * 128:(oc + 1) * 128])])

                pipe_i = Pipe(11, 2, load_i, comp_i)
                pipe_i.prefetch()
                pre_o = []
                norm_to_hT(k, nb_, stats_done=(k == 0))
                pipe_i.run()
                rot2 = Rot([4, 5, 6, 7])

                def comp_o(oc, t, key):
                    GK = k
                    for tt in range(NTT):
                        ci = 0 if tt == 0 else 1
                        cs = slice(tt * 512, (tt + 1) * 512)
                        b = rot2.next()
                        for fc in range(NFC):
                            mm(ps[b][:], t[:, fc, :], hid[:, fc, cs], fc == 0, fc == NFC - 1,
                               [key, ("hid", fc, tt)], [("ps", b)])
                        P.op("dve", lambda e, b=b, cs=cs, ci=ci: e.scalar_tensor_tensor(
                            out=xT[:, oc, cs], in0=ps[b][:], scalar=Gcoef[:, k, oc, ci:ci + 1], in1=xT[:, oc, cs],
                            op0=ALU.mult, op1=ALU.add), reads=[("ps", b), ("Gcoef", GK), ("xT", tt)], writes=[("xT", tt)])
                for oc in range(8):
                    comp_o(oc, *pre_o[oc])
                    if oc + 2 < 8:
                        pre_o.append(load_o(oc + 2))
                if k == 0:
                    mods_coefs(acoef_ks=(1,), gcoef_ks=())

        def mixer():
            with Scope() as sm:
                attnT = sm.T("attnT", [128, 4, NTOK], BF16)
                bdw = sm.T("bdw", [128, 2, 2, 8, 128], BF16)

                def load_bdw(d):
                    if d == 0:
                        P.op("pool", lambda e: e.memset(bdw[:], 0.0), writes=["bdw"])
                    for gi, wd in enumerate((wa_d, wi_d)):
                        for e_ in range(2):
                            src_ = wd[d].rearrange("(c e) i o -> e i c o", e=2)[e_]
                            P.dma("pool", lambda e, gi=gi, e_=e_, src_=src_: e.dma_start(
                                out=bdw[e_ * 64:(e_ + 1) * 64, d, gi, :, e_ * 64:(e_ + 1) * 64], in_=src_),
                                "bdw", writes=["bdw"])
                attention(attnT, load_bdw)
                dump("attnT", attnT[:], [128, 4, NTOK], dt=BF16, reads=[("attnT", 0), ("attnT", 1), ("attnT", 2)])
                if stop_after == "attn":
                    return
                lruT = sm.T("lruT", [128, 8, NTOK], BF16)
                lru(lruT, bdw)
                dump("lruT", lruT[:], [128, 8, NTOK], dt=BF16, reads=["lruT"])
                if stop_after == "lru":
                    return
                merge(attnT, lruT)
            dump("x2", xT[:], [128, 8, NTOK], reads=XKEYS)

        def attention(attnT, load_bdw):
            plan = attn_plan()
            with Scope() as sa:
                nb_ = norm_bufs(sa, 1)
                qT = sa.T("qT", [128, NTOK], BF16)
                kTz = sa.T("kTz", [128, 2, NTOK], BF16)
                ckTz = sa.T("ckTz", [128, 2, 256], BF16)
                Vp = sa.T("Vp", [128, 14, 2, 128], BF16)
                btab = sa.T("btab_sb", [128, 2, NSLOT * 64], BF16)
                Eb = [sa.T("Eb%d" % i, [128, 512], BF16) for i in range(3)]
                ktok = [sa.T("ktok%d" % i, [128, 128]) for i in range(8)]
                kcount = [0]
                cst = [sa.T("cst%d" % i, [128, 2, 8, 64]) for i in range(2)]
                rD = sa.T("rD", [128, 512])
                wqkv = WS(sa, "wqkv", 2, [128, 8, 384])
                VK = [("Vp", j) for j in range(14)]
                def setup_memsets():
                    P.op("pool", lambda e: e.memset(kTz[:], 0.0), writes=["kTz"])
                    P.op("pool", lambda e: e.memset(ckTz[:], 0.0), writes=["ckTz"])
                    P.op("pool", lambda e: e.memset(Vp[:], 1.0), writes=VK)
                for w_, src in enumerate((ck_d, cv_d)):
                    for kb in range(2):
                      if "cst" not in SKIP:
                        P.dma("sp", lambda e, w_=w_, src=src, kb=kb: e.dma_start(
                            out=cst[w_][:, kb, :, :], in_=src[:, kb * 128:(kb + 1) * 128, :].rearrange("h p d -> p h d")),
                            ("cst", w_), writes=[("cst", w_)])

                def vp_diag(j):
                    base = Vp[:, j, 0, 0:64]
                    return bass.AP(base.tensor, base.offset, [list(base.ap[0]), [192, 2], [1, 64]])

                rot = Rot([0, 1, 2, 3])
                srot = Rot([0, 1, 2, 3])
                ecount = [0]
                gcount = [0]

                def load(hp):
                    return wqkv.load(lambda t: [(t[:, :, 0:128], winv[:, :, hp * 128:(hp + 1) * 128]),
                                                (t[:, :, 128:256], winv[:, :, 512 + hp * 128:512 + (hp + 1) * 128]),
                                                (t[:, :, 256:384], winv[:, :, 1024 + hp * 128:1024 + (hp + 1) * 128])])

                def s_part(blk):
                    (h, e_, kT_ap, q_ap, ncols, c0, bias_segs, v_ap, rkeys, vkey) = blk["args"]
                    sb = srot.next()
                    nb = len(bias_segs)
                    mm(ps[sb][:, c0:c0 + ncols], kT_ap, q_ap, True, nb == 0, rkeys, [("ps", sb)])
                    off = c0
                    for bi, (tab, jj0, nr) in enumerate(bias_segs):
                        sl = slot_of(tab, jj0)
                        mm(ps[sb][:, off:off + nr * 64], identb[:], btab[:, e_, sl * 64:(sl + nr) * 64], False,
                           bi == nb - 1, ["identb", "btab"], [("ps", sb)])
                        off += nr * 64
                    ei = ecount[0] % len(Eb)
                    ecount[0] += 1
                    blk["ei"] = ei
                    P.op("act", lambda e: e.activation(out=Eb[ei][:, 0:ncols], in_=ps[sb][:, c0:c0 + ncols], func=AF.Exp),
                         reads=[("ps", sb)], writes=[("Eb", ei)])

                def pv_part(blk):
                    (h, e_, kT_ap, q_ap, ncols, c0, bias_segs, v_ap, rkeys, vkey) = blk["args"]
                    ei = blk["ei"]
                    ob = blk["banks"]
                    mm(ps[ob][:, c0:c0 + ncols], v_ap, Eb[ei][:, 0:ncols], blk["first"], blk["last"],
                       [("Eb", ei), vkey], [("ps", ob)])

                def finish(hp, e_, tok0, ncols, c0, tt, ob):
                    pr = slice(e_ * 64, (e_ + 1) * 64)
                    dr = slice((1 - e_) * 64, (2 - e_) * 64)
                    P.op("dve", lambda e: e.reciprocal(out=rD[pr, c0:c0 + ncols], in_=ps[ob][dr, c0:c0 + ncols]),
                         reads=[("ps", ob)], writes=[("rD", e_)])
                    P.op("dve", lambda e: e.tensor_tensor(out=attnT[pr, hp, tok0:tok0 + ncols],
                                                          in0=ps[ob][pr, c0:c0 + ncols], in1=rD[pr, c0:c0 + ncols],
                                                          op=ALU.mult),
                         reads=[("ps", ob), ("rD", e_)], writes=[("attnT", tt)])

                def comp(hp, t, key):
                    for h2 in range(2):
                      if "btab" not in SKIP:
                        P.dma("pool", lambda e, h2=h2: e.dma_start(out=btab[:, h2, :], in_=btab_d[2 * hp + h2]),
                              "btab", writes=["btab"])
                    if "proj" in SKIP:
                        return
                    for tt in range(NTT if "qk" not in SKIP else 0):
                        cs = slice(tt * 512, (tt + 1) * 512)
                        b = rot.next()
                        for kc in range(8):
                            mm(ps[b][:], t[:, kc, 0:128], hT[:, kc, cs], kc == 0, kc == 7, [key, ("hT", tt)], [("ps", b)])
                        copy_op(evac_eng(), qT[:, cs], ps[b][:], [("ps", b)], ["qT"], scale=0.125)
                        b = rot.next()
                        for kc in range(8):
                            mm(ps[b][:], t[:, kc, 128:256], hT[:, kc, cs], kc == 0, kc == 7, [key, ("hT", tt)], [("ps", b)])
                        eng_ = evac_eng()
                        for e_ in range(2):
                            pr = slice(e_ * 64, (e_ + 1) * 64)
                            copy_op(eng_, kTz[pr, e_, cs], ps[b][pr, :], [("ps", b)], ["kTz"])
                    for j in range(4 if "ktok" not in SKIP else 0):
                        b = rot.next()
                        for kc in range(8):
                            mm(ps[b][:, 0:128], hT[:, kc, j * 128:(j + 1) * 128], t[:, kc, 128:256], kc == 0, kc == 7,
                               [key, ("hT", 0)], [("ps", b)])
                        s = kcount[0] % 8
                        kcount[0] += 1
                        copy_op(evac_eng(), ktok[s][:], ps[b][:, 0:128], [("ps", b)], [("ktok", s)])
                        sq, t0 = j // 2, (j % 2) * 128
                        if "kvout" not in SKIP:
                          P.dma("sp", lambda e, s=s, sq=sq, t0=t0: e.dma_start(
                            out=nk_d[sq][2 * hp:2 * hp + 2, t0:t0 + 128, :].rearrange("h t d -> t h d"),
                            in_=ktok[s][:].rearrange("p (h d) -> p h d", h=2)), ("kst", s), reads=[("ktok", s)])
                    for j in range(12 if "vtok" not in SKIP else 0):
                        b = rot.next()
                        for kc in range(8):
                            mm(ps[b][:, 0:128], hT[:, kc, j * 128:(j + 1) * 128], t[:, kc, 256:384], kc == 0, kc == 7,
                               [key, ("hT", j // 4)], [("ps", b)])
                        if j >= 4:
                            copy_op(evac_eng(), vp_diag(j), ps[b][:, 0:128].rearrange("p (e d) -> p e d", e=2),
                                    [("ps", b)], [("Vp", j)])
                        else:
                            s = kcount[0] % 8
                            kcount[0] += 1
                            copy_op("act", ktok[s][:], ps[b][:, 0:128], [("ps", b)], [("ktok", s)])
                            copy_op("dve", vp_diag(j), ktok[s][:].rearrange("p (e d) -> p e d", e=2),
                                    [("ktok", s)], [("Vp", j)])
                            sq, t0 = j // 2, (j % 2) * 128
                            if "kvout" not in SKIP:
                              P.dma("sp", lambda e, s=s, sq=sq, t0=t0: e.dma_start(
                                out=nv_d[sq][2 * hp:2 * hp + 2, t0:t0 + 128, :].rearrange("h t d -> t h d"),
                                in_=ktok[s][:].rearrange("p (h d) -> p h d", h=2)), ("kst", s), reads=[("ktok", s)])
                    for kb in range(2 if "ctx" not in SKIP else 0):
                        b = rot.next()
                        P.op("pe", lambda e, kb=kb, b=b: e.transpose(
                            ps[b][:, 0:128], cst[0][:, kb, 2 * hp:2 * hp + 2, :].rearrange("p h d -> p (h d)"), identf[:]),
                            reads=[("cst", 0), "identf"], writes=[("ps", b)])
                        eng_ = evac_eng()
                        for e_ in range(2):
                            pr = slice(e_ * 64, (e_ + 1) * 64)
                            copy_op(eng_, ckTz[pr, e_, kb * 128:(kb + 1) * 128], ps[b][pr, 0:128],
                                    [("ps", b)], ["ckTz"])
                        copy_op("dve", vp_diag(12 + kb), cst[1][:, kb, 2 * hp:2 * hp + 2, :], [("cst", 1)],
                                [("Vp", 12 + kb)])
                    if hp == 0:
                        dump("qT", qT[:], [128, NTOK], dt=BF16, reads=["qT"])
                        dump("kTz", kTz[:], [128, 2, NTOK], dt=BF16, reads=["kTz"])
                        dump("Vp", Vp[:], [128, 14, 2, 128], dt=BF16, reads=VK)
                        dump("ckTz", ckTz[:], [128, 2, 256], dt=BF16, reads=["ckTz"])
                    groups = []
                    for sq in range(2):
                        for e_ in range(2):
                            blks = []
                            for kb in range(2):
                                k0 = sq * 256 + kb * 128
                                blks.append((2 * hp + e_, e_, kTz[:, e_, k0:k0 + 128], qT[:, sq * 256:(sq + 1) * 256], 256,
                                             sq * 256, [], Vp[:, 2 * sq + kb, e_, :], ["kTz", "qT"], ("Vp", 2 * sq + kb)))
                            groups.append((blks, (hp, e_, sq * 256, 256, sq * 256, 0)))
                    for qt in range(2):
                        q0 = 512 + qt * 512
                        for e_ in range(2):
                            blks = []
                            for kb in range(2):
                                blks.append((2 * hp + e_, e_, ckTz[:, e_, kb * 128:(kb + 1) * 128], qT[:, q0:q0 + 512], 512,
                                             0, [], Vp[:, 12 + kb, e_, :], ["ckTz", "qT"], ("Vp", 12 + kb)))
                            for (kb, c0, c1, segs) in plan[qt]:
                                blks.append((2 * hp + e_, e_, kTz[:, e_, 512 + kb * 128:512 + (kb + 1) * 128],
                                             qT[:, q0 + c0:q0 + c1], c1 - c0, c0, segs, Vp[:, 4 + kb, e_, :],
                                             ["kTz", "qT"], ("Vp", 4 + kb)))
                            groups.append((blks, (hp, e_, q0, 512, 0, 1 + qt)))
                    flat = []
                    for gi, (blks, fin) in enumerate(groups):
                        banks = 4 + (gcount[0] % 4)
                        gcount[0] += 1
                        for bi, args in enumerate(blks):
                            flat.append({"args": args, "first": bi == 0, "last": bi == len(blks) - 1,
                                         "banks": banks, "fin": fin})
                    LA = 2
                    for n in range(min(LA, len(flat))):
                        s_part(flat[n])
                    for n, blk in enumerate(flat):
                        if n + LA < len(flat):
                            s_part(flat[n + LA])
                        pv_part(blk)
                        if blk["last"]:
                            finish(*blk["fin"], blk["banks"])
                wsm2 = WS(sa, "wmod2", 2, [128, 8, 512])
                late = list(range(N_MODS_FFN1, 18))
                mloaded = {}

                def comp2(hp, t, key):
                    for i in late[2 * hp:2 * hp + 2]:
                        mloaded[i] = mods_load(wsm2, i)
                    if hp in (1, 2):
                        load_bdw(hp - 1)
                    comp(hp, t, key)
                    for i in late[2 * hp:2 * hp + 2]:
                        mods_chunk(i, *mloaded[i], rot.next())

                pipe_a = Pipe(4, 1, load, comp2)
                pipe_a.prefetch()
                setup_memsets()
                norm_to_hT(1, nb_)
                dump("h2", hT[:], [128, 8, NTOK], dt=BF16, reads=HKEYS)
                pipe_a.run()
                mods_coefs(acoef_ks=(2,), gcoef_ks=(1, 2))

        def lru(lruT, bdw):
            with Scope() as sl:
                W = 1024
                xlp = sl.T("xlp", [128, W + 6])
                xc = [sl.T("xc%d" % s, [128, W]) for s in range(2)]
                xcb = [sl.T("xcb%d" % s, [128, W], BF16) for s in range(2)]
                tr = [[sl.T("tr%d_%d" % (s, d), [128, W]) for d in range(2)] for s in range(2)]
                ti = [[sl.T("ti%d_%d" % (s, d), [128, W]) for d in range(2)] for s in range(2)]
                a2 = [sl.T("a2%d" % d, [128, W]) for d in range(2)]
                xg = [sl.T("xg%d" % s, [128, W]) for s in range(2)]
                x2 = [sl.T("x2%d" % s, [128, W]) for s in range(2)]
                wl = WS(sl, "wl", 2, [128, 8, 256])
                P.op("pool", lambda e: e.memset(xlp[:], 0.0), writes=["xlp"])
                rot = Rot([0, 1, 2, 3, 4, 5, 6, 7])
                units = [(0, c) for c in range(4)] + [(1, c) for c in range(8)] + [(0, c) for c in range(4, 8)]
                loaded = {}

                def load(i):
                    pss, c = units[i]
                    loaded[i] = wl.load(lambda t: [(t[:, :, 0:128], winv[:, :, 1536 + c * 128:1536 + (c + 1) * 128]),
                                                   (t[:, :, 128:256], winv[:, :, 2560 + c * 128:2560 + (c + 1) * 128])])

                def geom(pss):
                    if pss == 0:
                        return 0, 512, [0], [(0, 256, 0), (256, 256, 259)]
                    return 512, 1024, [1, 2], [(0, 1024, 0)]

                def stage_f1(i):
                    pss, c = units[i]
                    s = i % 2
                    t, key = loaded[i]
                    tok0, ntok, tiles, segs = geom(pss)
                    xl_b, gl_b = [], []
                    for tt in tiles:
                        cs = slice(tt * 512, (tt + 1) * 512)
                        b = rot.next()
                        xl_b.append(b)
                        for kc in range(8):
                            mm(ps[b][:], t[:, kc, 0:128], hT[:, kc, cs], kc == 0, kc == 7, [key, ("hT", tt)], [("ps", b)])
                    for tt in tiles:
                        cs = slice(tt * 512, (tt + 1) * 512)
                        b = rot.next()
                        gl_b.append(b)
                        for kc in range(8):
                            mm(ps[b][:], t[:, kc, 128:256], hT[:, kc, cs], kc == 0, kc == 7, [key, ("hT", tt)], [("ps", b)])
                    if pss == 0 and i > 0 and units[i - 1][0] == 1:
                        P.op("dve", lambda e: e.memset(xlp[:, 258:261], 0.0), reads=["xlp"], writes=["xlp"])
                        P.op("dve", lambda e: e.memset(xlp[:, 517:518], 0.0), reads=["xlp"], writes=["xlp"])
                    if pss == 0:
                        for sq in range(2):
                            copy_op("act", xlp[:, 259 * sq + 2:259 * sq + 258], ps[xl_b[0]][:, sq * 256:(sq + 1) * 256],
                                    [("ps", xl_b[0])], ["xlp"])
                    else:
                        for k_, b in enumerate(xl_b):
                            copy_op("act", xlp[:, 2 + k_ * 512:2 + (k_ + 1) * 512], ps[b][:], [("ps", b)], ["xlp"])
                    for k_, b in enumerate(gl_b):
                        ls = slice(k_ * 512, (k_ + 1) * 512)
                        P.op("act", lambda e, b=b, ls=ls: e.activation(out=xg[s][:, ls], in_=ps[b][:], func=AF.Copy),
                             reads=[("ps", b)], writes=[("xg", s)])
                    for (t0, ln, pb) in segs:
                        P.op("dve", lambda e, t0=t0, ln=ln, pb=pb: e.tensor_scalar(
                            out=xc[s][:, t0:t0 + ln], in0=xlp[:, pb:pb + ln], scalar1=convw[:, c:c + 1],
                            scalar2=convb[:, c:c + 1], op0=ALU.mult, op1=ALU.add), reads=["xlp"] + PKEY, writes=[("xc", s)])
                        for j in range(1, 4):
                            P.op("dve", lambda e, t0=t0, ln=ln, pb=pb, j=j: e.scalar_tensor_tensor(
                                out=xc[s][:, t0:t0 + ln], in0=xlp[:, pb + j:pb + j + ln],
                                scalar=convw[:, j * 8 + c:j * 8 + c + 1], in1=xc[s][:, t0:t0 + ln],
                                op0=ALU.mult, op1=ALU.add), reads=["xlp", ("xc", s)] + PKEY, writes=[("xc", s)])

                def stage_f2(i):
                    pss, c = units[i]
                    s = i % 2
                    tok0, ntok, tiles, segs = geom(pss)
                    P.op("act", lambda e: e.activation(out=xcb[s][:, 0:ntok], in_=xc[s][:, 0:ntok], func=AF.Copy),
                         reads=[("xc", s)], writes=[("xcb", s)])

                def stage_a1(i):
                    pss, c = units[i]
                    s = i % 2
                    tok0, ntok, tiles, segs = geom(pss)
                    P.op("dve", lambda e: e.scalar_tensor_tensor(
                        out=x2[s][:, 0:ntok], in0=xg[s][:, 0:ntok], scalar=0.044715 * GELU_K, in1=xg[s][:, 0:ntok],
                        op0=ALU.mult, op1=ALU.mult), reads=[("xg", s)], writes=[("x2", s)])
                    P.op("dve", lambda e: e.scalar_tensor_tensor(
                        out=x2[s][:, 0:ntok], in0=x2[s][:, 0:ntok], scalar=GELU_K, in1=xg[s][:, 0:ntok],
                        op0=ALU.add, op1=ALU.mult), reads=[("x2", s), ("xg", s)], writes=[("x2", s)])
                    for d in range(2):
                        col = d * 8 + c
                        for gi, dst, dkey, brow in ((0, tr[s][d], ("tr", s, d), 3), (1, ti[s][d], ("ti", s, d), 4)):
                            for k_ in range(len(tiles)):
                                ls = slice(k_ * 512, (k_ + 1) * 512)
                                b = rot.next()
                                mm(ps[b][:], bdw[:, d, gi, c, :], xcb[s][:, ls], True, True, ["bdw", ("xcb", s)], [("ps", b)])
                                P.op("act", lambda e, b=b, ls=ls, dst=dst, brow=brow, col=col: e.activation(
                                    out=dst[:, ls], in_=ps[b][:], func=AF.Tanh, scale=0.5,
                                    bias=lruc[:, brow, col:col + 1]), reads=[("ps", b), "lruc"], writes=[dkey])
                    P.op("act", lambda e: e.activation(out=x2[s][:, 0:ntok], in_=x2[s][:, 0:ntok], func=AF.Tanh),
                         reads=[("x2", s)], writes=[("x2", s)])
                    for d in range(2):
                        col = d * 8 + c
                        P.op("act", lambda e, d=d, col=col: e.activation(
                            out=a2[d][:, 0:ntok], in_=tr[s][d][:, 0:ntok], func=AF.Exp, scale=lruc[:, 1, col:col + 1],
                            bias=lruc[:, 2, col:col + 1]), reads=[("tr", s, d), "lruc"], writes=[("a2", d)])
                        P.op("act", lambda e, d=d, col=col: e.activation(
                            out=tr[s][d][:, 0:ntok], in_=tr[s][d][:, 0:ntok], func=AF.Exp, scale=lruc[:, 0, col:col + 1],
                            bias=lruc[:, 0, col:col + 1]), reads=[("tr", s, d), "lruc"], writes=[("tr", s, d)])
                    for d in range(2):
                        P.op("act", lambda e, d=d: e.activation(out=a2[d][:, 0:ntok], in_=a2[d][:, 0:ntok], func=AF.Sqrt,
                                                                scale=-1.0, bias=qb25[:, 0:1]),
                             reads=[("a2", d), "qb25"], writes=[("a2", d)])

                def stage_a2(i):
                    pss, c = units[i]
                    s = i % 2
                    tok0, ntok, tiles, segs = geom(pss)
                    P.op("dve", lambda e: e.scalar_tensor_tensor(out=x2[s][:, 0:ntok], in0=x2[s][:, 0:ntok], scalar=1.0,
                                                                 in1=xg[s][:, 0:ntok], op0=ALU.add, op1=ALU.mult),
                         reads=[("x2", s), ("xg", s)], writes=[("x2", s)])
                    for d in range(2):
                        P.op("dve", lambda e, d=d: e.scalar_tensor_tensor(
                            out=ti[s][d][:, 0:ntok], in0=ti[s][d][:, 0:ntok], scalar=1.0, in1=xc[s][:, 0:ntok],
                            op0=ALU.add, op1=ALU.mult), reads=[("ti", s, d), ("xc", s)], writes=[("ti", s, d)])
                        P.op("dve", lambda e, d=d: e.tensor_tensor(out=ti[s][d][:, 0:ntok], in0=ti[s][d][:, 0:ntok],
                                                                   in1=a2[d][:, 0:ntok], op=ALU.mult),
                             reads=[("ti", s, d), ("a2", d)], writes=[("ti", s, d)])

                def stage_b(i):
                    pss, c = units[i]
                    s = i % 2
                    tok0, ntok, tiles, segs = geom(pss)
                    for d in range(2):
                        col = d * 8 + c
                        hdst, hkey = ti[s][d], ("ti", s, d)
                        for si, (t0, ln, pb) in enumerate(segs):
                            init = 0.0 if pss == 0 else st0[:, col:col + 1]
                            if d == 0:
                                P.op("dve", lambda e, t0=t0, ln=ln, init=init, hdst=hdst, d=d: e.tensor_tensor_scan(
                                    out=hdst[:, t0:t0 + ln], data0=tr[s][d][:, t0:t0 + ln], data1=ti[s][d][:, t0:t0 + ln],
                                    initial=init, op0=ALU.mult, op1=ALU.add),
                                    reads=[("tr", s, d), ("ti", s, d)] + PKEY, writes=[hkey])
                            else:
                                P.op("dve", lambda e, t0=t0, ln=ln, init=init, hdst=hdst, d=d: e.tensor_tensor_scan(
                                    out=hdst[:, t0:t0 + ln][:, ::-1], data0=tr[s][d][:, t0:t0 + ln][:, ::-1],
                                    data1=ti[s][d][:, t0:t0 + ln][:, ::-1], initial=init,
                                    op0=ALU.mult, op1=ALU.add), reads=[("tr", s, d), ("ti", s, d)] + PKEY, writes=[hkey])
                            if pss == 0:
                                srccol = (t0 + ln - 1) if d == 0 else t0
                                dcol = si * 16 + d * 8 + c
                                P.op("dve", lambda e, srccol=srccol, dcol=dcol, hdst=hdst: e.tensor_copy(
                                    out=nst[:, dcol:dcol + 1], in_=hdst[:, srccol:srccol + 1]),
                                    reads=[hkey], writes=["nst"])
                    P.op("pool", lambda e: e.tensor_tensor(out=ti[s][0][:, 0:ntok], in0=ti[s][0][:, 0:ntok],
                                                           in1=ti[s][1][:, 0:ntok], op=ALU.add),
                         reads=[("ti", s, 0), ("ti", s, 1)], writes=[("ti", s, 0)])

                def stage_b2(i):
                    pss, c = units[i]
                    s = i % 2
                    tok0, ntok, tiles, segs = geom(pss)
                    P.op("dve", lambda e: e.scalar_tensor_tensor(out=lruT[:, c, tok0:tok0 + ntok], in0=x2[s][:, 0:ntok],
                                                                 scalar=0.5, in1=ti[s][0][:, 0:ntok], op0=ALU.mult,
                                                                 op1=ALU.mult),
                         reads=[("x2", s), ("ti", s, 0)], writes=["lruT"])

                n = len(units)
                load(0)
                load(1)
                stage_f1(0)
                load(2)
                stage_f2(0)
                stage_f1(1)
                load(3)
                stage_a1(0)
                stage_f2(1)
                stage_a2(0)
                for i in range(n):
                    if i + 2 < n:
                        stage_f1(i + 2)
                        if i + 4 < n:
                            load(i + 4)
                    if i + 1 < n:
                        stage_a1(i + 1)
                    if i + 2 < n:
                        stage_f2(i + 2)
                    stage_b(i)
                    if i + 1 < n:
                        stage_a2(i + 1)
                    stage_b2(i)

        def merge(attnT, lruT):
            with Scope() as sg:
                mT = sg.T("mT", [128, 8, NTOK], BF16)
                wba = sg.T("wba", [128, 4, D], BF16)
                wbl = sg.T("wbl", [128, 8, D], BF16)
                wo = sg.T("wo", [128, 8, D], BF16)
                sga = [sg.T("sga%d" % i, [128, 512]) for i in range(2)]
                sgb = [sg.T("sgb%d" % i, [128, 512]) for i in range(2)]
                wg = WS(sg, "wg", 2, [128, 8, 256])
                rot = Rot([0, 1, 2, 3, 4, 5, 6, 7])

                def load(oc):
                    r = wg.load(lambda t: [(t[:, :, 0:128], winv[:, :, 3584 + oc * 128:3584 + (oc + 1) * 128]),
                                           (t[:, :, 128:256], winv[:, :, 4608 + oc * 128:4608 + (oc + 1) * 128])])
                    if oc == 0:
                        for part, (c0_, c1_) in enumerate(((0, 256), (256, D))):
                            P.dma("pool", lambda e, c0_=c0_, c1_=c1_: e.dma_start(
                                out=wba[:, :, c0_:c1_], in_=wbra_d.rearrange("(kc p) n -> p kc n", p=128)[:, :, c0_:c1_]),
                                ("wba", part), writes=[("wba", part)])
                            P.dma("pool", lambda e, c0_=c0_, c1_=c1_: e.dma_start(
                                out=wbl[:, :, c0_:c1_], in_=wbrl_d.rearrange("(kc p) n -> p kc n", p=128)[:, :, c0_:c1_]),
                                ("wbl", part), writes=[("wbl", part)])
                    return r

                def comp(oc, t, key):
                    if oc == 2:
                        P.dma("pool", lambda e: e.dma_start(out=wo[:], in_=wout_d.rearrange("(kc p) n -> p kc n", p=128)),
                              "wo", writes=["wo"])
                    for tt in range(NTT):
                        cs = slice(tt * 512, (tt + 1) * 512)
                        s = (oc * NTT + tt) % 2
                        bga, bgb, bpa, bpl = rot.next(), rot.next(), rot.next(), rot.next()
                        for kc in range(8):
                            mm(ps[bga][:], t[:, kc, 0:128], hT[:, kc, cs], kc == 0, kc == 7, [key, ("hT", tt)], [("ps", bga)])
                        for kc in range(8):
                            mm(ps[bgb][:], t[:, kc, 128:256], hT[:, kc, cs], kc == 0, kc == 7, [key, ("hT", tt)], [("ps", bgb)])
                        wpart = 0 if oc < 2 else 1
                        for kc in range(4):
                            mm(ps[bpa][:], wba[:, kc, oc * 128:(oc + 1) * 128], attnT[:, kc, cs], kc == 0, kc == 3,
                               [("wba", wpart), ("attnT", tt)], [("ps", bpa)])
                        for kc in range(8):
                            mm(ps[bpl][:], wbl[:, kc, oc * 128:(oc + 1) * 128], lruT[:, kc, cs], kc == 0, kc == 7,
                               [("wbl", wpart), "lruT"], [("ps", bpl)])
                        P.op("act", lambda e, bga=bga, s=s: e.activation(out=sga[s][:], in_=ps[bga][:], func=AF.Tanh, scale=0.5),
                             reads=[("ps", bga)], writes=[("sga", s)])
                        P.op("act", lambda e, bgb=bgb, s=s: e.activation(out=sgb[s][:], in_=ps[bgb][:], func=AF.Tanh, scale=0.5),
                             reads=[("ps", bgb)], writes=[("sgb", s)])
                        P.op("dve", lambda e, bpa=bpa, s=s: e.scalar_tensor_tensor(
                            out=sga[s][:], in0=sga[s][:], scalar=1.0, in1=ps[bpa][:], op0=ALU.add, op1=ALU.mult),
                            reads=[("sga", s), ("ps", bpa)], writes=[("sga", s)])
                        P.op("dve", lambda e, bpl=bpl, s=s: e.scalar_tensor_tensor(
                            out=sgb[s][:], in0=sgb[s][:], scalar=1.0, in1=ps[bpl][:], op0=ALU.add, op1=ALU.mult),
                            reads=[("sgb", s), ("ps", bpl)], writes=[("sgb", s)])
                        P.op("dve", lambda e, s=s, cs=cs: e.tensor_tensor(
                            out=mT[:, oc, cs], in0=sga[s][:], in1=sgb[s][:], op=ALU.add),
                            reads=[("sga", s), ("sgb", s)], writes=[("mT", tt)])
                pipeline(8, 1, load, comp)
                GK = 1
                for oc in range(8):
                    for tt in range(NTT):
                        ci = 0 if tt == 0 else 1
                        cs = slice(tt * 512, (tt + 1) * 512)
                        b = rot.next()
                        for kc in range(8):
                            mm(ps[b][:], wo[:, kc, oc * 128:(oc + 1) * 128], mT[:, kc, cs], kc == 0, kc == 7,
                               ["wo", ("mT", tt)], [("ps", b)])
                        P.op("dve", lambda e, b=b, oc=oc, cs=cs, ci=ci: e.scalar_tensor_tensor(
                            out=xT[:, oc, cs], in0=ps[b][:], scalar=Gcoef[:, 1, oc, ci:ci + 1], in1=xT[:, oc, cs],
                            op0=ALU.mult, op1=ALU.add), reads=[("ps", b), ("Gcoef", GK), ("xT", tt)], writes=[("xT", tt)])

        if stop_after != "mods":
            run_ffn(0, f1wi_d, f1wo_d)
            dump("x1", xT[:], [128, 8, NTOK], reads=XKEYS)
        if stop_after not in ("mods", "ffn1"):
            mixer()
        if stop_after is None:
            run_ffn(2, f2wi_d, f2wo_d)
            dump("x3", xT[:], [128, 8, NTOK], reads=XKEYS)

        with Scope() as sc:
            ytok = [sc.T("ytok%d" % i, [128, D]) for i in range(4)]
            gbc = sc.T("gbc", [128, D])
            grow = sc.T("grow", [1, D])
            onesr = sc.T("onesr", [1, 128])
            junk = [sc.T("junk%d" % i, [128, 512], BF16) for i in range(2)]
            ssq = sc.T("ssq", [128, 24])
            rs = sc.T("rs", [128, 12])
            nsT = sc.T("nsT", [32, 128])
            P.dma("sp", lambda e: e.dma_start(out=grow[:], in_=pv_d[184:192, :].rearrange("(o r) n -> o (r n)", o=1)),
                  "grow", writes=["grow"])
            P.op("pool", lambda e: e.memset(onesr[:], 1.0), writes=["onesr"])
            for half in range(2):
                P.op("pe", lambda e, half=half: e.matmul(ps[half][:], lhsT=onesr[:], rhs=grow[:, half * 512:(half + 1) * 512],
                                                         start=True, stop=True),
                     reads=["onesr", "grow"], writes=[("ps", half)])
                copy_op("dve", gbc[:, half * 512:(half + 1) * 512], ps[half][:], [("ps", half)], [("gbc", half)])
            frot = Rot([2, 3, 4, 5, 6, 7, 0, 1])
            for j in range(12):
                tt = j // 4
                s = j % 4
                bb = [frot.next(), frot.next()]
                for half in range(2):
                    b = bb[half]
                    for q in range(4):
                        c2 = half * 4 + q
                        P.op("pe", lambda e, b=b, q=q, c2=c2, j=j: e.transpose(
                            ps[b][:, q * 128:(q + 1) * 128], xT[:, c2, j * 128:(j + 1) * 128], identf[:]),
                            reads=[("xT", tt), "identf"], writes=[("ps", b)])
                    P.op("act", lambda e, b=b, half=half, j=j: e.activation(
                        out=junk[half][:], in_=ps[b][:], func=AF.Square,
                        accum_out=ssq[:, 2 * j + half:2 * j + half + 1]),
                        reads=[("ps", b)], writes=[("junk", half), ("ssq", j, half)])
                P.op("dve", lambda e, j=j: e.tensor_tensor(out=rs[:, j:j + 1], in0=ssq[:, 2 * j:2 * j + 1],
                                                           in1=ssq[:, 2 * j + 1:2 * j + 2], op=ALU.add),
                     reads=[("ssq", j, 0), ("ssq", j, 1)], writes=[("rs", j)])
                P.op("act", lambda e, j=j: e.activation(out=rs[:, j:j + 1], in_=rs[:, j:j + 1], func=AF.Sqrt,
                                                        scale=1.0 / D, bias=epsb[:, 0:1]),
                     reads=[("rs", j), "epsb"], writes=[("rs", j)])
                P.op("dve", lambda e, j=j: e.reciprocal(out=rs[:, j:j + 1], in_=rs[:, j:j + 1]),
                     reads=[("rs", j)], writes=[("rs", j)])
                for half in range(2):
                    b = bb[half]
                    P.op("dve", lambda e, b=b, half=half, j=j, s=s: e.scalar_tensor_tensor(
                        out=ytok[s][:, half * 512:(half + 1) * 512], in0=ps[b][:], scalar=rs[:, j:j + 1],
                        in1=gbc[:, half * 512:(half + 1) * 512], op0=ALU.mult, op1=ALU.mult),
                        reads=[("ps", b), ("rs", j), ("gbc", half)], writes=[("ytok", s, half)])
                P.dma("sp", lambda e, j=j, s=s: e.dma_start(out=y_d[j * 128:(j + 1) * 128, :], in_=ytok[s][:]),
                      ("yst", s), reads=[("ytok", s, 0), ("ytok", s, 1)])

            P.op("pe", lambda e: e.transpose(ps[7][0:32, 0:128], nst[:], identf[:]), reads=["nst", "identf"],
                 writes=[("ps", 7)])
            copy_op("dve", nsT[:], ps[7][0:32, 0:128], [("ps", 7)], ["nsT"])
            P.dma("sp", lambda e: e.dma_start(out=ns_d, in_=nsT[:]), "nsst", reads=["nsT"])

        P.emit(nc)
    return nc, dump_outs


_NC_CACHE = {}


def make_in_maps(inp):
    f = lambda a: np.ascontiguousarray(np.asarray(a, dtype=np.float32))
    x_prompt = f(inp["x_prompt"]); x_sample = f(inp["x_sample"])
    cache_k = f(inp["cache_k"]); cache_v = f(inp["cache_v"]); state = f(inp["state_lru"])
    c = f(inp["c"]); c_ctx = f(inp["c_ctx"])
    shared = {
        "ident": np.eye(128, dtype=np.float32),
        "btab": host_bias_table(f(inp["rpb"])[0]),
        "w_mod": f(inp["w_mod"])[0], "ffn1_w_in": f(inp["ffn1_w_in"])[0], "ffn1_w_out": f(inp["ffn1_w_out"])[0],
        "w_in": f(inp["w_in"])[0], "lru_wa": f(inp["lru_wa"])[0], "lru_wi": f(inp["lru_wi"])[0],
        "w_br_attn": f(inp["w_br_attn"])[0], "w_br_lru": f(inp["w_br_lru"])[0], "w_out": f(inp["w_out"])[0],
        "ffn2_w_in": f(inp["ffn2_w_in"])[0], "ffn2_w_out": f(inp["ffn2_w_out"])[0],
    }
    common_rows = [f(inp["b_mod"])[0].reshape(72, 128), f(inp["norm_g"])[0].reshape(24, 128),
                   f(inp["conv_w"])[0].reshape(32, 128), f(inp["conv_b"])[0].reshape(8, 128),
                   f(inp["lru_ba"])[0].reshape(16, 128), f(inp["lru_bi"])[0].reshape(16, 128),
                   f(inp["lru_lambda"])[0].reshape(16, 128), f(inp["final_g"]).reshape(8, 128)]
    maps = []
    for i in range(8):
        pv = np.zeros((256, 128), np.float32)
        rows = common_rows + [state[i, 0].reshape(16, 128), c_ctx.reshape(8, 128), c[i].reshape(8, 128)]
        cat = np.concatenate(rows, axis=0)
        pv[:cat.shape[0]] = cat
        m = dict(shared)
        m["xin"] = np.concatenate([x_prompt[2 * i:2 * i + 2].reshape(512, D), x_sample[i]], axis=0)
        m["ck"] = cache_k[i, 0]
        m["cv"] = cache_v[i, 0]
        m["pv"] = pv
        maps.append(m)
    return maps


def kernel(**inputs):
    if "nc" not in _NC_CACHE:
        _NC_CACHE["nc"] = build_nc()[0]
    nc = _NC_CACHE["nc"]
    maps = make_in_maps(inputs)
    res = run_bass_kernel_spmd(nc, maps, core_ids=list(range(8)))
    r = res.results
    y_prompt = np.concatenate([r[i]["y"][:512].reshape(2, 256, D) for i in range(8)], axis=0).astype(np.float32)
    y_sample = np.stack([r[i]["y"][512:] for i in range(8)], axis=0).astype(np.float32)
    nk = np.concatenate([r[i]["nk"] for i in range(8)], axis=0)[:, None].astype(np.float32)
    nv = np.concatenate([r[i]["nv"] for i in range(8)], axis=0)[:, None].astype(np.float32)
    ns = np.concatenate([r[i]["ns"].reshape(2, 2, D) for i in range(8)], axis=0)[:, None].astype(np.float32)
    return (y_prompt, y_sample, nk, nv, ns)
```

```python
from contextlib import ExitStack
import numpy as np
import concourse.bass as bass
import concourse.mybir as mybir
from concourse.bass_utils import run_bass_kernel_spmd

F32 = mybir.dt.float32
BF16 = mybir.dt.bfloat16
AF = mybir.ActivationFunctionType
ALU = mybir.AluOpType

ENGINES = ("pe", "act", "dve", "pool", "sp")


class _Op:
    __slots__ = ("idx", "eng", "fn", "reads", "writes", "is_dma", "dsem", "dcum", "signal", "count", "waits",
                 "bar", "snap")

    def __init__(self, idx, eng, fn, reads, writes, is_dma, dsem):
        self.idx = idx
        self.eng = eng
        self.fn = fn
        self.reads = tuple(reads)
        self.writes = tuple(writes)
        self.is_dma = is_dma
        self.dsem = dsem
        self.dcum = 0
        self.signal = False
        self.count = 0
        self.waits = []
        self.bar = -1
        self.snap = None


class Prog:
    def __init__(self):
        self.ops = []
        self.dma_cum = {}
        self.nbar = 0

    def barrier(self):
        if self.ops and self.ops[-1].bar >= 0:
            return
        for e in ENGINES:
            o = _Op(len(self.ops), e, None, (), (), False, None)
            o.bar = self.nbar
            o.snap = dict(self.dma_cum)
            self.ops.append(o)
        self.nbar += 1

    def op(self, eng, fn, reads=(), writes=()):
        o = _Op(len(self.ops), eng, fn, reads, writes, False, None)
        self.ops.append(o)
        return o

    def dma(self, eng, fn, sem, reads=(), writes=()):
        o = _Op(len(self.ops), eng, fn, reads, writes, True, sem)
        self.dma_cum[sem] = self.dma_cum.get(sem, 0) + 16
        o.dcum = self.dma_cum[sem]
        self.ops.append(o)
        return o

    def _analyze(self):
        ops = self.ops
        last_write = {}
        readers = {}
        cum_now = {}
        for o in ops:
            if o.bar >= 0:
                last_write = {}
                readers = {}
                o.waits = ({}, {})
                for e in ENGINES:
                    for q in range(o.idx - 1, -1, -1):
                        pq = ops[q]
                        if pq.eng == e and pq.bar < 0 and not pq.is_dma:
                            pq.signal = True
                            break
                continue
            raw = set()
            war = set()
            for b in o.reads:
                if b in last_write:
                    raw.add(last_write[b])
            for b in o.writes:
                if b in last_write:
                    raw.add(last_write[b])
                for r in readers.get(b, ()):
                    war.add(r)
            raw.discard(o.idx)
            war.discard(o.idx)
            war -= raw
            dma_w = {}
            eng_deps = {}
            for d in raw | war:
                p = ops[d]
                if p.is_dma:
                    dma_w[p.dsem] = max(dma_w.get(p.dsem, 0), cum_now.get(p.dsem, 0))
                    continue
                if p.eng == o.eng and not o.is_dma:
                    if o.eng == "pe":
                        continue
                p.signal = True
                eng_deps.setdefault(p.eng, []).append(d)
            o.waits = (dma_w, eng_deps)
            if o.is_dma:
                cum_now[o.dsem] = o.dcum
            for b in o.reads:
                readers.setdefault(b, []).append(o.idx)
            for b in o.writes:
                last_write[b] = o.idx
                readers[b] = []
        cnt = {e: 0 for e in ENGINES}
        for o in ops:
            if o.signal:
                cnt[o.eng] += 1
                o.count = cnt[o.eng]
        waited = {e: {} for e in ENGINES}
        sofar = {e: 0 for e in ENGINES}
        for o in ops:
            if o.signal:
                sofar[o.eng] = o.count
            wd = waited[o.eng]
            if o.bar >= 0:
                fin = []
                for e in ENGINES:
                    if sofar[e] > wd.get(("e", e), 0):
                        wd[("e", e)] = sofar[e]
                        fin.append((("e", e), sofar[e]))
                for s, v in o.snap.items():
                    if v > wd.get(("d", s), 0):
                        wd[("d", s)] = v
                        fin.append((("d", s), v))
                o.waits = fin
                continue
            dma_w, eng_deps = o.waits
            fin = []
            for s, v in dma_w.items():
                k = ("d", s)
                if v > wd.get(k, 0):
                    wd[k] = v
                    fin.append((k, v))
            for e, ds in eng_deps.items():
                v = max(ops[d].count for d in ds)
                k = ("e", e)
                if v > wd.get(k, 0):
                    wd[k] = v
                    fin.append((k, v))
            o.waits = fin

    def emit(self, nc):
        self._analyze()
        ops = self.ops
        with ExitStack() as st:
            esem = {e: st.enter_context(nc.semaphore("s_" + e)) for e in ENGINES}
            bsem = st.enter_context(nc.semaphore("s_bar"))
            dsem = {k: st.enter_context(nc.semaphore("d_%d" % i)) for i, k in enumerate(self.dma_cum)}
            block = st.enter_context(nc.Block())

            def semof(key):
                return esem[key[1]] if key[0] == "e" else dsem[key[1]]

            def run(engname, eng):
                for o in ops:
                    if o.eng != engname:
                        continue
                    if o.bar >= 0:
                        for key, val in o.waits:
                            eng.wait_ge(semof(key), val)
                        continue
                    for key, val in o.waits:
                        eng.wait_ge(semof(key), val)
                    ins = o.fn(eng)
                    if o.is_dma:
                        ins.then_inc(dsem[o.dsem], 16)
                    elif o.signal:
                        ins.then_inc(esem[engname], 1)
                if engname == "sp":
                    for k, v in self.dma_cum.items():
                        eng.wait_ge(dsem[k], v)

            @block.tensor
            def _(e):
                run("pe", e)

            @block.scalar
            def _(e):
                run("act", e)

            @block.vector
            def _(e):
                run("dve", e)

            @block.gpsimd
            def _(e):
                run("pool", e)

            @block.sync
            def _(e):
                run("sp", e)


D = 1024
NTOK = 1536
NTT = 3
DFF = 2816
NFC = 22
NH = 8
EPS = 1e-6
NEG = -1e30
GELU_K = 0.7978845608028654
FULL_J0, FULL_N = 2, 14
INT_J0, INT_N = 5, 9
NSLOT = FULL_N + INT_N
SEG = [(0, 256, 0), (256, 256, 259), (512, 1024, 518)]
XLP = 518 + 1027


def _row_start(r):
    return min(max(r - 4, 0), 8)


def attn_plan():
    plan = []
    for qt in range(2):
        blocks = []
        for kb in range(8):
            rows = [r for r in range(8 * qt, 8 * qt + 8)
                    if _row_start(r) <= 2 * kb + 1 and _row_start(r) + 8 > 2 * kb]
            if not rows:
                continue
            assert rows == list(range(rows[0], rows[-1] + 1))
            segs = []
            for r in rows:
                tab = "full" if (r <= 3 or r >= 13) else "int"
                jj = r - 2 * kb + 8
                if segs and segs[-1][0] == tab and segs[-1][1] + segs[-1][2] == jj:
                    segs[-1][2] += 1
                else:
                    segs.append([tab, jj, 1])
            blocks.append((kb, (rows[0] - 8 * qt) * 64, (rows[-1] + 1 - 8 * qt) * 64, segs))
        plan.append(blocks)
    return plan


def host_bias_table(rpb):
    qc = np.arange(64)
    cs = np.clip(qc - 8, 0, 48)
    kc = np.arange(64)
    inwin = (kc[:, None] >= cs[None, :]) & (kc[:, None] < cs[None, :] + 16)
    dc = np.clip(kc[:, None] - qc[None, :], -15, 15) + 15
    out = np.full((NH, 128, NSLOT, 64), NEG, np.float32)
    slot = 0
    for tab, j0, n in (("full", FULL_J0, FULL_N), ("int", INT_J0, INT_N)):
        for jj in range(j0, j0 + n):
            for half in range(2):
                dr = 8 - jj + half
                ok = (abs(dr) <= 7) if tab == "full" else (-4 <= dr <= 3)
                if ok:
                    vals = rpb[:, dr + 7, :][:, dc]
                    out[:, half * 64:(half + 1) * 64, slot, :] = np.where(inwin[None], vals, NEG)
            slot += 1
    return out.reshape(NH, 128, NSLOT * 64)


def slot_of(tab, jj):
    return (jj - FULL_J0) if tab == "full" else (FULL_N + jj - INT_J0)


SKIP = set()


def build_nc(dumps=(), stop_after=None):
    nc = bass.Bass("TRN2", target_bir_lowering=False)
    P = Prog()

    def dram(name, shape, kind="ExternalInput", dt=F32):
        return nc.dram_tensor(name, list(shape), dt, kind=kind).ap()

    xin = dram("xin", [NTOK, D])
    ck_d = dram("ck", [8, 256, 64])
    cv_d = dram("cv", [8, 256, 64])
    pv_d = dram("pv", [256, 128])
    ident_d = dram("ident", [128, 128])
    btab_d = dram("btab", [NH, 128, NSLOT * 64])
    w_mod_d = dram("w_mod", [D, 9 * D])
    f1wi_d = dram("ffn1_w_in", [D, 2 * DFF])
    f1wo_d = dram("ffn1_w_out", [DFF, D])
    w_in_d = dram("w_in", [D, 5632])
    wa_d = dram("lru_wa", [2, 16, 64, 64])
    wi_d = dram("lru_wi", [2, 16, 64, 64])
    wbra_d = dram("w_br_attn", [512, D])
    wbrl_d = dram("w_br_lru", [D, D])
    wout_d = dram("w_out", [D, D])
    f2wi_d = dram("ffn2_w_in", [D, 2 * DFF])
    f2wo_d = dram("ffn2_w_out", [DFF, D])
    y_d = dram("y", [NTOK, D], kind="ExternalOutput")
    nk_d = dram("nk", [2, 8, 256, 64], kind="ExternalOutput")
    nv_d = dram("nv", [2, 8, 256, 64], kind="ExternalOutput")
    ns_d = dram("ns", [32, 128], kind="ExternalOutput")
    winv = w_in_d.rearrange("(kc p) n -> p kc n", p=128)

    dump_outs = {}
    XKEYS = [("xT", 0), ("xT", 1), ("xT", 2)]
    HKEYS = [("hT", 0), ("hT", 1), ("hT", 2)]

    class Scope:
        def __init__(self):
            self.st = ExitStack()

        def __enter__(self):
            self.st.__enter__()
            return self

        def T(self, name, shape, dt=F32):
            return self.st.enter_context(nc.sbuf_tensor(name, list(shape), dt))

        def __exit__(self, *a):
            P.barrier()
            return self.st.__exit__(*a)

    class Rot:
        def __init__(self, banks):
            self.b = list(banks)
            self.i = 0

        def next(self):
            r = self.b[self.i % len(self.b)]
            self.i += 1
            return r

    class WS:
        def __init__(self, sc, name, nslots, shape):
            self.name = name
            self.t = [sc.T("%s%d" % (name, i), shape, BF16) for i in range(nslots)]
            self.n = 0

        def load(self, fn):
            s = self.n % len(self.t)
            self.n += 1
            t = self.t[s]
            key = (self.name, s)
            for o, i in fn(t):
                P.dma("pool", lambda e, o=o, i=i: e.dma_start(out=o, in_=i), key, writes=[key])
            return t, key

    class Pipe:
        def __init__(self, n, depth, load, compute):
            self.n, self.depth, self.load, self.compute = n, depth, load, compute
            self.loaded = []

        def prefetch(self):
            for i in range(min(self.depth, self.n)):
                self.loaded.append(self.load(i))

        def step(self, i):
            if i + self.depth < self.n:
                self.loaded.append(self.load(i + self.depth))
            self.compute(i, *self.loaded[i])

        def run(self):
            if not self.loaded:
                self.prefetch()
            for i in range(self.n):
                self.step(i)

    def pipeline(n, depth, load, compute):
        Pipe(n, depth, load, compute).run()

    evac_i = [0]

    def evac_eng():
        evac_i[0] += 1
        return "act" if evac_i[0] % 2 else "dve"

    def copy_op(eng, out, in_, reads, writes, scale=None):
        if eng == "act":
            if scale is None:
                P.op("act", lambda e: e.activation(out=out, in_=in_, func=AF.Copy), reads, writes)
            else:
                P.op("act", lambda e: e.activation(out=out, in_=in_, func=AF.Copy, scale=scale), reads, writes)
        else:
            if scale is None:
                P.op(eng, lambda e: e.tensor_copy(out=out, in_=in_), reads, writes)
            else:
                P.op(eng, lambda e: e.tensor_scalar(out=out, in0=in_, scalar1=scale, scalar2=None, op0=ALU.mult),
                     reads, writes)

    def mm(out, lhsT, rhs, start, stop, reads, writes):
        P.op("pe", lambda e: e.matmul(out, lhsT=lhsT, rhs=rhs, start=start, stop=stop), reads, writes)

    def dump(name, ap, shape, dt=F32, reads=()):
        if name not in dumps:
            return
        d = dram("dbg_" + name, shape, kind="ExternalOutput", dt=dt)
        dump_outs[name] = d
        P.dma("sp", lambda e: e.dma_start(out=d, in_=ap), ("dump", name), reads=reads)

    with ExitStack() as st:
        def T(name, shape, dt=F32):
            return st.enter_context(nc.sbuf_tensor(name, list(shape), dt))

        ps = [st.enter_context(nc.psum_tensor("ps%d" % i, [128, 512], F32)) for i in range(8)]

        xT = T("xT", [128, 8, NTOK])
        hT = T("hT", [128, 8, NTOK], BF16)
        identf = T("identf", [128, 128])
        identb = T("identb", [128, 128], BF16)
        onesb = T("onesb", [128, 128], BF16)
        PA = T("PA", [128, 128])
        PB = T("PB", [128, 128])
        mods = T("mods", [128, 72, 2])
        Acoef = T("Acoef", [128, 3, 8, 2])
        Gcoef = T("Gcoef", [128, 3, 8, 2])
        lruc = T("lruc", [128, 5, 16])
        ltmp = T("ltmp", [128, 16])
        scb = T("scb", [128, 8, 2], BF16)
        rstd3 = T("rstd3", [128, 3, 512])
        nst = T("nst", [128, 32])
        epsb = T("epsb", [128, 1])
        qb25 = T("qb25", [128, 1])
        oneb = T("oneb", [128, 1])

        bmod = PA[:, 0:72]
        normg = PA[:, 72:96]
        convw = PA[:, 96:128]
        convb = PB[:, 0:8]
        ba_ = PB[:, 8:24]
        bi_ = PB[:, 24:40]
        lam = PB[:, 40:56]
        fing = PB[:, 56:64]
        st0 = PB[:, 64:80]
        cond = PB[:, 80:96]
        PKEY = ["PA", "PB"]

        P.op("pool", lambda e: e.memset(onesb[:], 1.0), writes=["onesb"])
        P.op("pool", lambda e: e.memset(epsb[:], EPS), writes=["epsb"])
        P.op("pool", lambda e: e.memset(qb25[:], 0.25 + 2e-7), writes=["qb25"])
        P.op("pool", lambda e: e.memset(oneb[:], 1.0), writes=["oneb"])
        P.op("pool", lambda e: e.memset(nst[:], 0.0), writes=["nst"])
        P.dma("sp", lambda e: e.dma_start(out=identf[:], in_=ident_d), "c0", writes=["identf"])
        P.dma("pool", lambda e: e.dma_start(out=identb[:], in_=ident_d), "c1", writes=["identb"])
        pstage = rstd3[:, 0, 0:256].rearrange("p (g n) -> p g n", g=2)
        P.dma("sp", lambda e: e.dma_start(out=pstage, in_=pv_d.rearrange("(g r) n -> r g n", g=2)),
              "c2", writes=["pstage"])
        for g_, (dst, key) in enumerate(((PA, "PA"), (PB, "PB"))):
            P.op("pe", lambda e, g_=g_: e.transpose(ps[g_][:, 0:128], pstage[:, g_, :], identf[:]),
                 reads=["pstage", "identf"], writes=[("ps", g_)])
            copy_op("dve", dst[:], ps[g_][:, 0:128], [("ps", g_)], [key])

        for ci in range(2):
            P.op("act", lambda e, ci=ci: e.activation(out=scb[:, :, ci], in_=cond[:, ci * 8:(ci + 1) * 8],
                                                      func=AF.Silu), reads=PKEY, writes=["scb"])
        P.op("act", lambda e: e.activation(out=ltmp[:], in_=lam, func=AF.Exp, scale=-1.0), reads=PKEY, writes=["ltmp"])
        P.op("act", lambda e: e.activation(out=ltmp[:], in_=ltmp[:], func=AF.Ln, bias=oneb[:, 0:1]),
             reads=["ltmp", "oneb"], writes=["ltmp"])
        P.op("dve", lambda e: e.tensor_scalar(out=lruc[:, 0, :], in0=ltmp[:], scalar1=-4.0, scalar2=None, op0=ALU.mult),
             reads=["ltmp"], writes=["lruc"])
        P.op("dve", lambda e: e.tensor_scalar(out=lruc[:, 1, :], in0=ltmp[:], scalar1=-8.0, scalar2=None, op0=ALU.mult),
             reads=["ltmp"], writes=["lruc"])
        P.op("dve", lambda e: e.tensor_scalar(out=lruc[:, 2, :], in0=ltmp[:], scalar1=-8.0, scalar2=float(np.log(0.25)),
                                              op0=ALU.mult, op1=ALU.add), reads=["ltmp"], writes=["lruc"])
        P.op("dve", lambda e: e.tensor_scalar(out=lruc[:, 3, :], in0=ba_, scalar1=0.5, scalar2=None, op0=ALU.mult),
             reads=PKEY, writes=["lruc"])
        P.op("dve", lambda e: e.tensor_scalar(out=lruc[:, 4, :], in0=bi_, scalar1=0.5, scalar2=None, op0=ALU.mult),
             reads=PKEY, writes=["lruc"])

        def norm_sq(sqt, tt):
            cs = slice(tt * 512, (tt + 1) * 512)
            b = 4 + tt
            for h in range(2):
                P.op("act", lambda e, h=h: e.activation(out=sqt[:, 4 * h:4 * h + 4, :], in_=xT[:, 4 * h:4 * h + 4, cs],
                                                        func=AF.Square), reads=[("xT", tt)], writes=[("sqt", h)])
                for c in range(4 * h, 4 * h + 4):
                    mm(ps[b][:], onesb[:], sqt[:, c, :], c == 0, c == 7, [("sqt", h), "onesb"], [("ps", b)])

        def norm_fin(tt):
            b = 4 + tt
            P.op("act", lambda e: e.activation(out=rstd3[:, tt, :], in_=ps[b][:], func=AF.Sqrt, scale=1.0 / D,
                                               bias=epsb[:, 0:1]), reads=[("ps", b), "epsb"], writes=[("rstd", tt)])
            P.op("dve", lambda e: e.reciprocal(out=rstd3[:, tt, :], in_=rstd3[:, tt, :]), reads=[("rstd", tt)],
                 writes=[("rstd", tt)])

        def norm_stats(sqt, tt):
            norm_sq(sqt, tt)
            norm_fin(tt)

        N_MODS_EARLY = 6
        N_MODS_FFN1 = 10
        wmv = w_mod_d.rearrange("(kc p) n -> p kc n", p=128)

        def mods_load(ws, i):
            return ws.load(lambda t: [(t[:], wmv[:, :, i * 512:(i + 1) * 512])])

        def mods_chunk(i, t, key, b):
            for q in range(4):
                for kc in range(8):
                    mm(ps[b][:, 2 * q:2 * q + 2], t[:, kc, q * 128:(q + 1) * 128], scb[:, kc, :],
                       kc == 0, kc == 7, [key, "scb"], [("ps", b)])
            for ci in range(2):
                P.op("dve", lambda e, ci=ci: e.tensor_tensor(
                    out=mods[:, 4 * i:4 * i + 4, ci], in0=ps[b][:, 0:8].rearrange("p (o c) -> p o c", c=2)[:, :, ci],
                    in1=bmod[:, 4 * i:4 * i + 4], op=ALU.add), reads=[("ps", b)] + PKEY, writes=[("mods", i // 2)])

        def mods_coefs(acoef_ks, gcoef_ks):
            for k in acoef_ks:
                for ci in range(2):
                    P.op("dve", lambda e, k=k, ci=ci: e.scalar_tensor_tensor(
                        out=Acoef[:, k, :, ci], in0=mods[:, (3 * k + 1) * 8:(3 * k + 2) * 8, ci], scalar=1.0,
                        in1=normg[:, k * 8:(k + 1) * 8], op0=ALU.add, op1=ALU.mult),
                        reads=[("mods", 3 * k + 1)] + PKEY, writes=[("Acoef", k)])
            for k in gcoef_ks:
                P.op("dve", lambda e, k=k: e.tensor_scalar(
                    out=Gcoef[:, k, :, :], in0=mods[:, (3 * k + 2) * 8:(3 * k + 3) * 8, :], scalar1=0.5, scalar2=None,
                    op0=ALU.mult), reads=[("mods", 3 * k + 2)], writes=[("Gcoef", k)])

        with Scope() as sc:
            xs = [sc.T("xs%d" % i, [128, D]) for i in range(4)]
            sqt_p = sc.T("sqt_p", [128, 8, 512], BF16)
            wsm = WS(sc, "wmod", 3, [128, 8, 512])
            mrot = Rot([0, 1])
            pipe_m = Pipe(N_MODS_EARLY, 2, lambda i: mods_load(wsm, i),
                          lambda i, t, key: mods_chunk(i, t, key, mrot.next()))
            pipe_m.prefetch()
            def x_dma(j):
                s = j % 4
                P.dma("sp", lambda e: e.dma_start(out=xs[s][:], in_=xin[j * 128:(j + 1) * 128, :]),
                      ("xs", s), writes=[("xs", s)])
            for j in range(4):
                x_dma(j)
            fin_at = {5: 0, 9: 1}
            for j in range(12):
                s = j % 4
                for half in range(2):
                    b = 2 + (2 * j + half) % 2
                    for q in range(4):
                        c = half * 4 + q
                        P.op("pe", lambda e, b=b, q=q, c=c, s=s: e.transpose(
                            ps[b][:, q * 128:(q + 1) * 128], xs[s][:, c * 128:(c + 1) * 128], identf[:]),
                            reads=[("xs", s), "identf"], writes=[("ps", b)])
                    copy_op(evac_eng(), xT[:, half * 4:half * 4 + 4, j * 128:(j + 1) * 128],
                            ps[b][:].rearrange("p (q t) -> p q t", q=4), [("ps", b)], [("xT", j // 4)])
                if j + 4 < 12:
                    x_dma(j + 4)
                if j % 2 == 1 and j // 2 < N_MODS_EARLY:
                    pipe_m.step(j // 2)
                if j % 4 == 3:
                    norm_sq(sqt_p, j // 4)
                if j in fin_at:
                    norm_fin(fin_at[j])
            norm_fin(2)
            mods_coefs(acoef_ks=(0,), gcoef_ks=(0,))
        dump("x0", xT[:], [128, 8, NTOK], reads=XKEYS)

        def rmsnorm(sqt, out_fn, stats_done=False):
            def apply(tt):
                ci = 0 if tt == 0 else 1
                cs = slice(tt * 512, (tt + 1) * 512)
                for c in range(8):
                    out_fn(tt, c, ci, cs)
            if stats_done:
                for tt in range(NTT):
                    apply(tt)
                return
            norm_sq(sqt, 0)
            norm_sq(sqt, 1)
            norm_fin(0)
            norm_sq(sqt, 2)
            norm_fin(1)
            apply(0)
            norm_fin(2)
            apply(1)
            apply(2)

        def norm_bufs(sc, k):
            sqt = sc.T("sqt%d" % k, [128, 8, 512], BF16)
            ntmp = [sc.T("ntmp%d_%d" % (k, i), [128, 512]) for i in range(2)]
            return sqt, ntmp

        def norm_to_hT(k, bufs, stats_done=False):
            sqt, ntmp = bufs

            def out_fn(tt, c, ci, cs):
                s = c % 2
                P.op("dve", lambda e: e.scalar_tensor_tensor(
                    out=ntmp[s][:], in0=xT[:, c, cs], scalar=Acoef[:, k, c, ci:ci + 1], in1=rstd3[:, tt, :],
                    op0=ALU.mult, op1=ALU.mult), reads=[("xT", tt), ("rstd", tt), ("Acoef", k)], writes=[("ntmp", s)])
                P.op("act", lambda e: e.activation(out=hT[:, c, cs], in_=ntmp[s][:], func=AF.Identity,
                                                   bias=mods[:, 3 * k * 8 + c, ci:ci + 1]),
                     reads=[("ntmp", s), ("mods", 3 * k)], writes=[("hT", tt)])
            rmsnorm(sqt, out_fn, stats_done)

        def run_ffn(k, wi_d, wo_d):
            with Scope() as sc:
                nb_ = norm_bufs(sc, k)
                hid = sc.T("hid%d" % k, [128, NFC, NTOK], BF16)
                gtmp = [sc.T("gtmp%d_%d" % (k, i), [128, 512]) for i in range(2)]
                wsi = WS(sc, "fwi%d" % k, 3, [128, 8, 512])
                wso = WS(sc, "fwo%d" % k, 2, [128, NFC, 128])
                wiv = wi_d.rearrange("(kc p) n -> p kc n", p=128)
                wov = wo_d.rearrange("(fc p) n -> p fc n", p=128)
                rot = Rot([0, 1, 2, 3])

                def load_i(i):
                    return wsi.load(lambda t: [(t[:, :, 0:256], wiv[:, :, i * 256:(i + 1) * 256]),
                                               (t[:, :, 256:512], wiv[:, :, DFF + i * 256:DFF + (i + 1) * 256])])

                mid = list(range(N_MODS_EARLY, N_MODS_FFN1)) if k == 0 else []
                mws = WS.__new__(WS)
                mws.name, mws.t, mws.n = "wmodf", [nb_[0]], 0
                mld = {}

                def comp_i(i, t, key):
                    if mid and i % 2 == 0 and i // 2 < len(mid):
                        mld[i // 2] = mods_load(mws, mid[i // 2])
                    if mid and i % 2 == 1 and i // 2 < len(mid):
                        mods_chunk(mid[i // 2], *mld[i // 2], 4 + (i // 2) % 2)
                    if i in (7, 9):
                        pre_o.append(load_o(len(pre_o)))
                    for q in range(2):
                        fc = 2 * i + q
                        for tt in range(NTT):
                            cs = slice(tt * 512, (tt + 1) * 512)
                            bg, bu = rot.next(), rot.next()
                            for kc in range(8):
                                mm(ps[bg][:], t[:, kc, q * 128:(q + 1) * 128], hT[:, kc, cs], kc == 0, kc == 7,
                                   [key, ("hT", tt)], [("ps", bg)])
                            for kc in range(8):
                                mm(ps[bu][:], t[:, kc, 256 + q * 128:256 + (q + 1) * 128], hT[:, kc, cs], kc == 0,
                                   kc == 7, [key, ("hT", tt)], [("ps", bu)])
                            s = (fc * NTT + tt) % 2
                            P.op("act", lambda e, bg=bg, s=s: e.activation(out=gtmp[s][:], in_=ps[bg][:], func=AF.Silu),
                                 reads=[("ps", bg)], writes=[("gtmp", s)])
                            P.op("dve", lambda e, bu=bu, s=s, fc=fc, cs=cs: e.tensor_tensor(
                                out=hid[:, fc, cs], in0=gtmp[s][:], in1=ps[bu][:], op=ALU.mult),
                                reads=[("gtmp", s), ("ps", bu)], writes=[("hid", fc, tt)])
                def load_o(oc):
                    return wso.load(lambda t: [(t[:], wov[:, :, oc * 128:(oc + 1) * 128])])

                pipe_i = Pipe(11, 2, load_i, comp_i)
                pipe_i.prefetch()
                pre_o = []
                norm_to_hT(k, nb_, stats_done=(k == 0))
                pipe_i.run()
                rot2 = Rot([4, 5, 6, 7])

                def comp_o(oc, t, key):
                    GK = k
                    for tt in range(NTT):
                        ci = 0 if tt == 0 else 1
                        cs = slice(tt * 512, (tt + 1) * 512)
                        b = rot2.next()
                        for fc in range(NFC):
                            mm(ps[b][:], t[:, fc, :], hid[:, fc, cs], fc == 0, fc == NFC - 1,
                               [key, ("hid", fc, tt)], [("ps", b)])
                        P.op("dve", lambda e, b=b, cs=cs, ci=ci: e.scalar_tensor_tensor(
                            out=xT[:, oc, cs], in0=ps[b][:], scalar=Gcoef[:, k, oc, ci:ci + 1], in1=xT[:, oc, cs],
                            op0=ALU.mult, op1=ALU.add), reads=[("ps", b), ("Gcoef", GK), ("xT", tt)], writes=[("xT", tt)])
                for oc in range(8):
                    comp_o(oc, *pre_o[oc])
                    if oc + 2 < 8:
                        pre_o.append(load_o(oc + 2))
                if k == 0:
                    mods_coefs(acoef_ks=(1,), gcoef_ks=())

        def mixer():
            with Scope() as sm:
                attnT = sm.T("attnT", [128, 4, NTOK], BF16)
                bdw = sm.T("bdw", [128, 2, 2, 8, 128], BF16)

                def load_bdw(d):
                    if d == 0:
                        P.op("pool", lambda e: e.memset(bdw[:], 0.0), writes=["bdw"])
                    for gi, wd in enumerate((wa_d, wi_d)):
                        for e_ in range(2):
                            src_ = wd[d].rearrange("(c e) i o -> e i c o", e=2)[e_]
                            P.dma("pool", lambda e, gi=gi, e_=e_, src_=src_: e.dma_start(
                                out=bdw[e_ * 64:(e_ + 1) * 64, d, gi, :, e_ * 64:(e_ + 1) * 64], in_=src_),
                                "bdw", writes=["bdw"])
                attention(attnT, load_bdw)
                dump("attnT", attnT[:], [128, 4, NTOK], dt=BF16, reads=[("attnT", 0), ("attnT", 1), ("attnT", 2)])
                if stop_after == "attn":
                    return
                lruT = sm.T("lruT", [128, 8, NTOK], BF16)
                lru(lruT, bdw)
                dump("lruT", lruT[:], [128, 8, NTOK], dt=BF16, reads=["lruT"])
                if stop_after == "lru":
                    return
                merge(attnT, lruT)
            dump("x2", xT[:], [128, 8, NTOK], reads=XKEYS)

        def attention(attnT, load_bdw):
            plan = attn_plan()
            with Scope() as sa:
                nb_ = norm_bufs(sa, 1)
                qT = sa.T("qT", [128, NTOK], BF16)
                kTz = sa.T("kTz", [128, 2, NTOK], BF16)
                ckTz = sa.T("ckTz", [128, 2, 256], BF16)
                Vp = sa.T("Vp", [128, 14, 2, 128], BF16)
                btab = sa.T("btab_sb", [128, 2, NSLOT * 64], BF16)
                Eb = [sa.T("Eb%d" % i, [128, 512], BF16) for i in range(3)]
                ktok = [sa.T("ktok%d" % i, [128, 128]) for i in range(8)]
                kcount = [0]
                cst = [sa.T("cst%d" % i, [128, 2, 8, 64]) for i in range(2)]
                rD = sa.T("rD", [128, 512])
                wqkv = WS(sa, "wqkv", 2, [128, 8, 384])
                VK = [("Vp", j) for j in range(14)]
                def setup_memsets():
                    P.op("pool", lambda e: e.memset(kTz[:], 0.0), writes=["kTz"])
                    P.op("pool", lambda e: e.memset(ckTz[:], 0.0), writes=["ckTz"])
                    P.op("pool", lambda e: e.memset(Vp[:], 1.0), writes=VK)
                for w_, src in enumerate((ck_d, cv_d)):
                    for kb in range(2):
                      if "cst" not in SKIP:
                        P.dma("sp", lambda e, w_=w_, src=src, kb=kb: e.dma_start(
                            out=cst[w_][:, kb, :, :], in_=src[:, kb * 128:(kb + 1) * 128, :].rearrange("h p d -> p h d")),
                            ("cst", w_), writes=[("cst", w_)])

                def vp_diag(j):
                    base = Vp[:, j, 0, 0:64]
                    return bass.AP(base.tensor, base.offset, [list(base.ap[0]), [192, 2], [1, 64]])

                rot = Rot([0, 1, 2, 3])
                srot = Rot([0, 1, 2, 3])
                ecount = [0]
                gcount = [0]

                def load(hp):
                    return wqkv.load(lambda t: [(t[:, :, 0:128], winv[:, :, hp * 128:(hp + 1) * 128]),
                                                (t[:, :, 128:256], winv[:, :, 512 + hp * 128:512 + (hp + 1) * 128]),
                                                (t[:, :, 256:384], winv[:, :, 1024 + hp * 128:1024 + (hp + 1) * 128])])

                def s_part(blk):
                    (h, e_, kT_ap, q_ap, ncols, c0, bias_segs, v_ap, rkeys, vkey) = blk["args"]
                    sb = srot.next()
                    nb = len(bias_segs)
                    mm(ps[sb][:, c0:c0 + ncols], kT_ap, q_ap, True, nb == 0, rkeys, [("ps", sb)])
                    off = c0
                    for bi, (tab, jj0, nr) in enumerate(bias_segs):
                        sl = slot_of(tab, jj0)
                        mm(ps[sb][:, off:off + nr * 64], identb[:], btab[:, e_, sl * 64:(sl + nr) * 64], False,
                           bi == nb - 1, ["identb", "btab"], [("ps", sb)])
                        off += nr * 64
                    ei = ecount[0] % len(Eb)
                    ecount[0] += 1
                    blk["ei"] = ei
                    P.op("act", lambda e: e.activation(out=Eb[ei][:, 0:ncols], in_=ps[sb][:, c0:c0 + ncols], func=AF.Exp),
                         reads=[("ps", sb)], writes=[("Eb", ei)])

                def pv_part(blk):
                    (h, e_, kT_ap, q_ap, ncols, c0, bias_segs, v_ap, rkeys, vkey) = blk["args"]
                    ei = blk["ei"]
                    ob = blk["banks"]
                    mm(ps[ob][:, c0:c0 + ncols], v_ap, Eb[ei][:, 0:ncols], blk["first"], blk["last"],
                       [("Eb", ei), vkey], [("ps", ob)])

                def finish(hp, e_, tok0, ncols, c0, tt, ob):
                    pr = slice(e_ * 64, (e_ + 1) * 64)
                    dr = slice((1 - e_) * 64, (2 - e_) * 64)
                    P.op("dve", lambda e: e.reciprocal(out=rD[pr, c0:c0 + ncols], in_=ps[ob][dr, c0:c0 + ncols]),
                         reads=[("ps", ob)], writes=[("rD", e_)])
                    P.op("dve", lambda e: e.tensor_tensor(out=attnT[pr, hp, tok0:tok0 + ncols],
                                                          in0=ps[ob][pr, c0:c0 + ncols], in1=rD[pr, c0:c0 + ncols],
                                                          op=ALU.mult),
                         reads=[("ps", ob), ("rD", e_)], writes=[("attnT", tt)])

                def comp(hp, t, key):
                    for h2 in range(2):
                      if "btab" not in SKIP:
                        P.dma("pool", lambda e, h2=h2: e.dma_start(out=btab[:, h2, :], in_=btab_d[2 * hp + h2]),
                              "btab", writes=["btab"])
                    if "proj" in SKIP:
                        return
                    for tt in range(NTT if "qk" not in SKIP else 0):
                        cs = slice(tt * 512, (tt + 1) * 512)
                        b = rot.next()
                        for kc in range(8):
                            mm(ps[b][:], t[:, kc, 0:128], hT[:, kc, cs], kc == 0, kc == 7, [key, ("hT", tt)], [("ps", b)])
                        copy_op(evac_eng(), qT[:, cs], ps[b][:], [("ps", b)], ["qT"], scale=0.125)
                        b = rot.next()
                        for kc in range(8):
                            mm(ps[b][:], t[:, kc, 128:256], hT[:, kc, cs], kc == 0, kc == 7, [key, ("hT", tt)], [("ps", b)])
                        eng_ = evac_eng()
                        for e_ in range(2):
                            pr = slice(e_ * 64, (e_ + 1) * 64)
                            copy_op(eng_, kTz[pr, e_, cs], ps[b][pr, :], [("ps", b)], ["kTz"])
                    for j in range(4 if "ktok" not in SKIP else 0):
                        b = rot.next()
                        for kc in range(8):
                            mm(ps[b][:, 0:128], hT[:, kc, j * 128:(j + 1) * 128], t[:, kc, 128:256], kc == 0, kc == 7,
                               [key, ("hT", 0)], [("ps", b)])
                        s = kcount[0] % 8
                        kcount[0] += 1
                        copy_op(evac_eng(), ktok[s][:], ps[b][:, 0:128], [("ps", b)], [("ktok", s)])
                        sq, t0 = j // 2, (j % 2) * 128
                        if "kvout" not in SKIP:
                          P.dma("sp", lambda e, s=s, sq=sq, t0=t0: e.dma_start(
                            out=nk_d[sq][2 * hp:2 * hp + 2, t0:t0 + 128, :].rearrange("h t d -> t h d"),
                            in_=ktok[s][:].rearrange("p (h d) -> p h d", h=2)), ("kst", s), reads=[("ktok", s)])
                    for j in range(12 if "vtok" not in SKIP else 0):
                        b = rot.next()
                        for kc in range(8):
                            mm(ps[b][:, 0:128], hT[:, kc, j * 128:(j + 1) * 128], t[:, kc, 256:384], kc == 0, kc == 7,
                               [key, ("hT", j // 4)], [("ps", b)])
                        if j >= 4:
                            copy_op(evac_eng(), vp_diag(j), ps[b][:, 0:128].rearrange("p (e d) -> p e d", e=2),
                                    [("ps", b)], [("Vp", j)])
                        else:
                            s = kcount[0] % 8
                            kcount[0] += 1
                            copy_op("act", ktok[s][:], ps[b][:, 0:128], [("ps", b)], [("ktok", s)])
                            copy_op("dve", vp_diag(j), ktok[s][:].rearrange("p (e d) -> p e d", e=2),
                                    [("ktok", s)], [("Vp", j)])
                            sq, t0 = j // 2, (j % 2) * 128
                            if "kvout" not in SKIP:
                              P.dma("sp", lambda e, s=s, sq=sq, t0=t0: e.dma_start(
                                out=nv_d[sq][2 * hp:2 * hp + 2, t0:t0 + 128, :].rearrange("h t d -> t h d"),
                                in_=ktok[s][:].rearrange("p (h d) -> p h d", h=2)), ("kst", s), reads=[("ktok", s)])
                    for kb in range(2 if "ctx" not in SKIP else 0):
                        b = rot.next()
                        P.op("pe", lambda e, kb=kb, b=b: e.transpose(
                            ps[b][:, 0:128], cst[0][:, kb, 2 * hp:2 * hp + 2, :].rearrange("p h d -> p (h d)"), identf[:]),
                            reads=[("cst", 0), "identf"], writes=[("ps", b)])
                        eng_ = evac_eng()
                        for e_ in range(2):
                            pr = slice(e_ * 64, (e_ + 1) * 64)
                            copy_op(eng_, ckTz[pr, e_, kb * 128:(kb + 1) * 128], ps[b][pr, 0:128],
                                    [("ps", b)], ["ckTz"])
                        copy_op("dve", vp_diag(12 + kb), cst[1][:, kb, 2 * hp:2 * hp + 2, :], [("cst", 1)],
                                [("Vp", 12 + kb)])
                    if hp == 0:
                        dump("qT", qT[:], [128, NTOK], dt=BF16, reads=["qT"])
                        dump("kTz", kTz[:], [128, 2, NTOK], dt=BF16, reads=["kTz"])
                        dump("Vp", Vp[:], [128, 14, 2, 128], dt=BF16, reads=VK)
                        dump("ckTz", ckTz[:], [128, 2, 256], dt=BF16, reads=["ckTz"])
                    groups = []
                    for sq in range(2):
                        for e_ in range(2):
                            blks = []
                            for kb in range(2):
                                k0 = sq * 256 + kb * 128
                                blks.append((2 * hp + e_, e_, kTz[:, e_, k0:k0 + 128], qT[:, sq * 256:(sq + 1) * 256], 256,
                                             sq * 256, [], Vp[:, 2 * sq + kb, e_, :], ["kTz", "qT"], ("Vp", 2 * sq + kb)))
                            groups.append((blks, (hp, e_, sq * 256, 256, sq * 256, 0)))
                    for qt in range(2):
                        q0 = 512 + qt * 512
                        for e_ in range(2):
                            blks = []
                            for kb in range(2):
                                blks.append((2 * hp + e_, e_, ckTz[:, e_, kb * 128:(kb + 1) * 128], qT[:, q0:q0 + 512], 512,
                                             0, [], Vp[:, 12 + kb, e_, :], ["ckTz", "qT"], ("Vp", 12 + kb)))
                            for (kb, c0, c1, segs) in plan[qt]:
                                blks.append((2 * hp + e_, e_, kTz[:, e_, 512 + kb * 128:512 + (kb + 1) * 128],
                                             qT[:, q0 + c0:q0 + c1], c1 - c0, c0, segs, Vp[:, 4 + kb, e_, :],
                                             ["kTz", "qT"], ("Vp", 4 + kb)))
                            groups.append((blks, (hp, e_, q0, 512, 0, 1 + qt)))
                    flat = []
                    for gi, (blks, fin) in enumerate(groups):
                        banks = 4 + (gcount[0] % 4)
                        gcount[0] += 1
                        for bi, args in enumerate(blks):
                            flat.append({"args": args, "first": bi == 0, "last": bi == len(blks) - 1,
                                         "banks": banks, "fin": fin})
                    LA = 2
                    for n in range(min(LA, len(flat))):
                        s_part(flat[n])
                    for n, blk in enumerate(flat):
                        if n + LA < len(flat):
                            s_part(flat[n + LA])
                        pv_part(blk)
                        if blk["last"]:
                            finish(*blk["fin"], blk["banks"])
                wsm2 = WS(sa, "wmod2", 2, [128, 8, 512])
                late = list(range(N_MODS_FFN1, 18))
                mloaded = {}

                def comp2(hp, t, key):
                    for i in late[2 * hp:2 * hp + 2]:
                        mloaded[i] = mods_load(wsm2, i)
                    if hp in (1, 2):
                        load_bdw(hp - 1)
                    comp(hp, t, key)
                    for i in late[2 * hp:2 * hp + 2]:
                        mods_chunk(i, *mloaded[i], rot.next())

                pipe_a = Pipe(4, 1, load, comp2)
                pipe_a.prefetch()
                setup_memsets()
                norm_to_hT(1, nb_)
                dump("h2", hT[:], [128, 8, NTOK], dt=BF16, reads=HKEYS)
                pipe_a.run()
                mods_coefs(acoef_ks=(2,), gcoef_ks=(1, 2))

        def lru(lruT, bdw):
            with Scope() as sl:
                W = 1024
                xlp = sl.T("xlp", [128, W + 6])
                xc = [sl.T("xc%d" % s, [128, W]) for s in range(2)]
                xcb = [sl.T("xcb%d" % s, [128, W], BF16) for s in range(2)]
                tr = [[sl.T("tr%d_%d" % (s, d), [128, W]) for d in range(2)] for s in range(2)]
                ti = [[sl.T("ti%d_%d" % (s, d), [128, W]) for d in range(2)] for s in range(2)]
                a2 = [sl.T("a2%d" % d, [128, W]) for d in range(2)]
                xg = [sl.T("xg%d" % s, [128, W]) for s in range(2)]
                x2 = [sl.T("x2%d" % s, [128, W]) for s in range(2)]
                wl = WS(sl, "wl", 2, [128, 8, 256])
                P.op("pool", lambda e: e.memset(xlp[:], 0.0), writes=["xlp"])
                rot = Rot([0, 1, 2, 3, 4, 5, 6, 7])
                units = [(0, c) for c in range(4)] + [(1, c) for c in range(8)] + [(0, c) for c in range(4, 8)]
                loaded = {}

                def load(i):
                    pss, c = units[i]
                    loaded[i] = wl.load(lambda t: [(t[:, :, 0:128], winv[:, :, 1536 + c * 128:1536 + (c + 1) * 128]),
                                                   (t[:, :, 128:256], winv[:, :, 2560 + c * 128:2560 + (c + 1) * 128])])

                def geom(pss):
                    if pss == 0:
                        return 0, 512, [0], [(0, 256, 0), (256, 256, 259)]
                    return 512, 1024, [1, 2], [(0, 1024, 0)]

                def stage_f1(i):
                    pss, c = units[i]
                    s = i % 2
                    t, key = loaded[i]
                    tok0, ntok, tiles, segs = geom(pss)
                    xl_b, gl_b = [], []
                    for tt in tiles:
                        cs = slice(tt * 512, (tt + 1) * 512)
                        b = rot.next()
                        xl_b.append(b)
                        for kc in range(8):
                            mm(ps[b][:], t[:, kc, 0:128], hT[:, kc, cs], kc == 0, kc == 7, [key, ("hT", tt)], [("ps", b)])
                    for tt in tiles:
                        cs = slice(tt * 512, (tt + 1) * 512)
                        b = rot.next()
                        gl_b.append(b)
                        for kc in range(8):
                            mm(ps[b][:], t[:, kc, 128:256], hT[:, kc, cs], kc == 0, kc == 7, [key, ("hT", tt)], [("ps", b)])
                    if pss == 0 and i > 0 and units[i - 1][0] == 1:
                        P.op("dve", lambda e: e.memset(xlp[:, 258:261], 0.0), reads=["xlp"], writes=["xlp"])
                        P.op("dve", lambda e: e.memset(xlp[:, 517:518], 0.0), reads=["xlp"], writes=["xlp"])
                    if pss == 0:
                        for sq in range(2):
                            copy_op("act", xlp[:, 259 * sq + 2:259 * sq + 258], ps[xl_b[0]][:, sq * 256:(sq + 1) * 256],
                                    [("ps", xl_b[0])], ["xlp"])
                    else:
                        for k_, b in enumerate(xl_b):
                            copy_op("act", xlp[:, 2 + k_ * 512:2 + (k_ + 1) * 512], ps[b][:], [("ps", b)], ["xlp"])
                    for k_, b in enumerate(gl_b):
                        ls = slice(k_ * 512, (k_ + 1) * 512)
                        P.op("act", lambda e, b=b, ls=ls: e.activation(out=xg[s][:, ls], in_=ps[b][:], func=AF.Copy),
                             reads=[("ps", b)], writes=[("xg", s)])
                    for (t0, ln, pb) in segs:
                        P.op("dve", lambda e, t0=t0, ln=ln, pb=pb: e.tensor_scalar(
                            out=xc[s][:, t0:t0 + ln], in0=xlp[:, pb:pb + ln], scalar1=convw[:, c:c + 1],
                            scalar2=convb[:, c:c + 1], op0=ALU.mult, op1=ALU.add), reads=["xlp"] + PKEY, writes=[("xc", s)])
                        for j in range(1, 4):
                            P.op("dve", lambda e, t0=t0, ln=ln, pb=pb, j=j: e.scalar_tensor_tensor(
                                out=xc[s][:, t0:t0 + ln], in0=xlp[:, pb + j:pb + j + ln],
                                scalar=convw[:, j * 8 + c:j * 8 + c + 1], in1=xc[s][:, t0:t0 + ln],
                                op0=ALU.mult, op1=ALU.add), reads=["xlp", ("xc", s)] + PKEY, writes=[("xc", s)])

                def stage_f2(i):
                    pss, c = units[i]
                    s = i % 2
                    tok0, ntok, tiles, segs = geom(pss)
                    P.op("act", lambda e: e.activation(out=xcb[s][:, 0:ntok], in_=xc[s][:, 0:ntok], func=AF.Copy),
                         reads=[("xc", s)], writes=[("xcb", s)])

                def stage_a1(i):
                    pss, c = units[i]
                    s = i % 2
                    tok0, ntok, tiles, segs = geom(pss)
                    P.op("dve", lambda e: e.scalar_tensor_tensor(
                        out=x2[s][:, 0:ntok], in0=xg[s][:, 0:ntok], scalar=0.044715 * GELU_K, in1=xg[s][:, 0:ntok],
                        op0=ALU.mult, op1=ALU.mult), reads=[("xg", s)], writes=[("x2", s)])
                    P.op("dve", lambda e: e.scalar_tensor_tensor(
                        out=x2[s][:, 0:ntok], in0=x2[s][:, 0:ntok], scalar=GELU_K, in1=xg[s][:, 0:ntok],
                        op0=ALU.add, op1=ALU.mult), reads=[("x2", s), ("xg", s)], writes=[("x2", s)])
                    for d in range(2):
                        col = d * 8 + c
                        for gi, dst, dkey, brow in ((0, tr[s][d], ("tr", s, d), 3), (1, ti[s][d], ("ti", s, d), 4)):
                            for k_ in range(len(tiles)):
                                ls = slice(k_ * 512, (k_ + 1) * 512)
                                b = rot.next()
                                mm(ps[b][:], bdw[:, d, gi, c, :], xcb[s][:, ls], True, True, ["bdw", ("xcb", s)], [("ps", b)])
                                P.op("act", lambda e, b=b, ls=ls, dst=dst, brow=brow, col=col: e.activation(
                                    out=dst[:, ls], in_=ps[b][:], func=AF.Tanh, scale=0.5,
                                    bias=lruc[:, brow, col:col + 1]), reads=[("ps", b), "lruc"], writes=[dkey])
                    P.op("act", lambda e: e.activation(out=x2[s][:, 0:ntok], in_=x2[s][:, 0:ntok], func=AF.Tanh),
                         reads=[("x2", s)], writes=[("x2", s)])
                    for d in range(2):
                        col = d * 8 + c
                        P.op("act", lambda e, d=d, col=col: e.activation(
                            out=a2[d][:, 0:ntok], in_=tr[s][d][:, 0:ntok], func=AF.Exp, scale=lruc[:, 1, col:col + 1],
                            bias=lruc[:, 2, col:col + 1]), reads=[("tr", s, d), "lruc"], writes=[("a2", d)])
                        P.op("act", lambda e, d=d, col=col: e.activation(
                            out=tr[s][d][:, 0:ntok], in_=tr[s][d][:, 0:ntok], func=AF.Exp, scale=lruc[:, 0, col:col + 1],
                            bias=lruc[:, 0, col:col + 1]), reads=[("tr", s, d), "lruc"], writes=[("tr", s, d)])
                    for d in range(2):
                        P.op("act", lambda e, d=d: e.activation(out=a2[d][:, 0:ntok], in_=a2[d][:, 0:ntok], func=AF.Sqrt,
                                                                scale=-1.0, bias=qb25[:, 0:1]),
                             reads=[("a2", d), "qb25"], writes=[("a2", d)])

                def stage_a2(i):
                    pss, c = units[i]
                    s = i % 2
                    tok0, ntok, tiles, segs = geom(pss)
                    P.op("dve", lambda e: e.scalar_tensor_tensor(out=x2[s][:, 0:ntok], in0=x2[s][:, 0:ntok], scalar=1.0,
                                                                 in1=xg[s][:, 0:ntok], op0=ALU.add, op1=ALU.mult),
                         reads=[("x2", s), ("xg", s)], writes=[("x2", s)])
                    for d in range(2):
                        P.op("dve", lambda e, d=d: e.scalar_tensor_tensor(
                            out=ti[s][d][:, 0:ntok], in0=ti[s][d][:, 0:ntok], scalar=1.0, in1=xc[s][:, 0:ntok],
                            op0=ALU.add, op1=ALU.mult), reads=[("ti", s, d), ("xc", s)], writes=[("ti", s, d)])
                        P.op("dve", lambda e, d=d: e.tensor_tensor(out=ti[s][d][:, 0:ntok], in0=ti[s][d][:, 0:ntok],
                                                                   in1=a2[d][:, 0:ntok], op=ALU.mult),
                             reads=[("ti", s, d), ("a2", d)], writes=[("ti", s, d)])

                def stage_b(i):
                    pss, c = units[i]
                    s = i % 2
                    tok0, ntok, tiles, segs = geom(pss)
                    for d in range(2):
                        col = d * 8 + c
                        hdst, hkey = ti[s][d], ("ti", s, d)
                        for si, (t0, ln, pb) in enumerate(segs):
                            init = 0.0 if pss == 0 else st0[:, col:col + 1]
                            if d == 0:
                                P.op("dve", lambda e, t0=t0, ln=ln, init=init, hdst=hdst, d=d: e.tensor_tensor_scan(
                                    out=hdst[:, t0:t0 + ln], data0=tr[s][d][:, t0:t0 + ln], data1=ti[s][d][:, t0:t0 + ln],
                                    initial=init, op0=ALU.mult, op1=ALU.add),
                                    reads=[("tr", s, d), ("ti", s, d)] + PKEY, writes=[hkey])
                            else:
                                P.op("dve", lambda e, t0=t0, ln=ln, init=init, hdst=hdst, d=d: e.tensor_tensor_scan(
                                    out=hdst[:, t0:t0 + ln][:, ::-1], data0=tr[s][d][:, t0:t0 + ln][:, ::-1],
                                    data1=ti[s][d][:, t0:t0 + ln][:, ::-1], initial=init,
                                    op0=ALU.mult, op1=ALU.add), reads=[("tr", s, d), ("ti", s, d)] + PKEY, writes=[hkey])
                            if pss == 0:
                                srccol = (t0 + ln - 1) if d == 0 else t0
                                dcol = si * 16 + d * 8 + c
                                P.op("dve", lambda e, srccol=srccol, dcol=dcol, hdst=hdst: e.tensor_copy(
                                    out=nst[:, dcol:dcol + 1], in_=hdst[:, srccol:srccol + 1]),
                                    reads=[hkey], writes=["nst"])
                    P.op("pool", lambda e: e.tensor_tensor(out=ti[s][0][:, 0:ntok], in0=ti[s][0][:, 0:ntok],
                                                           in1=ti[s][1][:, 0:ntok], op=ALU.add),
                         reads=[("ti", s, 0), ("ti", s, 1)], writes=[("ti", s, 0)])

                def stage_b2(i):
                    pss, c = units[i]
                    s = i % 2
                    tok0, ntok, tiles, segs = geom(pss)
                    P.op("dve", lambda e: e.scalar_tensor_tensor(out=lruT[:, c, tok0:tok0 + ntok], in0=x2[s][:, 0:ntok],
                                                                 scalar=0.5, in1=ti[s][0][:, 0:ntok], op0=ALU.mult,
                                                                 op1=ALU.mult),
                         reads=[("x2", s), ("ti", s, 0)], writes=["lruT"])

                n = len(units)
                load(0)
                load(1)
                stage_f1(0)
                load(2)
                stage_f2(0)
                stage_f1(1)
                load(3)
                stage_a1(0)
                stage_f2(1)
                stage_a2(0)
                for i in range(n):
                    if i + 2 < n:
                        stage_f1(i + 2)
                        if i + 4 < n:
                            load(i + 4)
                    if i + 1 < n:
                        stage_a1(i + 1)
                    if i + 2 < n:
                        stage_f2(i + 2)
                    stage_b(i)
                    if i + 1 < n:
                        stage_a2(i + 1)
                    stage_b2(i)

        def merge(attnT, lruT):
            with Scope() as sg:
                mT = sg.T("mT", [128, 8, NTOK], BF16)
                wba = sg.T("wba", [128, 4, D], BF16)
                wbl = sg.T("wbl", [128, 8, D], BF16)
                wo = sg.T("wo", [128, 8, D], BF16)
                sga = [sg.T("sga%d" % i, [128, 512]) for i in range(2)]
                sgb = [sg.T("sgb%d" % i, [128, 512]) for i in range(2)]
                wg = WS(sg, "wg", 2, [128, 8, 256])
                rot = Rot([0, 1, 2, 3, 4, 5, 6, 7])

                def load(oc):
                    r = wg.load(lambda t: [(t[:, :, 0:128], winv[:, :, 3584 + oc * 128:3584 + (oc + 1) * 128]),
                                           (t[:, :, 128:256], winv[:, :, 4608 + oc * 128:4608 + (oc + 1) * 128])])
                    if oc == 0:
                        for part, (c0_, c1_) in enumerate(((0, 256), (256, D))):
                            P.dma("pool", lambda e, c0_=c0_, c1_=c1_: e.dma_start(
                                out=wba[:, :, c0_:c1_], in_=wbra_d.rearrange("(kc p) n -> p kc n", p=128)[:, :, c0_:c1_]),
                                ("wba", part), writes=[("wba", part)])
                            P.dma("pool", lambda e, c0_=c0_, c1_=c1_: e.dma_start(
                                out=wbl[:, :, c0_:c1_], in_=wbrl_d.rearrange("(kc p) n -> p kc n", p=128)[:, :, c0_:c1_]),
                                ("wbl", part), writes=[("wbl", part)])
                    return r

                def comp(oc, t, key):
                    if oc == 2:
                        P.dma("pool", lambda e: e.dma_start(out=wo[:], in_=wout_d.rearrange("(kc p) n -> p kc n", p=128)),
                              "wo", writes=["wo"])
                    for tt in range(NTT):
                        cs = slice(tt * 512, (tt + 1) * 512)
                        s = (oc * NTT + tt) % 2
                        bga, bgb, bpa, bpl = rot.next(), rot.next(), rot.next(), rot.next()
                        for kc in range(8):
                            mm(ps[bga][:], t[:, kc, 0:128], hT[:, kc, cs], kc == 0, kc == 7, [key, ("hT", tt)], [("ps", bga)])
                        for kc in range(8):
                            mm(ps[bgb][:], t[:, kc, 128:256], hT[:, kc, cs], kc == 0, kc == 7, [key, ("hT", tt)], [("ps", bgb)])
                        wpart = 0 if oc < 2 else 1
                        for kc in range(4):
                            mm(ps[bpa][:], wba[:, kc, oc * 128:(oc + 1) * 128], attnT[:, kc, cs], kc == 0, kc == 3,
                               [("wba", wpart), ("attnT", tt)], [("ps", bpa)])
                        for kc in range(8):
                            mm(ps[bpl][:], wbl[:, kc, oc * 128:(oc + 1) * 128], lruT[:, kc, cs], kc == 0, kc == 7,
                               [("wbl", wpart), "lruT"], [("ps", bpl)])
                        P.op("act", lambda e, bga=bga, s=s: e.activation(out=sga[s][:], in_=ps[bga][:], func=AF.Tanh, scale=0.5),
                             reads=[("ps", bga)], writes=[("sga", s)])
                        P.op("act", lambda e, bgb=bgb, s=s: e.activation(out=sgb[s][:], in_=ps[bgb][:], func=AF.Tanh, scale=0.5),
                             reads=[("ps", bgb)], writes=[("sgb", s)])
                        P.op("dve", lambda e, bpa=bpa, s=s: e.scalar_tensor_tensor(
                            out=sga[s][:], in0=sga[s][:], scalar=1.0, in1=ps[bpa][:], op0=ALU.add, op1=ALU.mult),
                            reads=[("sga", s), ("ps", bpa)], writes=[("sga", s)])
                        P.op("dve", lambda e, bpl=bpl, s=s: e.scalar_tensor_tensor(
                            out=sgb[s][:], in0=sgb[s][:], scalar=1.0, in1=ps[bpl][:], op0=ALU.add, op1=ALU.mult),
                            reads=[("sgb", s), ("ps", bpl)], writes=[("sgb", s)])
                        P.op("dve", lambda e, s=s, cs=cs: e.tensor_tensor(
                            out=mT[:, oc, cs], in0=sga[s][:], in1=sgb[s][:], op=ALU.add),
                            reads=[("sga", s), ("sgb", s)], writes=[("mT", tt)])
                pipeline(8, 1, load, comp)
                GK = 1
                for oc in range(8):
                    for tt in range(NTT):
                        ci = 0 if tt == 0 else 1
                        cs = slice(tt * 512, (tt + 1) * 512)
                        b = rot.next()
                        for kc in range(8):
                            mm(ps[b][:], wo[:, kc, oc * 128:(oc + 1) * 128], mT[:, kc, cs], kc == 0, kc == 7,
                               ["wo", ("mT", tt)], [("ps", b)])
                        P.op("dve", lambda e, b=b, oc=oc, cs=cs, ci=ci: e.scalar_tensor_tensor(
                            out=xT[:, oc, cs], in0=ps[b][:], scalar=Gcoef[:, 1, oc, ci:ci + 1], in1=xT[:, oc, cs],
                            op0=ALU.mult, op1=ALU.add), reads=[("ps", b), ("Gcoef", GK), ("xT", tt)], writes=[("xT", tt)])

        if stop_after != "mods":
            run_ffn(0, f1wi_d, f1wo_d)
            dump("x1", xT[:], [128, 8, NTOK], reads=XKEYS)
        if stop_after not in ("mods", "ffn1"):
            mixer()
        if stop_after is None:
            run_ffn(2, f2wi_d, f2wo_d)
            dump("x3", xT[:], [128, 8, NTOK], reads=XKEYS)

        with Scope() as sc:
            ytok = [sc.T("ytok%d" % i, [128, D]) for i in range(4)]
            gbc = sc.T("gbc", [128, D])
            grow = sc.T("grow", [1, D])
            onesr = sc.T("onesr", [1, 128])
            junk = [sc.T("junk%d" % i, [128, 512], BF16) for i in range(2)]
            ssq = sc.T("ssq", [128, 24])
            rs = sc.T("rs", [128, 12])
            nsT = sc.T("nsT", [32, 128])
            P.dma("sp", lambda e: e.dma_start(out=grow[:], in_=pv_d[184:192, :].rearrange("(o r) n -> o (r n)", o=1)),
                  "grow", writes=["grow"])
            P.op("pool", lambda e: e.memset(onesr[:], 1.0), writes=["onesr"])
            for half in range(2):
                P.op("pe", lambda e, half=half: e.matmul(ps[half][:], lhsT=onesr[:], rhs=grow[:, half * 512:(half + 1) * 512],
                                                         start=True, stop=True),
                     reads=["onesr", "grow"], writes=[("ps", half)])
                copy_op("dve", gbc[:, half * 512:(half + 1) * 512], ps[half][:], [("ps", half)], [("gbc", half)])
            frot = Rot([2, 3, 4, 5, 6, 7, 0, 1])
            for j in range(12):
                tt = j // 4
                s = j % 4
                bb = [frot.next(), frot.next()]
                for half in range(2):
                    b = bb[half]
                    for q in range(4):
                        c2 = half * 4 + q
                        P.op("pe", lambda e, b=b, q=q, c2=c2, j=j: e.transpose(
                            ps[b][:, q * 128:(q + 1) * 128], xT[:, c2, j * 128:(j + 1) * 128], identf[:]),
                            reads=[("xT", tt), "identf"], writes=[("ps", b)])
                    P.op("act", lambda e, b=b, half=half, j=j: e.activation(
                        out=junk[half][:], in_=ps[b][:], func=AF.Square,
                        accum_out=ssq[:, 2 * j + half:2 * j + half + 1]),
                        reads=[("ps", b)], writes=[("junk", half), ("ssq", j, half)])
                P.op("dve", lambda e, j=j: e.tensor_tensor(out=rs[:, j:j + 1], in0=ssq[:, 2 * j:2 * j + 1],
                                                           in1=ssq[:, 2 * j + 1:2 * j + 2], op=ALU.add),
                     reads=[("ssq", j, 0), ("ssq", j, 1)], writes=[("rs", j)])
                P.op("act", lambda e, j=j: e.activation(out=rs[:, j:j + 1], in_=rs[:, j:j + 1], func=AF.Sqrt,
                                                        scale=1.0 / D, bias=epsb[:, 0:1]),
                     reads=[("rs", j), "epsb"], writes=[("rs", j)])
                P.op("dve", lambda e, j=j: e.reciprocal(out=rs[:, j:j + 1], in_=rs[:, j:j + 1]),
                     reads=[("rs", j)], writes=[("rs", j)])
                for half in range(2):
                    b = bb[half]
                    P.op("dve", lambda e, b=b, half=half, j=j, s=s: e.scalar_tensor_tensor(
                        out=ytok[s][:, half * 512:(half + 1) * 512], in0=ps[b][:], scalar=rs[:, j:j + 1],
                        in1=gbc[:, half * 512:(half + 1) * 512], op0=ALU.mult, op1=ALU.mult),
                        reads=[("ps", b), ("rs", j), ("gbc", half)], writes=[("ytok", s, half)])
                P.dma("sp", lambda e, j=j, s=s: e.dma_start(out=y_d[j * 128:(j + 1) * 128, :], in_=ytok[s][:]),
                      ("yst", s), reads=[("ytok", s, 0), ("ytok", s, 1)])

            P.op("pe", lambda e: e.transpose(ps[7][0:32, 0:128], nst[:], identf[:]), reads=["nst", "identf"],
                 writes=[("ps", 7)])
            copy_op("dve", nsT[:], ps[7][0:32, 0:128], [("ps", 7)], ["nsT"])
            P.dma("sp", lambda e: e.dma_start(out=ns_d, in_=nsT[:]), "nsst", reads=["nsT"])

        P.emit(nc)
    return nc, dump_outs


_NC_CACHE = {}


def make_in_maps(inp):
    f = lambda a: np.ascontiguousarray(np.asarray(a, dtype=np.float32))
    x_prompt = f(inp["x_prompt"]); x_sample = f(inp["x_sample"])
    cache_k = f(inp["cache_k"]); cache_v = f(inp["cache_v"]); state = f(inp["state_lru"])
    c = f(inp["c"]); c_ctx = f(inp["c_ctx"])
    shared = {
        "ident": np.eye(128, dtype=np.float32),
        "btab": host_bias_table(f(inp["rpb"])[0]),
        "w_mod": f(inp["w_mod"])[0], "ffn1_w_in": f(inp["ffn1_w_in"])[0], "ffn1_w_out": f(inp["ffn1_w_out"])[0],
        "w_in": f(inp["w_in"])[0], "lru_wa": f(inp["lru_wa"])[0], "lru_wi": f(inp["lru_wi"])[0],
        "w_br_attn": f(inp["w_br_attn"])[0], "w_br_lru": f(inp["w_br_lru"])[0], "w_out": f(inp["w_out"])[0],
        "ffn2_w_in": f(inp["ffn2_w_in"])[0], "ffn2_w_out": f(inp["ffn2_w_out"])[0],
    }
    common_rows = [f(inp["b_mod"])[0].reshape(72, 128), f(inp["norm_g"])[0].reshape(24, 128),
                   f(inp["conv_w"])[0].reshape(32, 128), f(inp["conv_b"])[0].reshape(8, 128),
                   f(inp["lru_ba"])[0].reshape(16, 128), f(inp["lru_bi"])[0].reshape(16, 128),
                   f(inp["lru_lambda"])[0].reshape(16, 128), f(inp["final_g"]).reshape(8, 128)]
    maps = []
    for i in range(8):
        pv = np.zeros((256, 128), np.float32)
        rows = common_rows + [state[i, 0].reshape(16, 128), c_ctx.reshape(8, 128), c[i].reshape(8, 128)]
        cat = np.concatenate(rows, axis=0)
        pv[:cat.shape[0]] = cat
        m = dict(shared)
        m["xin"] = np.concatenate([x_prompt[2 * i:2 * i + 2].reshape(512, D), x_sample[i]], axis=0)
        m["ck"] = cache_k[i, 0]
        m["cv"] = cache_v[i, 0]
        m["pv"] = pv
        maps.append(m)
    return maps


def kernel(**inputs):
    if "nc" not in _NC_CACHE:
        _NC_CACHE["nc"] = build_nc()[0]
    nc = _NC_CACHE["nc"]
    maps = make_in_maps(inputs)
    res = run_bass_kernel_spmd(nc, maps, core_ids=list(range(8)))
    r = res.results
    y_prompt = np.concatenate([r[i]["y"][:512].reshape(2, 256, D) for i in range(8)], axis=0).astype(np.float32)
    y_sample = np.stack([r[i]["y"][512:] for i in range(8)], axis=0).astype(np.float32)
    nk = np.concatenate([r[i]["nk"] for i in range(8)], axis=0)[:, None].astype(np.float32)
    nv = np.concatenate([r[i]["nv"] for i in range(8)], axis=0)[:, None].astype(np.float32)
    ns = np.concatenate([r[i]["ns"].reshape(2, 2, D) for i in range(8)], axis=0)[:, None].astype(np.float32)
    return (y_prompt, y_sample, nk, nv, ns)
```
